# Optimizing a Trainium2 kernel written in Bass

```python
import jax, jax.numpy as jnp
from jax import lax
import numpy as np

D_MODEL = 4096
BATCH = 1
SEQ = 8192
DEPTH = 2

CHUNK = 64
N_MEM = 256
EPS = 1e-6

ML_HEADS = 4
ML_DV = D_MODEL // (2 * ML_HEADS)
ML_DQK = ML_DV // 2
GATE_CAP = 15.0
GLA_HEADS = 4
GLA_DV = D_MODEL // (2 * GLA_HEADS)
GLA_DK = GLA_DV // 2
GLA_RANK = 16
GLA_TAU = 16.0
ML_QK = ML_HEADS * ML_DQK
ML_V = ML_HEADS * ML_DV
GLA_QK = GLA_HEADS * GLA_DK
GLA_V = GLA_HEADS * GLA_DV
AB_IN_SIZES = (ML_QK, ML_QK, ML_V, ML_V, ML_HEADS, ML_HEADS,
               GLA_QK, GLA_QK, GLA_V, GLA_V, GLA_RANK)
AB_IN = 2 * ML_QK + 2 * ML_V + 2 * ML_HEADS + 2 * GLA_QK + 2 * GLA_V + GLA_RANK

SB_HEADS = 32
SB_DH = D_MODEL // SB_HEADS
SB_BLOCK = 128

XA_HEADS = 4
XA_DH = 256
XA_W = XA_HEADS * XA_DH

D_FF = 11008
CONV_W = 3

kernel_name = "hybrid_mlstm_gla_stickbreak_convffn"


def rmsnorm(x, g):
    xf = x.astype(jnp.float32)
    y = xf * lax.rsqrt(jnp.mean(xf * xf, axis=-1, keepdims=True) + EPS)
    return (y * g.astype(jnp.float32)).astype(x.dtype)


def softcap(x):
    return GATE_CAP * jnp.tanh(x / GATE_CAP)


def split_cols(t, sizes):
    outs, start = [], 0
    for n in sizes:
        outs.append(t[..., start:start + n])
        start += n
    return outs


def to_chunks(t, h, d):
    b, s, _ = t.shape
    return t.reshape(b, s // CHUNK, CHUNK, h, d).transpose(1, 0, 3, 2, 4)


def gates_to_chunks(t):
    b, s, h = t.shape
    return t.reshape(b, s // CHUNK, CHUNK, h).transpose(1, 0, 3, 2)


def from_chunks(t):
    nc, b, h, l, d = t.shape
    return t.transpose(1, 0, 3, 2, 4).reshape(b, nc * l, h, d)


def mlstm_chunkwise(q, k, v, i_pre, f_pre):
    f32 = jnp.float32
    q, v = q.astype(f32), v.astype(f32)
    k = k.astype(f32) * (q.shape[-1] ** -0.5)
    li = i_pre.astype(f32)
    lf = jax.nn.log_sigmoid(f_pre.astype(f32))
    _, b, h, l, dk = q.shape
    dv = v.shape[-1]
    causal = jnp.tril(jnp.ones((l, l), bool))

    def step(carry, inp):
        c, n, m = carry
        qc, kc, vc, lic, lfc = inp
        bcum = jnp.cumsum(lfc, axis=-1)
        log_d = bcum[..., :, None] - bcum[..., None, :] + lic[..., None, :]
        log_d = jnp.where(causal, log_d, -jnp.inf)
        log_inter = bcum + m[..., None]
        m_t = jnp.maximum(jnp.max(log_d, axis=-1), log_inter)
        d_mat = jnp.exp(log_d - m_t[..., None])
        inter = jnp.exp(log_inter - m_t)
        s_qk = jnp.einsum('bhtd,bhsd->bhts', qc, kc) * d_mat
        num = (jnp.einsum('bhts,bhsv->bhtv', s_qk, vc)
               + inter[..., None] * jnp.einsum('bhtd,bhdv->bhtv', qc, c))
        den = jnp.sum(s_qk, axis=-1) + inter * jnp.einsum('bhtd,bhd->bht', qc, n)
        hc = num / jnp.maximum(jnp.abs(den), jnp.exp(-m_t))[..., None]
        b_last = bcum[..., -1]
        log_w = b_last[..., None] - bcum + lic
        m_new = jnp.maximum(b_last + m, jnp.max(log_w, axis=-1))
        w = jnp.exp(log_w - m_new[..., None])
        decay = jnp.exp(b_last + m - m_new)
        c = decay[..., None, None] * c + jnp.einsum('bhs,bhsd,bhsv->bhdv', w, kc, vc)
        n = decay[..., None] * n + jnp.einsum('bhs,bhsd->bhd', w, kc)
        return (c, n, m_new), hc

    init = (jnp.zeros((b, h, dk, dv), f32), jnp.zeros((b, h, dk), f32),
            jnp.zeros((b, h), f32))
    _, hs = lax.scan(step, init, (q, k, v, li, lf))
    return hs


def gla_chunked(q, k, v, log_a):
    f32 = jnp.float32
    q = q.astype(f32) * (q.shape[-1] ** -0.5)
    k, v, log_a = k.astype(f32), v.astype(f32), log_a.astype(f32)
    _, b, h, l, dk = q.shape
    dv = v.shape[-1]
    causal = jnp.tril(jnp.ones((l, l), bool))[..., None]

    def step(s, inp):
        qc, kc, vc, lac = inp
        g = jnp.cumsum(lac, axis=-2)
        o_inter = jnp.einsum('bhtd,bhdv->bhtv', qc * jnp.exp(g), s)
        diff = jnp.where(causal, g[..., :, None, :] - g[..., None, :, :], -jnp.inf)
        a = jnp.einsum('bhtd,bhsd,bhtsd->bhts', qc, kc, jnp.exp(diff))
        o = o_inter + jnp.einsum('bhts,bhsv->bhtv', a, vc)
        g_last = g[..., -1, :]
        s = (jnp.exp(g_last)[..., None] * s
             + jnp.einsum('bhsd,bhsv->bhdv', kc * jnp.exp(g_last[..., None, :] - g), vc))
        return s, o

    _, os_ = lax.scan(step, jnp.zeros((b, h, dk, dv), f32), (q, k, v, log_a))
    return os_


def stick_breaking(q, k, v):
    b, s, h, dh = q.shape
    scale = dh ** -0.5
    outs = []
    for blk in range(s // SB_BLOCK):
        t0, t1 = blk * SB_BLOCK, (blk + 1) * SB_BLOCK
        z = jnp.einsum('bqhd,bkhd->bhqk', q[:, t0:t1], k[:, :t1]).astype(jnp.float32) * scale
        strict = jnp.arange(t1)[None, :] < jnp.arange(t0, t1)[:, None]
        log_beta = jax.nn.log_sigmoid(z)
        log_1m = jnp.where(strict, jax.nn.log_sigmoid(-z), 0.0)
        between = lax.cumsum(log_1m, axis=3, reverse=True) - log_1m
        att = jnp.exp(jnp.where(strict, log_beta + between, -jnp.inf))
        outs.append(jnp.einsum('bhqk,bkhd->bqhd', att.astype(v.dtype), v[:, :t1]))
    return jnp.concatenate(outs, axis=1)


def ab_mixer(xn, w_in, ml_i_bias, ml_f_bias, ml_head_norm, gla_w_gate, gla_gate_bias,
             gla_head_norm, w_out):
    b, s, _ = xn.shape
    mq, mk, mv, mo, mi, mf, gq, gk, gv, gg, gr = split_cols(xn @ w_in, AB_IN_SIZES)
    i_pre = softcap(mi + ml_i_bias)
    f_pre = softcap(mf + ml_f_bias)
    h_ml = mlstm_chunkwise(to_chunks(mq, ML_HEADS, ML_DQK), to_chunks(mk, ML_HEADS, ML_DQK),
                           to_chunks(mv, ML_HEADS, ML_DV), gates_to_chunks(i_pre),
                           gates_to_chunks(f_pre))
    h_ml = rmsnorm(from_chunks(h_ml).astype(xn.dtype), ml_head_norm)
    h_ml = h_ml * jax.nn.sigmoid(mo.reshape(b, s, ML_HEADS, ML_DV))
    log_a = jax.nn.log_sigmoid((gr @ gla_w_gate + gla_gate_bias).astype(jnp.float32)) / GLA_TAU
    h_gla = gla_chunked(to_chunks(gq, GLA_HEADS, GLA_DK), to_chunks(gk, GLA_HEADS, GLA_DK),
                        to_chunks(gv, GLA_HEADS, GLA_DV), to_chunks(log_a, GLA_HEADS, GLA_DK))
    h_gla = rmsnorm(from_chunks(h_gla).astype(xn.dtype), gla_head_norm)
    h_gla = h_gla * jax.nn.silu(gg.reshape(b, s, GLA_HEADS, GLA_DV))
    h = jnp.concatenate([h_ml.reshape(b, s, ML_V), h_gla.reshape(b, s, GLA_V)], axis=-1)
    return h @ w_out


def sb_mixer(xn, w_qkv, w_out):
    b, s, _ = xn.shape
    q, k, v = split_cols(xn @ w_qkv, (D_MODEL, D_MODEL, D_MODEL))
    o = stick_breaking(q.reshape(b, s, SB_HEADS, SB_DH), k.reshape(b, s, SB_HEADS, SB_DH),
                       v.reshape(b, s, SB_HEADS, SB_DH))
    return o.reshape(b, s, D_MODEL) @ w_out


def cross_attn(xn, memn, wq, wk, wv, wo):
    b, s, _ = xn.shape
    nm = memn.shape[1]
    q = (xn @ wq).reshape(b, s, XA_HEADS, XA_DH)
    k = (memn @ wk).reshape(b, nm, XA_HEADS, XA_DH)
    v = (memn @ wv).reshape(b, nm, XA_HEADS, XA_DH)
    scores = jnp.einsum('bqhd,bkhd->bhqk', q, k).astype(jnp.float32) * (XA_DH ** -0.5)
    p = jax.nn.softmax(scores, axis=-1).astype(v.dtype)
    o = jnp.einsum('bhqk,bkhd->bqhd', p, v).reshape(b, s, XA_W)
    return o @ wo


def conv_ffn(xn, w_gate, w_up, conv_w, conv_b, w_down):
    s = xn.shape[1]
    g = xn @ w_gate
    gp = jnp.pad(g, ((0, 0), (CONV_W - 1, 0), (0, 0)))
    conv = conv_b
    for j in range(CONV_W):
        conv = conv + gp[:, j:j + s] * conv_w[j]
    h = jax.nn.gelu(conv, approximate=True) * (xn @ w_up)
    return h @ w_down


def setup_inputs(seed: int = 0) -> dict:
    key = jax.random.key(seed)
    ks = iter(jax.random.split(key, 32))
    f32 = jnp.float32
    n_even = (DEPTH + 1) // 2
    n_odd = DEPTH // 2

    def nrm(shape, scale):
        return jax.random.normal(next(ks), shape, f32) * scale

    def gain(shape):
        return 1.0 + nrm(shape, 0.02)

    return {
        "x": nrm((BATCH, SEQ, D_MODEL), 1.0),
        "mem": nrm((BATCH, N_MEM, D_MODEL), 1.0),
        "mix_norm_pre": gain((DEPTH, D_MODEL)),
        "mix_norm_post": gain((DEPTH, D_MODEL)),
        "ab_w_in": nrm((n_even, D_MODEL, AB_IN), D_MODEL ** -0.5),
        "ml_i_bias": nrm((n_even, ML_HEADS), 0.1),
        "ml_f_bias": 3.0 + nrm((n_even, ML_HEADS), 0.5),
        "ml_head_norm": gain((n_even, ML_HEADS, ML_DV)),
        "gla_w_gate": nrm((n_even, GLA_RANK, GLA_QK), GLA_RANK ** -0.5),
        "gla_gate_bias": nrm((n_even, GLA_QK), 0.1),
        "gla_head_norm": gain((n_even, GLA_HEADS, GLA_DV)),
        "ab_w_out": nrm((n_even, D_MODEL, D_MODEL), D_MODEL ** -0.5),
        "sb_w_qkv": nrm((n_odd, D_MODEL, 3 * D_MODEL), D_MODEL ** -0.5),
        "sb_w_out": nrm((n_odd, D_MODEL, D_MODEL), D_MODEL ** -0.5),
        "xa_norm_pre": gain((DEPTH, D_MODEL)),
        "xa_norm_post": gain((DEPTH, D_MODEL)),
        "mem_norm": gain((DEPTH, D_MODEL)),
        "xa_wq": nrm((DEPTH, D_MODEL, XA_W), D_MODEL ** -0.5),
        "xa_wk": nrm((DEPTH, D_MODEL, XA_W), D_MODEL ** -0.5),
        "xa_wv": nrm((DEPTH, D_MODEL, XA_W), D_MODEL ** -0.5),
        "xa_wo": nrm((DEPTH, XA_W, D_MODEL), XA_W ** -0.5),
        "ffn_norm_pre": gain((DEPTH, D_MODEL)),
        "ffn_norm_post": gain((DEPTH, D_MODEL)),
        "ffn_w_gate": nrm((DEPTH, D_MODEL, D_FF), D_MODEL ** -0.5),
        "ffn_w_up": nrm((DEPTH, D_MODEL, D_FF), D_MODEL ** -0.5),
        "ffn_conv_w": nrm((DEPTH, CONV_W, D_FF), CONV_W ** -0.5),
        "ffn_conv_b": nrm((DEPTH, D_FF), 0.02),
        "ffn_w_down": nrm((DEPTH, D_FF, D_MODEL), D_FF ** -0.5),
    }


def reference(x, mem, mix_norm_pre, mix_norm_post, ab_w_in, ml_i_bias, ml_f_bias, ml_head_norm,
              gla_w_gate, gla_gate_bias, gla_head_norm, ab_w_out, sb_w_qkv, sb_w_out,
              xa_norm_pre, xa_norm_post, mem_norm, xa_wq, xa_wk, xa_wv, xa_wo,
              ffn_norm_pre, ffn_norm_post, ffn_w_gate, ffn_w_up, ffn_conv_w, ffn_conv_b,
              ffn_w_down):
    for layer in range(DEPTH):
        e = layer // 2
        h = rmsnorm(x, mix_norm_pre[layer])
        if layer % 2 == 0:
            h = ab_mixer(h, ab_w_in[e], ml_i_bias[e], ml_f_bias[e], ml_head_norm[e],
                         gla_w_gate[e], gla_gate_bias[e], gla_head_norm[e], ab_w_out[e])
        else:
            h = sb_mixer(h, sb_w_qkv[e], sb_w_out[e])
        x = x + rmsnorm(h, mix_norm_post[layer])
        h = cross_attn(rmsnorm(x, xa_norm_pre[layer]), rmsnorm(mem, mem_norm[layer]),
                       xa_wq[layer], xa_wk[layer], xa_wv[layer], xa_wo[layer])
        x = x + rmsnorm(h, xa_norm_post[layer])
        h = conv_ffn(rmsnorm(x, ffn_norm_pre[layer]), ffn_w_gate[layer], ffn_w_up[layer],
                     ffn_conv_w[layer], ffn_conv_b[layer], ffn_w_down[layer])
        x = x + rmsnorm(h, ffn_norm_post[layer])
    return x
```

```python
import contextlib
import numpy as np
import concourse.bass as bass
import concourse.mybir as mybir
from concourse.bass_utils import run_bass_kernel_spmd

F32 = mybir.dt.float32
BF16 = mybir.dt.bfloat16
AF = mybir.ActivationFunctionType
ALU = mybir.AluOpType
AX = mybir.AxisListType

D = 4096
KC = 32
EPS = 1e-6
NCORES = 8
TOK = 1024
ENGS = ("pe", "act", "dve", "pool", "sp")


class Sch:
    def __init__(self, nc, same_eng_sync=True):
        self.nc = nc
        self.ops = {e: [] for e in ENGS}
        self.res = {}
        self.waited = {e: {} for e in ENGS}
        self.same_eng_sync = same_eng_sync
        self.dma_cnt = {}
        self.dma_sems = {}
        self.slot_alias = {}

    def _deps_for(self, reads, writes):
        deps = []
        for r in reads:
            st = self.res.get(r)
            if st and st["w"] is not None:
                deps.append(st["w"])
        for w in writes:
            st = self.res.get(w)
            if st:
                if st["w"] is not None:
                    deps.append(st["w"])
                deps.extend(st["r"].values())
        return deps

    def _commit(self, tok, reads, writes):
        for r in reads:
            st = self.res.setdefault(r, {"w": None, "r": {}})
            key = tok[0] if tok[0] != "dma" else ("dma", tok[1])
            st["r"][key] = tok
        for w in writes:
            self.res[w] = {"w": tok, "r": {}}

    def _add_waits(self, eng, deps):
        waits = []
        wd = self.waited[eng]
        for d in deps:
            if d[0] == "dma":
                key = ("dma", d[1])
                if wd.get(key, -1) >= d[2]:
                    continue
                wd[key] = d[2]
                waits.append(d)
            else:
                e2, idx = d
                if e2 == eng and (eng == "pe" or not self.same_eng_sync):
                    continue
                if wd.get(e2, -1) >= idx:
                    continue
                wd[e2] = idx
                self.ops[e2][idx]["sig"] = True
                waits.append(d)
        return waits

    def op(self, eng, fn, reads=(), writes=()):
        deps = self._deps_for(reads, writes)
        waits = self._add_waits(eng, deps)
        idx = len(self.ops[eng])
        self.ops[eng].append({"fn": fn, "waits": waits, "sig": False, "dma": None})
        self._commit((eng, idx), reads, writes)
        return (eng, idx)

    def dma(self, q, fn, reads=(), writes=(), slot=None, inc=16):
        assert slot is not None
        if inc == 16:
            key = (q, slot)
            if key not in self.slot_alias:
                n = sum(1 for (qq, _) in self.slot_alias if qq == q)
                self.slot_alias[key] = f"{'w' if q == 'pool' else 'q'}{n}"
            slot = self.slot_alias[key]
        deps = self._deps_for(reads, writes)
        waits = self._add_waits(q, deps)
        cnt = self.dma_cnt.get(slot, 0) + inc
        self.dma_cnt[slot] = cnt
        self.ops[q].append({"fn": fn, "waits": waits, "sig": False, "dma": slot, "inc": inc})
        self._commit(("dma", slot, cnt), reads, writes)
        return ("dma", slot, cnt)

    def barrier(self):
        toks = [("dma", s_, c) for s_, c in self.dma_cnt.items()]
        for e in ENGS:
            for i in range(len(self.ops[e]) - 1, -1, -1):
                o = self.ops[e][i]
                if o["fn"] is not None and o["dma"] is None:
                    toks.append((e, i))
                    break
        for e in ENGS:
            mine = [t for t in toks if not (t[0] == e)]
            waits = self._add_waits(e, mine)
            if waits:
                self.ops[e].append({"fn": None, "waits": waits, "sig": False, "dma": None})
        self.slot_alias = {}

    def finish(self, eng="sp"):
        toks = [("dma", s_, c) for s_, c in self.dma_cnt.items()]
        waits = self._add_waits(eng, toks)
        if waits:
            self.ops[eng].append({"fn": None, "waits": waits, "sig": False, "dma": None})

    def emit(self):
        nc = self.nc
        with contextlib.ExitStack() as st:
            esem = {e: st.enter_context(nc.semaphore("s_" + e)) for e in ENGS}
            for slot in self.dma_cnt:
                self.dma_sems[slot] = st.enter_context(nc.semaphore("d_" + str(slot)))
            sigcnt = {}
            for e in ENGS:
                c = 0
                arr = []
                for o in self.ops[e]:
                    if o["sig"]:
                        c += 1
                    arr.append(c)
                sigcnt[e] = arr
            block = st.enter_context(nc.Block())

            def run(e, engobj):
                for o in self.ops[e]:
                    for d in o["waits"]:
                        if d[0] == "dma":
                            engobj.wait_ge(self.dma_sems[d[1]], d[2])
                        else:
                            engobj.wait_ge(esem[d[0]], sigcnt[d[0]][d[1]])
                    if o["fn"] is None:
                        continue
                    ins = o["fn"](engobj)
                    if o["dma"] is not None:
                        ins.then_inc(self.dma_sems[o["dma"]], o["inc"])
                    elif o["sig"]:
                        ins.then_inc(esem[e], 1)

            @block.tensor
            def _(eng):
                run("pe", eng)

            @block.scalar
            def _(eng):
                run("act", eng)

            @block.vector
            def _(eng):
                run("dve", eng)

            @block.gpsimd
            def _(eng):
                run("pool", eng)

            @block.sync
            def _(eng):
                run("sp", eng)


class Buf:
    def __init__(self, ap, name):
        self.ap = ap
        self.name = name

    def __getitem__(self, key):
        return self.ap[key]


def _dsize(dt):
    return 4 if dt == F32 else 2


class KB:
    ARENA_BYTES = 204 * 1024

    def __init__(self):
        self.nc = bass.Bass("TRN2", target_bir_lowering=False)
        self.s = Sch(self.nc)
        self.st = contextlib.ExitStack()
        self.pi = 0
        self.uid = 0
        self.arena = self.st.enter_context(self.nc.sbuf_tensor("arena", [128, self.ARENA_BYTES // 2], BF16))
        self.aoff = 0
        self.ps = [self.st.enter_context(self.nc.psum_tensor(f"ps{i}", [128, 512], F32)) for i in range(8)]
        self.held = set()
        s = self.s
        self.ones_bf = self.sb("ones_bf", [128], BF16)
        s.op("pool", lambda e: e.memset(self.ones_bf[:], 1.0), writes=["ones_bf"])
        self.eps_sb = self.sb("eps", [1], F32)
        s.op("pool", lambda e: e.memset(self.eps_sb[:], EPS), writes=["eps"])

    def din(self, name, shape, dt=F32):
        return self.nc.dram_tensor(name, list(shape), dt, kind="ExternalInput").ap()

    def dout(self, name, shape, dt=F32):
        return self.nc.dram_tensor(name, list(shape), dt, kind="ExternalOutput").ap()

    def dscr(self, name, shape, dt=F32):
        return self.nc.dram_tensor(name, list(shape), dt, kind="Internal").ap()

    def sb(self, name, fshape, dt=F32):
        n = 1
        for d in fshape:
            n *= d
        nb = n * _dsize(dt)
        self.aoff = (self.aoff + 31) // 32 * 32
        assert self.aoff + nb <= self.ARENA_BYTES, f"arena overflow at {name}: {self.aoff + nb}"
        ap = self.arena[:, self.aoff // 2:(self.aoff + nb) // 2]
        self.aoff += nb
        if dt == F32:
            ap = ap.bitcast(F32)
        if len(fshape) == 2:
            ap = ap.rearrange("p (a b) -> p a b", a=fshape[0])
        elif len(fshape) == 3:
            ap = ap.rearrange("p (a b c) -> p a b c", a=fshape[0], b=fshape[1])
        return Buf(ap, name)

    def mark(self):
        return self.aoff

    def release(self, mark):
        self.s.barrier()
        self.aoff = mark

    def bank(self):
        while True:
            b = self.pi % 8
            self.pi += 1
            if b not in self.held:
                return b

    def hold_bank(self):
        b = self.bank()
        self.held.add(b)
        return b

    def unhold(self, b):
        self.held.discard(b)

    def load_small(self, buf, src_ap):
        self.s.dma("sp", lambda e: e.dma_start(out=buf[:], in_=src_ap), writes=[buf.name], slot=buf.name)

    def close(self):
        self.s.finish()
        self.s.emit()
        self.st.close()


def _ap(o, e):
    return o(e) if callable(o) else o


def _sub(o, e, r0, r1, c0=None, c1=None):
    if callable(o):
        return o(e, r0, r1, c0, c1)
    if c0 is None:
        return o[r0:r1, :]
    return o[r0:r1, c0:c1]


def _io(k, io, name, shape, dt, kind):
    if io is not None:
        return io[name]
    return k.din(name, shape, dt) if kind == "in" else k.dout(name, shape, dt)


def run_kb(k, in_maps, trace=False):
    return run_bass_kernel_spmd(k.nc, in_maps, core_ids=list(range(len(in_maps))), trace=trace)


def tiles_of(n, w=512):
    return [(a, min(a + w, n)) for a in range(0, n, w)]


def rstd_from_psum(k, bk, dst, a, b, dim, dres):
    s = k.s
    s.op("act", lambda e: e.activation(out=dst[:, a:b], in_=k.ps[bk][:, :b - a], func=AF.Ln,
                                       scale=1.0 / dim, bias=k.eps_sb[:, 0:1]),
         reads=[f"ps{bk}", "eps"], writes=[dres])
    s.op("act", lambda e: e.activation(out=dst[:, a:b], in_=dst[:, a:b], func=AF.Exp, scale=-0.5),
         reads=[dres], writes=[dres])


def norm_load(k, x_ap, TT, gain, xg, rstd, tiles, xstage, sq, dim=D, parts=None):
    s = k.s
    nchunk = dim // 128
    banks = [k.hold_bank() for _ in tiles]
    for c in range(nchunk):
        stg = xstage[c % len(xstage)]
        if parts is None:
            s.dma("sp", lambda e, stg=stg, c=c: e.dma_start(out=stg[:, :TT], in_=_ap(x_ap, e)[c * 128:(c + 1) * 128, :]),
                  writes=[stg.name], slot=stg.name)
        else:
            for pi, (pa, pb, srcf) in enumerate(parts):
                s.dma("sp", lambda e, stg=stg, c=c, pa=pa, pb=pb, srcf=srcf: e.dma_start(out=stg[:, pa:pb], in_=srcf(c, e)),
                      writes=[stg.name] if pi == 0 else [(stg.name, pi)], slot=f"{stg.name}_{pi}")
        prd = [stg.name] + ([(stg.name, pi) for pi in range(1, len(parts))] if parts else [])
        s.op("act", lambda e, stg=stg, c=c: e.activation(out=xg[:, c, :TT], in_=stg[:, :TT], func=AF.Copy,
                                                          scale=gain[:, c:c + 1]),
             reads=prd + [gain.name], writes=[(xg.name, c)])
        sqb = sq[c % len(sq)]
        s.op("dve", lambda e, stg=stg, sqb=sqb: e.tensor_tensor(out=sqb[:, :TT], in0=stg[:, :TT], in1=stg[:, :TT],
                                                                 op=ALU.mult),
             reads=prd, writes=[sqb.name])
        for ti, (a, b) in enumerate(tiles):
            s.op("pe", lambda e, sqb=sqb, a=a, b=b, bk=banks[ti], c=c: e.matmul(
                k.ps[bk][:, :b - a], lhsT=k.ones_bf[:], rhs=sqb[:, a:b], start=(c == 0), stop=(c == nchunk - 1)),
                reads=[sqb.name, "ones_bf"], writes=[f"ps{banks[ti]}"])
    for ti, (a, b) in enumerate(tiles):
        rstd_from_psum(k, banks[ti], rstd, a, b, dim, (rstd.name, ti))
        k.unhold(banks[ti])


class WStream:
    def __init__(self, k, slots, q="pool"):
        self.k = k
        self.slots = slots
        self.jobs = []
        self.issued = 0
        self.q = q

    def add(self, fn):
        self.jobs.append(fn)
        return len(self.jobs) - 1

    def ensure(self, j):
        while self.issued <= j and self.issued < len(self.jobs):
            i = self.issued
            sl = self.slots[i % len(self.slots)]
            self.k.s.dma(self.q, self.jobs[i](sl), writes=[sl.name], slot=sl.name)
            self.issued += 1

    def get(self, j):
        self.ensure(j)
        sl = self.slots[j % len(self.slots)]
        return sl

    def prefetch(self, j):
        self.ensure(j)


def proj_fm(k, w_ap, col0, ncols, nkc, rhs_fn, rhs_reads, tiles, wslots, gw, evac):
    s = k.s
    wv = w_ap.rearrange("(kc p) n -> p kc n", p=128)
    ngroups = (ncols + gw - 1) // gw
    ws = WStream(k, wslots)
    geo = []
    for g in range(ngroups):
        c0 = col0 + g * gw
        cw = min(gw, col0 + ncols - c0)
        geo.append((c0, cw))
        ws.add(lambda sl, c0=c0, cw=cw: (lambda e: e.dma_start(out=sl[:, :nkc, :cw], in_=wv[:, :, c0:c0 + cw])))
    for g in range(ngroups):
        c0, cw = geo[g]
        wt = ws.get(g)
        ws.prefetch(g + 1)
        for cb in range((cw + 127) // 128):
            m = min(128, cw - cb * 128)
            for ti, (a, b) in enumerate(tiles):
                bk = k.bank()
                for kc in range(nkc):
                    s.op("pe", lambda e, wt=wt, kc=kc, cb=cb, m=m, a=a, b=b, bk=bk: e.matmul(
                        k.ps[bk][:m, :b - a], lhsT=wt[:, kc, cb * 128:cb * 128 + m], rhs=rhs_fn(kc, a, b),
                        start=(kc == 0), stop=(kc == nkc - 1)),
                        reads=[wt.name] + rhs_reads(kc), writes=[f"ps{bk}"])
                evac((c0 - col0) // 128 + cb, m, ti, a, b, bk)


class YSink:
    def __init__(self, k, T, y_scr, dim=D):
        self.k = k
        self.T = T
        self.y_scr = y_scr
        self.dim = dim
        self.tiles = tiles_of(T)
        self.ysb = [k.sb(f"ysb{i}", [T], F32) for i in range(2)]
        self.sq = [k.sb(f"ysq{i}", [T], BF16) for i in range(2)]
        self.ssb = [k.hold_bank() for _ in self.tiles]
        self.nblk = dim // 128

    def evac(self, cbg, m, ti, a, b, bk):
        k, s = self.k, self.k.s
        y_ = self.ysb[cbg % 2]
        q_ = self.sq[cbg % 2]
        s.op("act", lambda e: e.copy(out=y_[:, a:b], in_=k.ps[bk][:, :b - a]),
             reads=[f"ps{bk}"], writes=[(y_.name, ti)])
        s.op("dve", lambda e: e.tensor_tensor(out=q_[:, a:b], in0=k.ps[bk][:, :b - a], in1=y_[:, a:b], op=ALU.mult),
             reads=[f"ps{bk}", (y_.name, ti)], writes=[(q_.name, ti)])
        sb_ = self.ssb[ti]
        s.op("pe", lambda e: e.matmul(k.ps[sb_][:, :b - a], lhsT=k.ones_bf[:], rhs=q_[:, a:b],
                                      start=(cbg == 0), stop=(cbg == self.nblk - 1)),
             reads=[(q_.name, ti), "ones_bf"], writes=[f"ps{sb_}"])
        if ti == len(self.tiles) - 1:
            s.dma("sp", lambda e: e.dma_start(out=self.y_scr[cbg * 128:(cbg + 1) * 128, :], in_=y_[:]),
                  reads=[(y_.name, i) for i in range(len(self.tiles))], writes=[("y_scr", cbg)], slot=y_.name + "o")

    def finish(self, x_src, gpost, x_out):
        k, s, T = self.k, self.k.s, self.T
        rstd2 = k.sb("rstd2", [T], F32)
        for ti, (a, b) in enumerate(self.tiles):
            rstd_from_psum(k, self.ssb[ti], rstd2, a, b, self.dim, ("rstd2", ti))
            k.unhold(self.ssb[ti])
        r2 = [("rstd2", i) for i in range(len(self.tiles))]
        xin = [k.sb(f"xin{i}", [T], F32) for i in range(2)]
        yin = [k.sb(f"yin{i}", [T], F32) for i in range(2)]
        for c in range(self.nblk):
            xi = xin[c % 2]
            yi = yin[c % 2]
            s.dma("sp", lambda e, xi=xi, c=c: e.dma_start(out=xi[:], in_=x_src(c)), writes=[xi.name], slot=xi.name)
            s.dma("sp", lambda e, yi=yi, c=c: e.dma_start(out=yi[:], in_=self.y_scr[c * 128:(c + 1) * 128, :]),
                  reads=[("y_scr", c)], writes=[yi.name], slot=yi.name)
            s.op("dve", lambda e, yi=yi, c=c: e.scalar_tensor_tensor(
                out=yi[:], in0=yi[:], scalar=gpost[:, c:c + 1], in1=rstd2[:], op0=ALU.mult, op1=ALU.mult),
                reads=[yi.name, gpost.name] + r2, writes=[yi.name])
            s.op("dve", lambda e, yi=yi, xi=xi: e.tensor_tensor(out=xi[:], in0=xi[:], in1=yi[:], op=ALU.add),
                 reads=[yi.name, xi.name], writes=[xi.name])
            s.dma("sp", lambda e, xi=xi, c=c: e.dma_start(out=x_out[c * 128:(c + 1) * 128, :], in_=xi[:]),
                  reads=[xi.name], writes=[("x_out", c)], slot=xi.name + "o")


DFF = 11008
NFB = DFF // 128


def build_ffn(T=TOK, DFF=DFF, dbg=0, k=None, io=None):
    NFB = DFF // 128
    own = k is None
    if own:
        k = KB()
    s = k.s
    TH = T + 2
    if io is None:
        xT = k.din("xT", [D, TH])
        hmask = k.din("hmask", [128, 1])
        xparts = None
        x_own = lambda c: xT[c * 128:(c + 1) * 128, 2:TH]
        h_scr = k.dscr("h_scr", [DFF, T], BF16)
        y_scr = k.dscr("y_scr", [D, T])
    else:
        xT = None
        hmask = None
        xparts = io["xparts"]
        x_own = io["x_own"]
        h_scr = io["h_scr"]
        y_scr = io["y_scr"]
    g_pre = _io(k, io, "g_pre", [128, KC], F32, "in")
    g_post = _io(k, io, "g_post", [128, KC], F32, "in")
    w_gate = _io(k, io, "w_gate", [D, DFF], F32, "in")
    w_up = _io(k, io, "w_up", [D, DFF], F32, "in")
    conv_w = _io(k, io, "conv_w", [128, NFB, 3], F32, "in")
    conv_b = _io(k, io, "conv_b", [128, NFB], F32, "in")
    w_down = _io(k, io, "w_down", [DFF, D], F32, "in")
    x_out = _io(k, io, "x_out", [D, T], F32, "out")
    mk_start = k.mark()

    gpost = k.sb("gpost", [KC])
    k.load_small(gpost, g_post)
    mk0 = k.mark()
    gpre = k.sb("gpre", [KC])
    cw_sb = k.sb("cw", [NFB, 3])
    cb_sb = k.sb("cb", [NFB])
    hm_sb = k.sb("hm", [1])
    for buf, src in ((gpre, g_pre), (cw_sb, conv_w), (cb_sb, conv_b)):
        k.load_small(buf, src)
    if hmask is not None:
        k.load_small(hm_sb, hmask)
    else:
        s.op("pool", lambda e: e.memset(hm_sb[:], 1.0), writes=["hm"])
    rstd = k.sb("rstd", [TH])

    xg = k.sb("xg", [KC, TH], BF16)
    xstage = [k.sb(f"xst{i}", [TH], F32) for i in range(3)]
    sq = [k.sb(f"sq{i}", [TH], BF16) for i in range(2)]
    GW = 256
    wgu = [k.sb(f"wgu{i}", [2, KC, GW], BF16) for i in range(2)]
    gsb = [k.sb(f"gsb{i}", [TH], F32) for i in range(2)]
    cc = [k.sb(f"cc{i}", [T], F32) for i in range(2)]
    gl = [k.sb(f"gl{i}", [T], F32) for i in range(2)]
    hb = [k.sb(f"hb{i}", [T], BF16) for i in range(2)]

    tiles_h = tiles_of(TH)
    norm_load(k, xT, TH, gpre, xg, rstd, tiles_h, xstage, sq, parts=xparts)
    s.op("dve", lambda e: e.tensor_scalar(out=rstd[:, 0:2], in0=rstd[:, 0:2], scalar1=hm_sb[:, 0:1], scalar2=None,
                                          op0=ALU.mult),
         reads=[("rstd", 0), "hm"], writes=[("rstd", 0)])
    rstd_reads = [("rstd", i) for i in range(len(tiles_h))]
    tiles_u = [(2 + a, 2 + b) for (a, b) in tiles_of(T)]
    wgv = w_gate.rearrange("(kc p) n -> p kc n", p=128)
    wuv = w_up.rearrange("(kc p) n -> p kc n", p=128)
    NG = DFF // GW
    ws = WStream(k, wgu)
    for g in range(NG):
        ws.add(lambda sl, g=g: (lambda e: e.dma_start(out=sl[:, 0, :, :], in_=wgv[:, :, g * GW:(g + 1) * GW])))
    upres = lambda g: f"wup{g % 2}"

    def issue_up(g):
        sl = wgu[g % 2]
        s.dma("pool", lambda e, sl=sl, g=g: e.dma_start(out=sl[:, 1, :, :], in_=wuv[:, :, g * GW:(g + 1) * GW]),
              writes=[upres(g)], slot=upres(g))

    ws.ensure(0)
    issue_up(0)
    for g in range(NG):
        wt = ws.get(g)
        if g + 1 < NG:
            ws.prefetch(g + 1)
            issue_up(g + 1)
        for cb in range(GW // 128):
            fb = g * (GW // 128) + cb
            gs = gsb[fb % 2]
            for ti, (a, b) in enumerate(tiles_h):
                bk = k.bank()
                for kc in range(KC):
                    s.op("pe", lambda e, wt=wt, kc=kc, cb=cb, a=a, b=b, bk=bk: e.matmul(
                        k.ps[bk][:, :b - a], lhsT=wt[:, 0, kc, cb * 128:(cb + 1) * 128], rhs=xg[:, kc, a:b],
                        start=(kc == 0), stop=(kc == KC - 1)),
                        reads=[wt.name, ("xg", kc)], writes=[f"ps{bk}"])
                s.op("dve", lambda e, gs=gs, a=a, b=b, bk=bk: e.tensor_tensor(
                    out=gs[:, a:b], in0=k.ps[bk][:, :b - a], in1=rstd[:, a:b], op=ALU.mult),
                    reads=[f"ps{bk}", ("rstd", ti)], writes=[(gs.name, ti)])
            ubanks = []
            for ti, (a, b) in enumerate(tiles_u):
                bk = k.bank()
                ubanks.append(bk)
                for kc in range(KC):
                    s.op("pe", lambda e, wt=wt, kc=kc, cb=cb, a=a, b=b, bk=bk: e.matmul(
                        k.ps[bk][:, :b - a], lhsT=wt[:, 1, kc, cb * 128:(cb + 1) * 128], rhs=xg[:, kc, a:b],
                        start=(kc == 0), stop=(kc == KC - 1)),
                        reads=[upres(g), ("xg", kc)], writes=[f"ps{bk}"])
            c_ = cc[fb % 2]
            g_reads = [(gs.name, i) for i in range(len(tiles_h))]
            s.op("dve", lambda e, c_=c_, gs=gs, fb=fb: e.tensor_scalar(
                out=c_[:], in0=gs[:, 2:TH], scalar1=cw_sb[:, fb, 2:3], scalar2=cb_sb[:, fb:fb + 1],
                op0=ALU.mult, op1=ALU.add), reads=g_reads + ["cw", "cb"], writes=[c_.name])
            s.op("dve", lambda e, c_=c_, gs=gs, fb=fb: e.scalar_tensor_tensor(
                out=c_[:], in0=gs[:, 1:TH - 1], scalar=cw_sb[:, fb, 1:2], in1=c_[:], op0=ALU.mult, op1=ALU.add),
                reads=g_reads + ["cw", c_.name], writes=[c_.name])
            s.op("dve", lambda e, c_=c_, gs=gs, fb=fb: e.scalar_tensor_tensor(
                out=c_[:], in0=gs[:, 0:T], scalar=cw_sb[:, fb, 0:1], in1=c_[:], op0=ALU.mult, op1=ALU.add),
                reads=g_reads + ["cw", c_.name], writes=[c_.name])
            gl_ = gl[fb % 2]
            s.op("act", lambda e, c_=c_, gl_=gl_: e.activation(out=gl_[:], in_=c_[:], func=AF.Gelu_apprx_tanh),
                 reads=[c_.name], writes=[gl_.name])
            s.op("dve", lambda e, gl_=gl_: e.tensor_tensor(out=gl_[:], in0=gl_[:], in1=rstd[:, 2:TH], op=ALU.mult),
                 reads=[gl_.name] + rstd_reads, writes=[gl_.name])
            h_ = hb[fb % 2]
            for ti, (a, b) in enumerate(tiles_u):
                bk = ubanks[ti]
                s.op("dve", lambda e, h_=h_, gl_=gl_, a=a, b=b, bk=bk: e.tensor_tensor(
                    out=h_[:, a - 2:b - 2], in0=k.ps[bk][:, :b - a], in1=gl_[:, a - 2:b - 2], op=ALU.mult),
                    reads=[f"ps{bk}", gl_.name], writes=[(h_.name, ti)])
            s.dma("sp", lambda e, h_=h_, fb=fb: e.dma_start(out=h_scr[fb * 128:(fb + 1) * 128, :], in_=h_[:]),
                  reads=[(h_.name, i) for i in range(len(tiles_u))], writes=[("h_scr", fb)], slot=h_.name + "o")
    k.release(mk0)

    hT = k.sb("hT", [NFB, T], BF16)
    _q = [(NFB * i) // 4 for i in range(5)]
    SPL = [(_q[i], _q[i + 1]) for i in range(4) if _q[i + 1] > _q[i]]
    wd = [k.sb(f"wd{i}", [max(b - a for a, b in SPL), 128], BF16) for i in range(2)]
    hv = h_scr.rearrange("(fc p) t -> p fc t", p=128)
    NHL = 8
    per = (NFB + NHL - 1) // NHL
    for i in range(NHL):
        f0, f1 = i * per, min(NFB, (i + 1) * per)
        if f0 >= f1:
            continue
        s.dma("sp", lambda e, f0=f0, f1=f1: e.dma_start(out=hT[:, f0:f1, :], in_=hv[:, f0:f1, :]),
              reads=[("h_scr", f) for f in range(f0, f1)], writes=[("hT", i)], slot=f"hT{i}")
    wdv = w_down.rearrange("(fc p) n -> p fc n", p=128)
    tiles_t = tiles_of(T)
    ysink = YSink(k, T, y_scr)
    ws = WStream(k, wd)
    for db in range(KC):
        for (f0, f1) in SPL:
            ws.add(lambda sl, db=db, f0=f0, f1=f1: (lambda e: e.dma_start(
                out=sl[:, :f1 - f0, :], in_=wdv[:, f0:f1, db * 128:(db + 1) * 128])))
    for db in range(KC):
        ybanks = [k.bank() for _ in tiles_t]
        for qi, (f0, f1) in enumerate(SPL):
            j = db * len(SPL) + qi
            wt = ws.get(j)
            ws.prefetch(j + 1)
            for ti, (a, b) in enumerate(tiles_t):
                bk = ybanks[ti]
                for fc in range(f0, f1):
                    s.op("pe", lambda e, wt=wt, fl=fc - f0, fc=fc, a=a, b=b, bk=bk: e.matmul(
                        k.ps[bk][:, :b - a], lhsT=wt[:, fl, :], rhs=hT[:, fc, a:b],
                        start=(fc == 0), stop=(fc == NFB - 1)),
                        reads=[wt.name, ("hT", fc // per)], writes=[f"ps{bk}"])
        for ti, (a, b) in enumerate(tiles_t):
            ysink.evac(db, 128, ti, a, b, ybanks[ti])
    mk1 = k.mark()
    k.release(mk0)
    ysink.finish(x_own, gpost, x_out)
    if own:
        k.close()
    else:
        k.release(mk_start)
    return k


def make_ident(k, name="ident"):
    idt = k.sb(name, [128], F32)
    k.s.op("pool", lambda e: e.memset(idt[:], 1.0), writes=[name])
    k.s.op("pool", lambda e: e.affine_select(out=idt[:], in_=idt[:], pattern=[[-1, 128]], compare_op=ALU.is_equal,
                                             fill=0.0, base=0, channel_multiplier=1), reads=[name], writes=[name])
    return idt


def build_proj(T, N, jobs, outs, k=None, io=None):
    own = k is None
    if own:
        k = KB()
    s = k.s
    xT = _io(k, io, "xT", [D, T], F32, "in")
    g = _io(k, io, "g", [128, KC], F32, "in")
    w = _io(k, io, "w", [D, N], F32, "in")
    od = {name: _io(k, io, name, shape, dt, "out") for name, (shape, dt) in outs.items()}
    mk_start = k.mark()
    gsb = k.sb("g", [KC])
    k.load_small(gsb, g)
    rstd = k.sb("rstd", [T])
    xg = k.sb("xg", [KC, T], BF16)
    xstage = [k.sb(f"xst{i}", [T], F32) for i in range(3)]
    sq = [k.sb(f"sq{i}", [T], BF16) for i in range(2)]
    tiles = tiles_of(T)
    norm_load(k, xT, T, gsb, xg, rstd, tiles, xstage, sq)
    rres = [("rstd", i) for i in range(len(tiles))]
    NTB = T // 128
    rcol = k.sb("rcol", [NTB], F32)
    if any(j[0] == "tm" for j in jobs):
        idt = make_ident(k)
        bk = k.bank()
        for tb in range(NTB):
            s.op("pe", lambda e, tb=tb, bk=bk: e.matmul(k.ps[bk][:, tb:tb + 1], lhsT=rstd[:, tb * 128:(tb + 1) * 128],
                                                        rhs=idt[:, 0:1], start=True, stop=True),
                 reads=rres + ["ident"], writes=[f"ps{bk}"])
        s.op("dve", lambda e, bk=bk: e.tensor_copy(out=rcol[:], in_=k.ps[bk][:, 0:NTB]), reads=[f"ps{bk}"],
             writes=["rcol"])
    GW = 512
    wsl = [k.sb(f"w{i}", [KC, GW], BF16) for i in range(2)]
    st16 = [k.sb(f"st16_{i}", [T], BF16) for i in range(2)]
    st32 = [k.sb(f"st32_{i}", [T], F32) for i in range(2)]
    sttm = [k.sb(f"sttm{i}", [GW], BF16) for i in range(3)]
    cnt = {"fm": 0, "tm": 0}
    wv = w.rearrange("(kc p) n -> p kc n", p=128)
    for (mode, col0, ncols, oname, off) in jobs:
        o = od[oname]
        odt = outs[oname][1]
        if mode == "fm":
            pool_ = st16 if odt == BF16 else st32

            def evac(cbg, m, ti, a, b, bk, pool_=pool_, o=o, off=off):
                if ti == 0:
                    cnt["fm"] += 1
                stg = pool_[cnt["fm"] % 2]
                s.op("dve", lambda e: e.tensor_tensor(out=stg[:m, a:b], in0=k.ps[bk][:m, :b - a], in1=rstd[:m, a:b],
                                                      op=ALU.mult),
                     reads=[f"ps{bk}", ("rstd", ti)], writes=[(stg.name, ti)])
                if ti == len(tiles) - 1:
                    s.dma("sp", lambda e: e.dma_start(out=_sub(o, e, off + cbg * 128, off + cbg * 128 + m), in_=stg[:m, :]),
                          reads=[(stg.name, i) for i in range(len(tiles))], writes=[(oname, off, cbg)],
                          slot=stg.name + "o")

            proj_fm(k, w, col0, ncols, KC, lambda kc, a, b: xg[:, kc, a:b], lambda kc: [("xg", kc)], tiles, wsl, GW, evac)
        else:
            ngroups = (ncols + GW - 1) // GW
            ws = WStream(k, wsl)
            geo = []
            for gi in range(ngroups):
                c0 = col0 + gi * GW
                cw = min(GW, col0 + ncols - c0)
                geo.append((c0, cw))
                ws.add(lambda sl, c0=c0, cw=cw: (lambda e: e.dma_start(out=sl[:, :, :cw], in_=wv[:, :, c0:c0 + cw])))
            for gi in range(ngroups):
                c0, cw = geo[gi]
                wt = ws.get(gi)
                ws.prefetch(gi + 1)
                for tb in range(NTB):
                    bk = k.bank()
                    for kc in range(KC):
                        s.op("pe", lambda e, wt=wt, kc=kc, tb=tb, cw=cw, bk=bk: e.matmul(
                            k.ps[bk][:, :cw], lhsT=xg[:, kc, tb * 128:(tb + 1) * 128], rhs=wt[:, kc, :cw],
                            start=(kc == 0), stop=(kc == KC - 1)),
                            reads=[wt.name, ("xg", kc)], writes=[f"ps{bk}"])
                    cnt["tm"] += 1
                    stg = sttm[cnt["tm"] % 3]
                    s.op("act", lambda e, stg=stg, tb=tb, cw=cw, bk=bk: e.activation(
                        out=stg[:, :cw], in_=k.ps[bk][:, :cw], func=AF.Copy, scale=rcol[:, tb:tb + 1]),
                        reads=[f"ps{bk}", "rcol"], writes=[stg.name])
                    s.dma("sp", lambda e, stg=stg, tb=tb, cw=cw, c0=c0, o=o, off=off, col0=col0: e.dma_start(
                        out=_sub(o, e, tb * 128, (tb + 1) * 128, off + c0 - col0, off + c0 - col0 + cw), in_=stg[:, :cw]),
                        reads=[stg.name], writes=[(oname, tb, c0)], slot=stg.name + "o")
    if own:
        k.close()
    else:
        k.release(mk_start)
    return k


def build_outproj(T, layer0, k=None, io=None):
    own = k is None
    if own:
        k = KB()
    s = k.s
    xT = _io(k, io, "xT", [D, T], F32, "in")
    hT = _io(k, io, "hT", [D, T], F32 if layer0 else BF16, "in")
    w_out = _io(k, io, "w_out", [D, D], F32, "in")
    g_post = _io(k, io, "g_post", [128, KC], F32, "in")
    x_out = _io(k, io, "x_out", [D, T], F32, "out")
    y_scr = io["y_scr"] if io else k.dscr("y_scr", [D, T])
    mk_start = k.mark()
    gpost = k.sb("gpost", [KC])
    k.load_small(gpost, g_post)
    hn = k.sb("hn", [KC, T], BF16)
    tiles = tiles_of(T)
    mk0 = k.mark()
    if layer0:
        ogT = _io(k, io, "ogT", [D, T], F32, "in")
        hg = _io(k, io, "hg", [128, KC], F32, "in")
        hgs = k.sb("hg", [KC])
        k.load_small(hgs, hg)
        hst = [k.sb(f"hst{i}", [4, T], F32) for i in range(2)]
        ogs = [k.sb(f"ogs{i}", [T], F32) for i in range(2)]
        sqh = [k.sb(f"sqh{i}", [T], BF16) for i in range(2)]
        rh = [k.sb(f"rh{i}", [T], F32) for i in range(2)]
        for hd in range(8):
            hb = hst[hd % 2]
            banks = [k.hold_bank() for _ in tiles]
            for j in range(4):
                c = hd * 4 + j
                s.dma("sp", lambda e, hb=hb, j=j, c=c: e.dma_start(out=hb[:, j, :], in_=_sub(hT, e, c * 128, (c + 1) * 128)),
                      writes=[(hb.name, j)], slot=f"{hb.name}_{j}")
                q_ = sqh[c % 2]
                s.op("dve", lambda e, hb=hb, j=j, q_=q_: e.tensor_tensor(out=q_[:], in0=hb[:, j, :], in1=hb[:, j, :],
                                                                         op=ALU.mult),
                     reads=[(hb.name, j)], writes=[q_.name])
                for ti, (a, b) in enumerate(tiles):
                    s.op("pe", lambda e, q_=q_, a=a, b=b, bk=banks[ti], j=j: e.matmul(
                        k.ps[bk][:, :b - a], lhsT=k.ones_bf[:], rhs=q_[:, a:b], start=(j == 0), stop=(j == 3)),
                        reads=[q_.name, "ones_bf"], writes=[f"ps{banks[ti]}"])
            r_ = rh[hd % 2]
            for ti, (a, b) in enumerate(tiles):
                rstd_from_psum(k, banks[ti], r_, a, b, 512, (r_.name, ti))
                k.unhold(banks[ti])
            rr = [(r_.name, i) for i in range(len(tiles))]
            for j in range(4):
                c = hd * 4 + j
                og = ogs[c % 2]
                s.dma("sp", lambda e, og=og, c=c: e.dma_start(out=og[:], in_=ogT[c * 128:(c + 1) * 128, :]),
                      writes=[og.name], slot=og.name)
                s.op("act", lambda e, og=og, hd=hd: e.activation(out=og[:], in_=og[:],
                                                                 func=(AF.Sigmoid if hd < 4 else AF.Silu)),
                     reads=[og.name], writes=[og.name])
                s.op("dve", lambda e, hb=hb, j=j, c=c, r_=r_: e.scalar_tensor_tensor(
                    out=hb[:, j, :], in0=hb[:, j, :], scalar=hgs[:, c:c + 1], in1=r_[:], op0=ALU.mult, op1=ALU.mult),
                    reads=[(hb.name, j), "hg"] + rr, writes=[(hb.name, j)])
                s.op("dve", lambda e, hb=hb, j=j, c=c, og=og: e.tensor_tensor(out=hn[:, c, :], in0=hb[:, j, :], in1=og[:],
                                                                              op=ALU.mult),
                     reads=[(hb.name, j), og.name], writes=[("hn", c)])
        k.release(mk0)
    else:
        for i in range(4):
            s.dma("sp", lambda e, i=i: e.dma_start(
                out=hn[:, i * 8:(i + 1) * 8, :],
                in_=_sub(hT, e, i * 1024, (i + 1) * 1024).rearrange("(c p) t -> p c t", p=128)),
                  writes=[("hn", c) for c in range(i * 8, i * 8 + 8)], slot=f"hn{i}")
    GW = 512
    wsl = [k.sb(f"w{i}", [KC, GW], BF16) for i in range(2)]
    ysink = YSink(k, T, y_scr)
    proj_fm(k, w_out, 0, D, KC, lambda kc, a, b: hn[:, kc, a:b], lambda kc: [("hn", kc)], tiles, wsl, GW, ysink.evac)
    k.release(mk0)
    ysink.finish(lambda c: xT[c * 128:(c + 1) * 128, :], gpost, x_out)
    if own:
        k.close()
    else:
        k.release(mk_start)
    return k


XA_W = 1024
NMEM = 256


def build_xattn(T, k=None, io=None):
    own = k is None
    if own:
        k = KB()
    s = k.s
    xT = _io(k, io, "xT", [D, T], F32, "in")
    memT = _io(k, io, "memT", [D, NMEM], F32, "in")
    g_pre = _io(k, io, "g_pre", [128, KC], F32, "in")
    g_mem = _io(k, io, "g_mem", [128, KC], F32, "in")
    g_post = _io(k, io, "g_post", [128, KC], F32, "in")
    wq = _io(k, io, "wq", [D, XA_W], F32, "in")
    wk = _io(k, io, "wk", [D, XA_W], F32, "in")
    wv = _io(k, io, "wv", [D, XA_W], F32, "in")
    wo = _io(k, io, "wo", [XA_W, D], F32, "in")
    x_out = _io(k, io, "x_out", [D, T], F32, "out")
    y_scr = io["y_scr"] if io else k.dscr("y_scr", [D, T])
    mk_start = k.mark()
    gpre = k.sb("gpre", [KC]); gmem = k.sb("gmem", [KC]); gpost = k.sb("gpost", [KC])
    for b_, src in ((gpre, g_pre), (gmem, g_mem), (gpost, g_post)):
        k.load_small(b_, src)
    NQC = XA_W // 128
    KT = k.sb("KT", [NQC, NMEM], BF16)
    VT = k.sb("VT", [NQC, NMEM], BF16)
    V = k.sb("V", [2, XA_W], BF16)
    qT = k.sb("qT", [NQC, T], BF16)
    rstd = k.sb("rstd", [T])
    rstdm = k.sb("rstdm", [NMEM])
    idb = k.sb("identb", [128], BF16)
    idt = make_ident(k)
    s.op("dve", lambda e: e.tensor_copy(out=idb[:], in_=idt[:]), reads=["ident"], writes=["identb"])
    tiles = tiles_of(T)
    tm_ = tiles_of(NMEM)
    mk0 = k.mark()
    xg = k.sb("xg", [KC, T], BF16)
    mg = k.sb("mg", [KC, NMEM], BF16)
    xstage = [k.sb(f"xst{i}", [T], F32) for i in range(3)]
    sq = [k.sb(f"sq{i}", [T], BF16) for i in range(2)]
    GW = 256
    wsl = [k.sb(f"w{i}", [KC, GW], BF16) for i in range(2)]
    norm_load(k, memT, NMEM, gmem, mg, rstdm, tm_, xstage, sq)
    norm_load(k, xT, T, gpre, xg, rstd, tiles, xstage, sq)

    def evac_to(dst, rs, rname):
        def f(cbg, m, ti, a, b, bk):
            s.op("dve", lambda e: e.tensor_tensor(out=dst[:, cbg, a:b], in0=k.ps[bk][:, :b - a], in1=rs[:, a:b],
                                                  op=ALU.mult),
                 reads=[f"ps{bk}", (rname, ti)], writes=[(dst.name, cbg)])
        return f

    proj_fm(k, wk, 0, XA_W, KC, lambda kc, a, b: mg[:, kc, a:b], lambda kc: [("mg", kc)], tm_, wsl, GW,
            evac_to(KT, rstdm, "rstdm"))
    proj_fm(k, wv, 0, XA_W, KC, lambda kc, a, b: mg[:, kc, a:b], lambda kc: [("mg", kc)], tm_, wsl, GW,
            evac_to(VT, rstdm, "rstdm"))
    proj_fm(k, wq, 0, XA_W, KC, lambda kc, a, b: xg[:, kc, a:b], lambda kc: [("xg", kc)], tiles, wsl, GW,
            evac_to(qT, rstd, "rstd"))
    psb = k.sb("pstmp", [128], BF16)
    for c in range(NQC):
        for mb in range(2):
            bk = k.bank()
            pst = k.ps[bk][:].bitcast(BF16)
            s.op("pe", lambda e, c=c, mb=mb, pst=pst: e.transpose(out=pst[:, 0:128], in_=VT[:, c, mb * 128:(mb + 1) * 128],
                                                                  identity=idb[:]),
                 reads=[("VT", c), "identb"], writes=[f"ps{bk}"])
            s.op("act", lambda e, c=c, mb=mb, pst=pst: e.copy(out=V[:, mb, c * 128:(c + 1) * 128], in_=pst[:, 0:128]),
                 reads=[f"ps{bk}"], writes=[("V", mb, c)])
    k.release(mk0)
    oT = k.sb("oT", [NQC, T], BF16)
    pT = [k.sb(f"pT{i}", [2, T], BF16) for i in range(2)]
    ee = [k.sb(f"ee{i}", [NMEM], F32) for i in range(2)]
    pp = [k.sb(f"pp{i}", [NMEM], BF16) for i in range(2)]
    mx = [k.sb(f"mx{i}", [1], F32) for i in range(2)]
    sm = [k.sb(f"sm{i}", [1], F32) for i in range(2)]
    SC = 1.0 / 16.0
    it = 0
    for h in range(4):
        pT_ = pT[h % 2]
        for tb in range(T // 128):
            i2 = it % 2
            it += 1
            bk = k.bank()
            for j in range(2):
                s.op("pe", lambda e, h=h, j=j, tb=tb, bk=bk: e.matmul(
                    k.ps[bk][:, :NMEM], lhsT=qT[:, 2 * h + j, tb * 128:(tb + 1) * 128], rhs=KT[:, 2 * h + j, :],
                    start=(j == 0), stop=(j == 1)),
                    reads=[("qT", 2 * h + j), ("KT", 2 * h + j)], writes=[f"ps{bk}"])
            m_, s_, e_, p_ = mx[i2], sm[i2], ee[i2], pp[i2]
            s.op("dve", lambda e, m_=m_, bk=bk: e.reduce_max(out=m_[:], in_=k.ps[bk][:, :NMEM], axis=AX.X),
                 reads=[f"ps{bk}"], writes=[m_.name])
            s.op("dve", lambda e, m_=m_: e.tensor_scalar(out=m_[:], in0=m_[:], scalar1=-SC, scalar2=None, op0=ALU.mult),
                 reads=[m_.name], writes=[m_.name])
            s.op("act", lambda e, m_=m_, s_=s_, e_=e_, bk=bk: e.activation(
                out=e_[:], in_=k.ps[bk][:, :NMEM], func=AF.Exp, scale=SC, bias=m_[:, 0:1], accum_out=s_[:, 0:1]),
                reads=[f"ps{bk}", m_.name], writes=[e_.name, s_.name])
            s.op("dve", lambda e, s_=s_: e.reciprocal(out=s_[:], in_=s_[:]), reads=[s_.name], writes=[s_.name])
            s.op("dve", lambda e, e_=e_, p_=p_, s_=s_: e.tensor_scalar(out=p_[:], in0=e_[:], scalar1=s_[:, 0:1],
                                                                       scalar2=None, op0=ALU.mult),
                 reads=[e_.name, s_.name], writes=[p_.name])
            for mb in range(2):
                bk2 = k.bank()
                pst = k.ps[bk2][:].bitcast(BF16)
                s.op("pe", lambda e, p_=p_, mb=mb, pst=pst: e.transpose(out=pst[:, 0:128], in_=p_[:, mb * 128:(mb + 1) * 128],
                                                                        identity=idb[:]),
                     reads=[p_.name, "identb"], writes=[f"ps{bk2}"])
                s.op("act", lambda e, pT_=pT_, mb=mb, tb=tb, pst=pst: e.copy(
                    out=pT_[:, mb, tb * 128:(tb + 1) * 128], in_=pst[:, 0:128]),
                    reads=[f"ps{bk2}"], writes=[(pT_.name, mb, tb)])
        for j in range(2):
            for ti, (a, b) in enumerate(tiles):
                bk = k.bank()
                for mb in range(2):
                    s.op("pe", lambda e, h=h, j=j, a=a, b=b, mb=mb, bk=bk, pT_=pT_: e.matmul(
                        k.ps[bk][:, :b - a], lhsT=V[:, mb, (2 * h + j) * 128:(2 * h + j + 1) * 128], rhs=pT_[:, mb, a:b],
                        start=(mb == 0), stop=(mb == 1)),
                        reads=[("V", mb, 2 * h + j)] + [(pT_.name, mb, tb) for tb in range(a // 128, b // 128)],
                        writes=[f"ps{bk}"])
                s.op("act", lambda e, h=h, j=j, a=a, b=b, bk=bk: e.copy(out=oT[:, 2 * h + j, a:b], in_=k.ps[bk][:, :b - a]),
                     reads=[f"ps{bk}"], writes=[("oT", 2 * h + j)])
    mk1 = k.mark()
    wso = [k.sb(f"wo{i}", [NQC, 512], BF16) for i in range(2)]
    ysink = YSink(k, T, y_scr)
    proj_fm(k, wo, 0, D, NQC, lambda kc, a, b: oT[:, kc, a:b], lambda kc: [("oT", kc)], tiles, wso, 512, ysink.evac)
    k.release(mk1)
    ysink.finish(lambda c: xT[c * 128:(c + 1) * 128, :], gpost, x_out)
    if own:
        k.close()
    else:
        k.release(mk_start)
    return k


def build_sb(S, NH, k=None, io=None):
    own = k is None
    if own:
        k = KB()
    s = k.s
    DH = 128
    if io is None:
        qT = k.din("qT", [NH * DH, S], BF16)
        kT = k.din("kT", [NH * DH, S], BF16)
        v = k.din("v", [S, NH * DH], BF16)
        oT = k.dout("oT", [NH * DH, S], BF16)
        vv = v.rearrange("(b p) c -> p b c", p=128)
        q_src = lambda e, h, r: qT[h * DH:(h + 1) * DH, r * (S // 8):(r + 1) * (S // 8)]
        k_src = lambda e, h, r: kT[h * DH:(h + 1) * DH, r * (S // 8):(r + 1) * (S // 8)]
        v_src = lambda e, h: vv[:, :, h * DH:(h + 1) * DH]
        o_dst = lambda e, h, QT: oT[h * DH:(h + 1) * DH, QT * 512:(QT + 1) * 512]
    else:
        q_src, k_src, v_src, o_dst = io["q_src"], io["k_src"], io["v_src"], io["o_dst"]
    mk_start = k.mark()
    NB = S // 128
    NQT = S // 512
    SCALE = DH ** -0.5
    U = k.sb("U", [128], BF16)
    negones = k.sb("negones", [128], BF16)
    utmp = k.sb("utmp", [128], F32)
    s.op("pool", lambda e: e.memset(utmp[:], -1.0), writes=["utmp"])
    s.op("pool", lambda e: e.affine_select(out=utmp[:], in_=utmp[:], pattern=[[-1, 128]], compare_op=ALU.is_ge,
                                           fill=0.0, base=0, channel_multiplier=1), reads=["utmp"], writes=["utmp"])
    s.op("dve", lambda e: e.tensor_copy(out=U[:], in_=utmp[:]), reads=["utmp"], writes=["U"])
    s.op("pool", lambda e: e.memset(negones[:], -1.0), writes=["negones"])
    qs = [k.sb(f"qs{i}", [S], BF16) for i in range(2)]
    ks = [k.sb(f"ks{i}", [S], BF16) for i in range(2)]
    vs = [k.sb(f"vs{i}", [NB, DH], BF16) for i in range(2)]
    e1 = [k.sb(f"e1_{i}", [512], F32) for i in range(2)]
    sp = [k.sb(f"sp{i}", [512], BF16) for i in range(2)]
    arg = [k.sb(f"arg{i}", [512], F32) for i in range(2)]
    att = [k.sb(f"att{i}", [512], BF16) for i in range(2)]
    carry = [k.sb(f"carry{i}", [512], F32) for i in range(2)]
    ost = [k.sb(f"ost{i}", [512], BF16) for i in range(2)]
    step = 0
    qti = 0
    for h in range(NH):
        q_, k_, v_ = qs[h % 2], ks[h % 2], vs[h % 2]
        S8 = S // 8
        for r in range(8):
            s.dma("sp", lambda e, q_=q_, h=h, r=r: e.dma_start(out=q_[:, r * S8:(r + 1) * S8], in_=q_src(e, h, r)),
                  writes=[q_.name] if r == 0 else [(q_.name, r)], slot=f"{q_.name}_{r}")
        s.op("act", lambda e, q_=q_: e.activation(out=q_[:], in_=q_[:], func=AF.Copy, scale=SCALE),
             reads=[q_.name] + [(q_.name, r) for r in range(1, 8)], writes=[q_.name])
        for r in range(8):
            s.dma("sp", lambda e, k_=k_, h=h, r=r: e.dma_start(out=k_[:, r * S8:(r + 1) * S8], in_=k_src(e, h, r)),
                  writes=[k_.name] if r == 0 else [(k_.name, r)], slot=f"{k_.name}_{r}")
        s.dma("sp", lambda e, v_=v_, h=h: e.dma_start(out=v_[:], in_=v_src(e, h)), writes=[v_.name], slot=v_.name)
        kres = [k_.name] + [(k_.name, r) for r in range(1, 8)]
        plan = []
        for QT in range(NQT):
            nkb = 4 * QT + 4
            for idx, kb in enumerate(range(nkb - 1, -1, -1)):
                plan.append({"QT": QT, "idx": idx, "kb": kb, "nkb": nkb})

        def stage0(st, q_=q_, k_=k_, kres=kres):
            nonlocal step, qti
            QT, idx, kb = st["QT"], st["idx"], st["kb"]
            if idx == 0:
                st["cr"] = carry[qti % 2]
                st["o_"] = ost[qti % 2]
                qti += 1
                cr = st["cr"]
                s.op("pool", lambda e: e.memset(cr[:], 0.0), writes=[cr.name])
                st["bO"] = k.hold_bank()
                qt_state[QT] = st
            st["cr"], st["o_"], st["bO"] = qt_state[QT]["cr"], qt_state[QT]["o_"], qt_state[QT]["bO"]
            i = kb - 4 * QT
            qa = 128 * i if i >= 0 else 0
            N = 512 - qa
            q0 = QT * 512 + qa
            st_ = step % 2
            step += 1
            e_, sp_ = e1[st_], sp[st_]
            st["ar_"], st["at_"] = arg[st_], att[st_]
            bA = k.bank()
            st.update(i=i, qa=qa, N=N, q0=q0, bA=bA, e_=e_, sp_=sp_)
            s.op("pe", lambda e: e.matmul(k.ps[bA][:, :N], lhsT=k_[:, kb * 128:(kb + 1) * 128], rhs=q_[:, q0:q0 + N],
                                          start=True, stop=True), reads=kres + [q_.name], writes=[f"ps{bA}"])

        def stage1(st, q_=q_, k_=k_, kres=kres):
            kb = st["kb"]
            i, qa, N, q0, bA, e_, sp_ = st["i"], st["qa"], st["N"], st["q0"], st["bA"], st["e_"], st["sp_"]
            bB, bT = k.bank(), k.bank()
            st.update(bB=bB, bT=bT)
            s.op("act", lambda e: e.activation(out=e_[:, :N], in_=k.ps[bA][:, :N], func=AF.Exp),
                 reads=[f"ps{bA}"], writes=[e_.name])
            s.op("act", lambda e: e.activation(out=sp_[:, :N], in_=e_[:, :N], func=AF.Ln, bias=1.0),
                 reads=[e_.name], writes=[sp_.name])
            if i >= 0:
                s.op("pool", lambda e: e.affine_select(out=sp_[:, 0:128], in_=sp_[:, 0:128], pattern=[[1, 128]],
                                                       compare_op=ALU.is_gt, fill=0.0, base=0, channel_multiplier=-1),
                     reads=[sp_.name], writes=[sp_.name])
            s.op("pe", lambda e: e.matmul(k.ps[bB][:, :N], lhsT=k_[:, kb * 128:(kb + 1) * 128], rhs=q_[:, q0:q0 + N],
                                          start=True, stop=False), reads=kres + [q_.name], writes=[f"ps{bB}"])
            s.op("pe", lambda e: e.matmul(k.ps[bB][:, :N], lhsT=U[:], rhs=sp_[:, :N], start=False, stop=True),
                 reads=[sp_.name, "U"], writes=[f"ps{bB}"])
            s.op("pe", lambda e: e.matmul(k.ps[bT][:, :N], lhsT=negones[:], rhs=sp_[:, :N], start=True, stop=True),
                 reads=[sp_.name, "negones"], writes=[f"ps{bT}"])

        def stage2(st, v_=v_, h=h):
            QT, idx, kb, nkb = st["QT"], st["idx"], st["kb"], st["nkb"]
            cr, o_, bO, ar_, at_ = st["cr"], st["o_"], st["bO"], st["ar_"], st["at_"]
            i, qa, N, bB, bT = st["i"], st["qa"], st["N"], st["bB"], st["bT"]
            s.op("dve", lambda e: e.tensor_tensor(out=ar_[:, :N], in0=k.ps[bB][:, :N], in1=cr[:, qa:512], op=ALU.add),
                 reads=[f"ps{bB}", cr.name], writes=[ar_.name])
            s.op("act", lambda e: e.activation(out=at_[:, :N], in_=ar_[:, :N], func=AF.Exp),
                 reads=[ar_.name], writes=[at_.name])
            if i >= 0:
                s.op("pool", lambda e: e.affine_select(out=at_[:, 0:128], in_=at_[:, 0:128], pattern=[[1, 128]],
                                                       compare_op=ALU.is_gt, fill=0.0, base=0, channel_multiplier=-1),
                     reads=[at_.name], writes=[at_.name])
            s.op("pe", lambda e: e.matmul(k.ps[bO][:, qa:512], lhsT=v_[:, kb, :], rhs=at_[:, :N], start=(idx == 0),
                                          stop=(idx == nkb - 1), skip_group_check=True),
                 reads=[v_.name, at_.name], writes=[f"ps{bO}"])
            s.op("dve", lambda e: e.tensor_tensor(out=cr[:, qa:512], in0=k.ps[bT][:, :N], in1=cr[:, qa:512], op=ALU.add),
                 reads=[f"ps{bT}", cr.name], writes=[cr.name])
            if idx == nkb - 1:
                s.op("act", lambda e: e.copy(out=o_[:], in_=k.ps[bO][:, :]), reads=[f"ps{bO}"], writes=[o_.name])
                k.unhold(bO)
                s.dma("sp", lambda e: e.dma_start(out=o_dst(e, h, QT), in_=o_[:]),
                      reads=[o_.name], writes=[("oT", h, QT)], slot=o_.name + "o")

        qt_state = {}
        NP = len(plan)
        stage0(plan[0])
        if NP > 1:
            stage0(plan[1])
        stage1(plan[0])
        for pi_ in range(NP):
            if pi_ + 2 < NP:
                stage0(plan[pi_ + 2])
            if pi_ + 1 < NP:
                stage1(plan[pi_ + 1])
            stage2(plan[pi_])
    if own:
        k.close()
    else:
        k.release(mk_start)
    return k


LN16 = -2.772588722239781
PAD = 64


def build_ab(S, k=None, io=None):
    own = k is None
    if own:
        k = KB()
    s = k.s
    NSEG = S // 512
    if io is None:
        mqT = k.din("mqT", [256, S], BF16)
        mkT = k.din("mkT", [256, S], BF16)
        mkt = k.din("mkt", [S, 256], BF16)
        mv = k.din("mv", [S, 256], BF16)
        mgate = k.din("mgate", [2, S])
        mbias = k.din("mbias", [1, 2])
        gqT = k.din("gqT", [256, S], BF16)
        gkT = k.din("gkT", [256, S], BF16)
        gv = k.din("gv", [S, 256], BF16)
        grT = k.din("grT", [16, S])
        W2 = k.din("W2", [16, 256])
        b2 = k.din("b2", [128, 2])
        hml = k.dout("hml", [256, S])
        hgla = k.dout("hgla", [256, S])
        scr = k.dscr("gscr", [3, S + 2 * PAD])
        fm3 = lambda t: t.rearrange("(j p) t -> p j t", p=128)
        tk3 = lambda t, t0: t[t0:t0 + 512, :].rearrange("(c s) d -> s c d", s=64)
        acc = {
            "mq": lambda e, t0: fm3(mqT)[:, :, t0:t0 + 512], "mk": lambda e, t0: fm3(mkT)[:, :, t0:t0 + 512],
            "gq": lambda e, t0: fm3(gqT)[:, :, t0:t0 + 512], "gk": lambda e, t0: fm3(gkT)[:, :, t0:t0 + 512],
            "mkt": lambda e, t0: tk3(mkt, t0), "mv": lambda e, t0: tk3(mv, t0), "gv": lambda e, t0: tk3(gv, t0),
            "gr": lambda e, t0: grT[:, t0:t0 + 512],
            "gate": lambda e, w, r: mgate[w:w + 1, r * (S // 8):(r + 1) * (S // 8)],
            "bias": lambda e, w: mbias[0:1, w:w + 1],
            "W2": lambda e: W2, "b2": lambda e: b2,
            "hml": lambda e, t0: fm3(hml)[:, :, t0:t0 + 512], "hgla": lambda e, t0: fm3(hgla)[:, :, t0:t0 + 512],
        }
    else:
        acc = io["acc"]
        scr = io["scr"]
    mk_start = k.mark()

    idt = make_ident(k)
    idb = k.sb("identb", [128], BF16)
    s.op("dve", lambda e: e.tensor_copy(out=idb[:], in_=idt[:]), reads=["ident"], writes=["identb"])
    tri = k.sb("tri", [64], F32)
    nm = k.sb("nm", [64], F32)
    s.op("pool", lambda e: e.memset(tri[:], 1.0), writes=["tri"])
    s.op("pool", lambda e: e.affine_select(out=tri[:64, :], in_=tri[:64, :], pattern=[[1, 64]], compare_op=ALU.is_ge,
                                           fill=0.0, base=0, channel_multiplier=-1), reads=["tri"], writes=["tri"])
    s.op("pool", lambda e: e.memset(nm[:], LN16), writes=["nm"])
    s.op("pool", lambda e: e.affine_select(out=nm[:64, :], in_=nm[:64, :], pattern=[[1, 64]], compare_op=ALU.is_ge,
                                           fill=-1e30, base=0, channel_multiplier=-1), reads=["nm"], writes=["nm"])

    mk0 = k.mark()
    gi = k.sb("gi", [S], F32)
    gf = k.sb("gf", [S], F32)
    ones_r = k.sb("ones_r", [S], F32)
    bn = k.sb("bn", [S], F32)
    arow = Buf(gi.ap, "gi")
    mrow = Buf(gf.ap, "gf")
    mtrow = Buf(bn.ap, "bn")
    bsb = k.sb("bsb", [2], F32)
    zpad = k.sb("zpad", [PAD], F32)
    S8 = S // 8
    for r in range(8):
        s.dma("sp", lambda e, r=r: e.dma_start(out=gi[0:1, r * S8:(r + 1) * S8], in_=acc["gate"](e, 0, r)),
              writes=[("gi", r)], slot=f"gi{r}")
        s.dma("sp", lambda e, r=r: e.dma_start(out=gf[0:1, r * S8:(r + 1) * S8], in_=acc["gate"](e, 1, r)),
              writes=[("gf", r)], slot=f"gf{r}")
    s.dma("sp", lambda e: e.dma_start(out=bsb[0:1, 0:1], in_=acc["bias"](e, 0)), writes=["bsb"], slot="bsb")
    s.dma("sp", lambda e: e.dma_start(out=bsb[0:1, 1:2], in_=acc["bias"](e, 1)), writes=[("bsb", 1)], slot="bsb1")
    s.op("dve", lambda e: e.tensor_scalar(out=bsb[0:1, :], in0=bsb[0:1, :], scalar1=1.0 / 15.0, scalar2=None, op0=ALU.mult),
         reads=["bsb", ("bsb", 1)], writes=["bsb"])
    s.op("pool", lambda e: e.memset(ones_r[0:1, :], 1.0), writes=["ones_r"])
    s.op("pool", lambda e: e.memset(zpad[0:1, :], 0.0), writes=["zpad"])
    s.op("act", lambda e: e.activation(out=gi[0:1, :], in_=gi[0:1, :], func=AF.Tanh, scale=1.0 / 15.0, bias=bsb[0:1, 0:1]),
         reads=[("gi", r) for r in range(8)] + ["bsb"], writes=["gi"])
    s.op("act", lambda e: e.activation(out=gf[0:1, :], in_=gf[0:1, :], func=AF.Tanh, scale=1.0 / 15.0, bias=bsb[0:1, 1:2]),
         reads=[("gf", r) for r in range(8)] + ["bsb"], writes=["gf"])
    s.op("act", lambda e: e.activation(out=gf[0:1, :], in_=gf[0:1, :], func=AF.Exp, scale=-15.0), reads=["gf"], writes=["gf"])
    s.op("act", lambda e: e.activation(out=gf[0:1, :], in_=gf[0:1, :], func=AF.Ln, bias=1.0), reads=["gf"], writes=["gf"])
    s.op("dve", lambda e: e.tensor_tensor_scan(out=bn[0:1, :], data0=ones_r[0:1, :], data1=gf[0:1, :], initial=0.0,
                                               op0=ALU.mult, op1=ALU.add), reads=["ones_r", "gf"], writes=["bn"])
    s.op("dve", lambda e: e.scalar_tensor_tensor(out=arow[0:1, :], in0=gi[0:1, :], scalar=15.0, in1=bn[0:1, :],
                                                 op0=ALU.mult, op1=ALU.add), reads=["gi", "bn"], writes=["gi"])
    s.op("dve", lambda e: e.tensor_tensor_scan(out=mrow[0:1, :], data0=ones_r[0:1, :], data1=arow[0:1, :], initial=0.0,
                                               op0=ALU.mult, op1=ALU.max), reads=["ones_r", "gi", "gf"], writes=["gf"])
    s.op("dve", lambda e: e.tensor_tensor(out=mtrow[0:1, :], in0=mrow[0:1, :], in1=bn[0:1, :], op=ALU.subtract),
         reads=["gf", "bn"], writes=["bn"])
    for r, (row, nm_) in enumerate(((arow, "gi"), (mrow, "gf"), (mtrow, "bn"))):
        s.dma("sp", lambda e, r=r: e.dma_start(out=scr[r:r + 1, 0:PAD], in_=zpad[0:1, :]), reads=["zpad"],
              writes=[("scrpad", r)], slot=f"scrp{r}")
        s.dma("sp", lambda e, r=r, row=row: e.dma_start(out=scr[r:r + 1, PAD:PAD + S], in_=row[0:1, :]), reads=[nm_],
              writes=[("scr", r)], slot=f"scr{r}")
    k.release(mk0)
    scr_reads = [("scr", r) for r in range(3)] + [("scrpad", r) for r in range(3)]

    CN = k.sb("CN", [2, 384], F32)
    CNb2 = [k.sb(f"CNb_{p}", [2, 384], BF16) for p in range(2)]
    SS = k.sb("SS", [2, 256], F32)
    SSb2 = [k.sb(f"SSb_{p}", [2, 256], BF16) for p in range(2)]
    s.op("pool", lambda e: e.memset(CN[:], 0.0), writes=["CN0", "CN1"])
    for p_ in range(2):
        s.op("pool", lambda e, p_=p_: e.memset(CNb2[p_][:], 0.0), writes=[f"CNb{p_}0", f"CNb{p_}1"])
        s.op("pool", lambda e, p_=p_: e.memset(SSb2[p_][:], 0.0), writes=[f"SSb{p_}0", f"SSb{p_}1"])
    s.op("pool", lambda e: e.memset(SS[:], 0.0), writes=["SS0", "SS1"])
    W2s = k.sb("W2s", [256], F32)
    W2b = k.sb("W2b", [256], BF16)
    nb2 = k.sb("nb2", [2], F32)
    s.dma("sp", lambda e: e.dma_start(out=W2s[0:16, :], in_=acc["W2"](e)), writes=["W2s"], slot="W2s")
    s.op("act", lambda e: e.copy(out=W2b[0:16, :], in_=W2s[0:16, :]), reads=["W2s"], writes=["W2b"])
    s.dma("sp", lambda e: e.dma_start(out=nb2[:], in_=acc["b2"](e)), writes=["nb2"], slot="nb2")
    s.op("dve", lambda e: e.tensor_scalar(out=nb2[:], in0=nb2[:], scalar1=-1.0, scalar2=None, op0=ALU.mult),
         reads=["nb2"], writes=["nb2"])

    def dbl(name, fshape, dt):
        return [k.sb(f"{name}{i}", fshape, dt) for i in range(2)]

    Mrep, mtrep = dbl("Mrep", [8, 64], F32), dbl("mtrep", [512], F32)
    acol, Mend = dbl("acol", [8], F32), dbl("Mend", [9], F32)
    rr, Dm = dbl("rr", [8, 64], F32), dbl("Dm", [8, 64], F32)
    wcol, dec = dbl("wcol", [8], F32), dbl("dec", [8], F32)
    qsg, ksg = dbl("mq", [2, 512], BF16), dbl("mk", [2, 512], BF16)
    qsc = dbl("qsc", [2, 512], BF16)
    ktk, vtk = dbl("mkt", [8, 256], BF16), dbl("mv", [8, 256], BF16)
    hst = dbl("hst", [2, 512], F32)
    gq, gk = dbl("gq", [2, 512], BF16), dbl("gk", [2, 512], BF16)
    gqt, gkt = dbl("gqt", [2, 512], BF16), dbl("gkt", [2, 512], BF16)
    gvt = dbl("gvt", [8, 256], BF16)
    grs, grb = dbl("grs", [512], F32), dbl("grb", [512], BF16)
    ez, la = dbl("ez", [2, 512], F32), dbl("la", [2, 512], F32)
    GG = dbl("GG", [2, 8, 64], F32)
    egq, engk = dbl("egq", [2, 512], F32), dbl("engk", [2, 512], F32)
    egl = dbl("egl", [2, 8], F32)
    ost = dbl("ost", [2, 512], F32)
    PT, kw, dd = dbl("PT", [64], BF16), dbl("kw", [256], BF16), dbl("dd", [64], F32)
    PG, ktil = dbl("PG", [64], BF16), dbl("ktil", [256], BF16)
    onesg = k.sb("ones_g", [S if S < 512 else 512], F32)
    s.op("pool", lambda e: e.memset(onesg[:], 1.0), writes=["ones_g"])


    def ld(buf, src, res=None, extra_reads=()):
        s.dma("sp", lambda e: e.dma_start(out=buf, in_=src), reads=list(extra_reads), writes=[res], slot=res)

    def seg_ml(g):
        b = g % 2
        t0 = g * 512
        Mr, mt_, ac, Me = Mrep[b], mtrep[b], acol[b], Mend[b]
        s.dma("sp", lambda e, Mr=Mr, t0=t0: e.dma_start(
            out=Mr[:].rearrange("p c t -> p (c t)"), in_=scr[1:2, PAD + t0:PAD + t0 + 512].broadcast_to([128, 512])),
            reads=scr_reads, writes=[Mr.name], slot=Mr.name)
        s.dma("sp", lambda e, mt_=mt_, t0=t0: e.dma_start(
            out=mt_[:], in_=scr[2:3, PAD + t0:PAD + t0 + 512].broadcast_to([128, 512])),
            reads=scr_reads, writes=[mt_.name], slot=mt_.name)
        s.dma("sp", lambda e, ac=ac, t0=t0: e.dma_start(
            out=ac[0:64, :], in_=scr[0:1, PAD + t0:PAD + t0 + 512].rearrange("o (c s) -> (o s) c", s=64),
            allow_slow_non_contiguous=True), reads=scr_reads, writes=[ac.name], slot=ac.name)
        s.dma("sp", lambda e, Me=Me, t0=t0: e.dma_start(
            out=Me[:], in_=scr[1:2, PAD + t0 - 1:PAD + t0 - 1 + 9 * 64].rearrange("o (i s) -> o i s", s=64)[:, :, 0]
            .broadcast_to([128, 9]), allow_slow_non_contiguous=True), reads=scr_reads, writes=[Me.name], slot=Me.name)
        q_, k_, kt_, v_ = qsg[b], ksg[b], ktk[b], vtk[b]
        s.dma("sp", lambda e, q_=q_, t0=t0: e.dma_start(out=q_[:], in_=acc["mq"](e, t0)), writes=[q_.name], slot=q_.name)
        s.dma("sp", lambda e, k_=k_, t0=t0: e.dma_start(out=k_[:], in_=acc["mk"](e, t0)), writes=[k_.name], slot=k_.name)
        s.dma("sp", lambda e, kt_=kt_, t0=t0: e.dma_start(
            out=kt_[0:64], in_=acc["mkt"](e, t0)), writes=[kt_.name], slot=kt_.name)
        s.dma("sp", lambda e, v_=v_, t0=t0: e.dma_start(
            out=v_[0:64], in_=acc["mv"](e, t0)), writes=[v_.name], slot=v_.name)
        r_, D_, w_, d_, qs_, h_ = rr[b], Dm[b], wcol[b], dec[b], qsc[b], hst[b]
        s.op("dve", lambda e, r_=r_, Me=Me, Mr=Mr: e.tensor_tensor(
            out=r_[:], in0=Me[:, 0:8].unsqueeze(2).to_broadcast([128, 8, 64]), in1=Mr[:], op=ALU.subtract),
            reads=[Me.name, Mr.name], writes=[r_.name])
        s.op("act", lambda e, r_=r_: e.activation(out=r_[:], in_=r_[:], func=AF.Exp), reads=[r_.name], writes=[r_.name])
        for j in range(2):
            s.op("dve", lambda e, qs_=qs_, q_=q_, r_=r_, j=j: e.tensor_tensor(
                out=qs_[:, j, :], in0=q_[:, j, :], in1=r_[:].rearrange("p c t -> p (c t)"), op=ALU.mult),
                reads=[q_.name, r_.name], writes=[(qs_.name, j)])
        s.op("dve", lambda e, D_=D_, ac=ac, Mr=Mr: e.tensor_tensor(
            out=D_[0:64], in0=ac[0:64, :].unsqueeze(2).to_broadcast([64, 8, 64]), in1=Mr[0:64], op=ALU.subtract),
            reads=[ac.name, Mr.name], writes=[D_.name])
        s.op("dve", lambda e, D_=D_: e.tensor_tensor(
            out=D_[0:64], in0=D_[0:64], in1=nm[0:64, :].unsqueeze(1).to_broadcast([64, 8, 64]), op=ALU.add),
            reads=[D_.name, "nm"], writes=[D_.name])
        s.op("act", lambda e, D_=D_: e.activation(out=D_[0:64], in_=D_[0:64], func=AF.Exp), reads=[D_.name], writes=[D_.name])
        s.op("dve", lambda e, w_=w_, ac=ac, Me=Me: e.tensor_tensor(out=w_[0:64, :], in0=ac[0:64, :], in1=Me[0:64, 1:9],
                                                                   op=ALU.subtract), reads=[ac.name, Me.name], writes=[w_.name])
        s.op("act", lambda e, w_=w_: e.activation(out=w_[0:64, :], in_=w_[0:64, :], func=AF.Exp, bias=nm[0:64, 63:64]),
             reads=[w_.name, "nm"], writes=[w_.name])
        s.op("dve", lambda e, d_=d_, Me=Me: e.tensor_tensor(out=d_[:], in0=Me[:, 0:8], in1=Me[:, 1:9], op=ALU.subtract),
             reads=[Me.name], writes=[d_.name])
        s.op("act", lambda e, d_=d_: e.activation(out=d_[:], in_=d_[:], func=AF.Exp), reads=[d_.name], writes=[d_.name])
        s.op("act", lambda e, mt_=mt_: e.activation(out=mt_[:], in_=mt_[:], func=AF.Exp, scale=-1.0), reads=[mt_.name],
             writes=[mt_.name])
        def chunk(c):
            cs = slice(c * 64, (c + 1) * 64)
            gc = g * 8 + c
            ci = gc % 2
            CNr, CNw = CNb2[(gc + 1) % 2], CNb2[gc % 2]
            pr, pw = (gc + 1) % 2, gc % 2
            P_, kw_, dd_ = PT[ci], kw[ci], dd[ci]
            bS = k.bank()
            for j in range(2):
                s.op("pe", lambda e, j=j: e.matmul(
                    k.ps[bS][:64, :64], lhsT=k_[:, j, cs], rhs=q_[:, j, cs], start=(j == 0), stop=(j == 1)),
                    reads=[k_.name, q_.name], writes=[f"ps{bS}"])
            s.op("dve", lambda e: e.tensor_tensor(
                out=P_[0:64, :], in0=k.ps[bS][:64, :64], in1=D_[0:64, c, :], op=ALU.mult),
                reads=[f"ps{bS}", D_.name], writes=[P_.name])
            s.op("act", lambda e: e.activation(
                out=kw_[0:64, :], in_=kt_[0:64, c, :], func=AF.Copy, scale=w_[0:64, c:c + 1]),
                reads=[kt_.name, w_.name], writes=[kw_.name])
            bCs = []
            for j in range(2):
                bC = k.bank()
                bCs.append(bC)
                s.op("pe", lambda e, j=j, bC=bC: e.matmul(
                    k.ps[bC][:, 0:256], lhsT=kw_[0:64, j * 128:(j + 1) * 128], rhs=v_[0:64, c, :], start=True, stop=True),
                    reads=[kw_.name, v_.name], writes=[f"ps{bC}"])
                s.op("pe", lambda e, j=j, bC=bC: e.matmul(
                    k.ps[bC][:, 256:384], lhsT=kw_[0:64, j * 128:(j + 1) * 128], rhs=k.ones_bf[0:64, :], start=True,
                    stop=True), reads=[kw_.name, "ones_bf"], writes=[f"ps{bC}"])
            for j in range(2):
                bC = bCs[j]
                s.op("dve", lambda e, j=j, bC=bC: e.scalar_tensor_tensor(
                    out=CN[:, j, :], in0=CN[:, j, :], scalar=d_[:, c:c + 1], in1=k.ps[bC][:, 0:384], op0=ALU.mult,
                    op1=ALU.add), reads=[f"CN{j}", d_.name, f"ps{bC}"], writes=[f"CN{j}"])
                s.op("act", lambda e, j=j: e.copy(out=CNw[:, j, :], in_=CN[:, j, :]), reads=[f"CN{j}"],
                     writes=[f"CNb{pw}{j}"])
            bN = k.bank()
            grp = [(0, lambda: v_[0:64, c, 0:128], lambda j: CNr[:, j, 0:128]),
                   (1, lambda: v_[0:64, c, 128:256], lambda j: CNr[:, j, 128:256]),
                   (2, lambda: k.ones_bf[0:64, :], lambda j: CNr[:, j, 256:384])]
            for (gi_, lh0, lhj) in grp:
                oc = slice(gi_ * 64, gi_ * 64 + 64)
                s.op("pe", lambda e, lh0=lh0, oc=oc: e.matmul(
                    k.ps[bN][:, oc], lhsT=lh0(), rhs=P_[0:64, :], start=True, stop=False),
                    reads=[v_.name, P_.name, "ones_bf"], writes=[f"ps{bN}"])
                for j in range(2):
                    s.op("pe", lambda e, lhj=lhj, j=j, oc=oc: e.matmul(
                        k.ps[bN][:, oc], lhsT=lhj(j), rhs=qs_[:, j, cs], start=False, stop=(j == 1)),
                        reads=[f"CNb{pr}{j}", (qs_.name, j)], writes=[f"ps{bN}"])
            s.op("act", lambda e: e.activation(out=dd_[:], in_=k.ps[bN][:, 128:192], func=AF.Abs),
                 reads=[f"ps{bN}"], writes=[dd_.name])
            s.op("dve", lambda e: e.tensor_tensor(out=dd_[:], in0=dd_[:], in1=mt_[:, cs], op=ALU.max),
                 reads=[dd_.name, mt_.name], writes=[dd_.name])
            s.op("dve", lambda e: e.reciprocal(out=dd_[:], in_=dd_[:]), reads=[dd_.name], writes=[dd_.name])
            for i in range(2):
                s.op("dve", lambda e, i=i: e.tensor_tensor(
                    out=h_[:, i, cs], in0=k.ps[bN][:, i * 64:(i + 1) * 64], in1=dd_[:], op=ALU.mult),
                    reads=[f"ps{bN}", dd_.name], writes=[(h_.name, i, c)])
        for c in range(8):
            chunk(c)
        s.dma("sp", lambda e, h_=h_, t0=t0: e.dma_start(out=acc["hml"](e, t0), in_=h_[:]),
              reads=[(h_.name, i, c) for i in range(2) for c in range(8)], writes=[("hml", g)], slot=h_.name + "o")

    def seg_gla(g):
        b = g % 2
        t0 = g * 512
        q_, k_, v_, gs_, gb_ = gq[b], gk[b], gvt[b], grs[b], grb[b]
        s.dma("sp", lambda e, q_=q_, t0=t0: e.dma_start(out=q_[:], in_=acc["gq"](e, t0)), writes=[q_.name], slot=q_.name)
        s.dma("sp", lambda e, k_=k_, t0=t0: e.dma_start(out=k_[:], in_=acc["gk"](e, t0)), writes=[k_.name], slot=k_.name)
        s.dma("sp", lambda e, v_=v_, t0=t0: e.dma_start(
            out=v_[0:64], in_=acc["gv"](e, t0)), writes=[v_.name], slot=v_.name)
        s.dma("sp", lambda e, gs_=gs_, t0=t0: e.dma_start(out=gs_[0:16, :], in_=acc["gr"](e, t0)), writes=[gs_.name],
              slot=gs_.name)
        s.op("act", lambda e, gs_=gs_, gb_=gb_: e.copy(out=gb_[0:16, :], in_=gs_[0:16, :]), reads=[gs_.name], writes=[gb_.name])
        ez_, la_, G_, eq_, ek_, el_, qt_, kt2, o_ = ez[b], la[b], GG[b], egq[b], engk[b], egl[b], gqt[b], gkt[b], ost[b]
        for j in range(2):
            bZ = k.bank()
            s.op("pe", lambda e, j=j, gb_=gb_, bZ=bZ: e.matmul(
                k.ps[bZ][:, :512], lhsT=W2b[0:16, j * 128:(j + 1) * 128], rhs=gb_[0:16, :], start=True, stop=True),
                reads=["W2b", gb_.name], writes=[f"ps{bZ}"])
            s.op("act", lambda e, j=j, ez_=ez_, bZ=bZ: e.activation(
                out=ez_[:, j, :], in_=k.ps[bZ][:, :512], func=AF.Exp, scale=-1.0, bias=nb2[:, j:j + 1]),
                reads=[f"ps{bZ}", "nb2"], writes=[(ez_.name, j)])
            s.op("act", lambda e, j=j, ez_=ez_, la_=la_: e.activation(out=la_[:, j, :], in_=ez_[:, j, :], func=AF.Ln, bias=1.0),
                 reads=[(ez_.name, j)], writes=[(la_.name, j)])
            for c in range(8):
                s.op("dve", lambda e, j=j, c=c, G_=G_, la_=la_: e.tensor_tensor_scan(
                    out=G_[:, j, c, :], data0=onesg[:, 0:64], data1=la_[:, j, c * 64:(c + 1) * 64], initial=0.0,
                    op0=ALU.mult, op1=ALU.add), reads=[(la_.name, j), "ones_g"], writes=[(G_.name, j)])
            s.op("act", lambda e, j=j, G_=G_, eq_=eq_: e.activation(
                out=eq_[:, j, :], in_=G_[:, j].rearrange("p c t -> p (c t)"), func=AF.Exp, scale=-1.0 / 16.0,
                bias=nm[:, 63:64]), reads=[(G_.name, j), "nm"], writes=[(eq_.name, j)])
            s.op("act", lambda e, j=j, G_=G_, ek_=ek_: e.activation(
                out=ek_[:, j, :], in_=G_[:, j].rearrange("p c t -> p (c t)"), func=AF.Exp, scale=1.0 / 16.0),
                reads=[(G_.name, j)], writes=[(ek_.name, j)])
            s.op("act", lambda e, j=j, G_=G_, el_=el_: e.activation(
                out=el_[:, j, :], in_=G_[:, j, :, 63], func=AF.Exp, scale=-1.0 / 16.0),
                reads=[(G_.name, j)], writes=[(el_.name, j)])
            s.op("dve", lambda e, j=j, qt_=qt_, q_=q_, eq_=eq_: e.tensor_tensor(
                out=qt_[:, j, :], in0=q_[:, j, :], in1=eq_[:, j, :], op=ALU.mult),
                reads=[q_.name, (eq_.name, j)], writes=[(qt_.name, j)])
            s.op("dve", lambda e, j=j, kt2=kt2, k_=k_, ek_=ek_: e.tensor_tensor(
                out=kt2[:, j, :], in0=k_[:, j, :], in1=ek_[:, j, :], op=ALU.mult),
                reads=[k_.name, (ek_.name, j)], writes=[(kt2.name, j)])
        def chunk(c):
            cs = slice(c * 64, (c + 1) * 64)
            gc = g * 8 + c
            ci = gc % 2
            SSr, SSw = SSb2[(gc + 1) % 2], SSb2[gc % 2]
            pr, pw = (gc + 1) % 2, gc % 2
            P_, kl_ = PG[ci], ktil[ci]
            bA = k.bank()
            for j in range(2):
                s.op("pe", lambda e, j=j: e.matmul(
                    k.ps[bA][:64, :64], lhsT=kt2[:, j, cs], rhs=qt_[:, j, cs], start=(j == 0), stop=(j == 1)),
                    reads=[(kt2.name, j), (qt_.name, j)], writes=[f"ps{bA}"])
            s.op("dve", lambda e: e.tensor_tensor(out=P_[0:64, :], in0=k.ps[bA][:64, :64], in1=tri[0:64, :], op=ALU.mult),
                 reads=[f"ps{bA}", "tri"], writes=[P_.name])
            bT = k.bank()
            pst = k.ps[bT][:].bitcast(BF16)
            for j in range(2):
                s.op("pe", lambda e, j=j: e.transpose(
                    out=pst[0:64, j * 128:(j + 1) * 128], in_=kt2[:, j, cs], identity=idb[:]),
                    reads=[(kt2.name, j), "identb"], writes=[f"ps{bT}"])
            s.op("act", lambda e: e.copy(out=kl_[0:64, :], in_=pst[0:64, 0:256]), reads=[f"ps{bT}"], writes=[kl_.name])
            bCs = []
            for j in range(2):
                bC = k.bank()
                bCs.append(bC)
                s.op("pe", lambda e, j=j, bC=bC: e.matmul(
                    k.ps[bC][:, 0:256], lhsT=kl_[0:64, j * 128:(j + 1) * 128], rhs=v_[0:64, c, :], start=True, stop=True),
                    reads=[kl_.name, v_.name], writes=[f"ps{bC}"])
            for j in range(2):
                bC = bCs[j]
                s.op("dve", lambda e, j=j, bC=bC: e.tensor_tensor(out=SS[:, j, :], in0=SS[:, j, :], in1=k.ps[bC][:, 0:256],
                                                                  op=ALU.add), reads=[f"SS{j}", f"ps{bC}"], writes=[f"SS{j}"])
                s.op("dve", lambda e, j=j: e.tensor_scalar(
                    out=SS[:, j, :], in0=SS[:, j, :], scalar1=el_[:, j, c:c + 1], scalar2=None, op0=ALU.mult),
                    reads=[f"SS{j}", (el_.name, j)], writes=[f"SS{j}"])
                s.op("act", lambda e, j=j: e.copy(out=SSw[:, j, :], in_=SS[:, j, :]), reads=[f"SS{j}"],
                     writes=[f"SSb{pw}{j}"])
            bO = k.bank()
            for i in range(2):
                oc = slice(i * 64, i * 64 + 64)
                s.op("pe", lambda e, i=i, oc=oc: e.matmul(
                    k.ps[bO][:, oc], lhsT=v_[0:64, c, i * 128:(i + 1) * 128], rhs=P_[0:64, :], start=True, stop=False),
                    reads=[v_.name, P_.name], writes=[f"ps{bO}"])
                for j in range(2):
                    s.op("pe", lambda e, i=i, j=j, oc=oc: e.matmul(
                        k.ps[bO][:, oc], lhsT=SSr[:, j, i * 128:(i + 1) * 128], rhs=qt_[:, j, cs], start=False,
                        stop=(j == 1)), reads=[f"SSb{pr}{j}", (qt_.name, j)], writes=[f"ps{bO}"])
            for i in range(2):
                s.op("act", lambda e, i=i: e.copy(out=o_[:, i, cs], in_=k.ps[bO][:, i * 64:(i + 1) * 64]),
                     reads=[f"ps{bO}"], writes=[(o_.name, i, c)])
        for c in range(8):
            chunk(c)
        s.dma("sp", lambda e, o_=o_, t0=t0: e.dma_start(out=acc["hgla"](e, t0), in_=o_[:]),
              reads=[(o_.name, i, c) for i in range(2) for c in range(8)], writes=[("hgla", g)], slot=o_.name + "o")

    for g in range(NSEG):
        seg_ml(g)
        seg_gla(g)
    if own:
        k.close()
    else:
        k.release(mk_start)
    return k


def build_cd(T, layer0):
    k = KB()
    io = {"xT": k.din("xT", [D, T]), "hT": k.din("hT", [D, T], F32 if layer0 else BF16), "w_out": k.din("w_out", [D, D]),
          "g_post": k.din("g_post_mix", [128, KC]), "x_out": k.dscr("x_mid", [D, T]), "y_scr": k.dscr("y_scr", [D, T])}
    if layer0:
        io["ogT"] = k.din("ogT", [D, T])
        io["hg"] = k.din("hg", [128, KC])
    build_outproj(T, layer0, k=k, io=io)
    io2 = {"xT": io["x_out"], "memT": k.din("memT", [D, NMEM]), "g_pre": k.din("g_pre", [128, KC]),
           "g_mem": k.din("g_mem", [128, KC]), "g_post": k.din("g_post", [128, KC]), "wq": k.din("wq", [D, XA_W]),
           "wk": k.din("wk", [D, XA_W]), "wv": k.din("wv", [D, XA_W]), "wo": k.din("wo", [XA_W, D]),
           "x_out": k.dout("x_out", [D, T]), "y_scr": io["y_scr"]}
    build_xattn(T, k=k, io=io2)
    k.close()
    return k


SEQ = 8192


def _gl(v):
    v = np.ascontiguousarray(np.asarray(v, np.float32).reshape(-1, 128).T)
    return v


def _launch(k, in_maps):
    res = run_bass_kernel_spmd(k.nc, in_maps, core_ids=list(range(len(in_maps))))
    return res.results


def _c(a):
    return np.ascontiguousarray(a)


def kernel(x, mem, mix_norm_pre, mix_norm_post, ab_w_in, ml_i_bias, ml_f_bias, ml_head_norm,
           gla_w_gate, gla_gate_bias, gla_head_norm, ab_w_out, sb_w_qkv, sb_w_out,
           xa_norm_pre, xa_norm_post, mem_norm, xa_wq, xa_wk, xa_wv, xa_wo,
           ffn_norm_pre, ffn_norm_post, ffn_w_gate, ffn_w_up, ffn_conv_w, ffn_conv_b, ffn_w_down):
    f32 = np.float32
    NC_ = NCORES
    T = TOK
    x = np.asarray(x, f32)
    xT = [_c(x[0, c * T:(c + 1) * T, :].T) for c in range(NC_)]
    memT = _c(np.asarray(mem, f32)[0].T)
    for layer in range(2):
        if layer == 0:
            jobs = [("fm", 0, 1024, "qk", 0), ("fm", 1024, 1024, "qk", 1024), ("fm", 6152, 1024, "qk", 2048),
                    ("fm", 7176, 1024, "qk", 3072), ("fm", 4096, 2048, "og", 0), ("fm", 10248, 2048, "og", 2048),
                    ("fm", 6144, 8, "gt", 0), ("fm", 12296, 16, "gt", 8),
                    ("tm", 1024, 1024, "vt", 0), ("tm", 2048, 2048, "vt", 1024), ("tm", 8200, 2048, "vt", 3072)]
            outs = {"qk": ([4096, T], BF16), "og": ([4096, T], F32), "gt": ([24, T], F32), "vt": ([T, 5120], BF16)}
            w_in = np.asarray(ab_w_in, f32)[0]
            kA = build_proj(T, w_in.shape[1], jobs, outs)
            g = _gl(mix_norm_pre[layer])
            rA = _launch(kA, [{"xT": xT[c], "g": g, "w": w_in} for c in range(NC_)])
            qk = np.concatenate([r["qk"] for r in rA], axis=1)
            vt = np.concatenate([r["vt"] for r in rA], axis=0)
            gt = np.concatenate([r["gt"] for r in rA], axis=1)
            kB = build_ab(SEQ)
            W2 = np.asarray(gla_w_gate, f32)[0]
            gb = np.asarray(gla_gate_bias, f32)[0]
            insB = []
            for c in range(NC_):
                hh, half = c // 2, c % 2
                insB.append({
                    "mqT": _c(qk[hh * 256:(hh + 1) * 256]), "mkT": _c(qk[1024 + hh * 256:1024 + (hh + 1) * 256]),
                    "mkt": _c(vt[:, hh * 256:(hh + 1) * 256]),
                    "mv": _c(vt[:, 1024 + hh * 512 + half * 256:1024 + hh * 512 + half * 256 + 256]),
                    "mgate": _c(gt[[hh, 4 + hh]]),
                    "mbias": np.array([[np.asarray(ml_i_bias, f32)[0, hh], np.asarray(ml_f_bias, f32)[0, hh]]], f32),
                    "gqT": _c(qk[2048 + hh * 256:2048 + (hh + 1) * 256]), "gkT": _c(qk[3072 + hh * 256:3072 + (hh + 1) * 256]),
                    "gv": _c(vt[:, 3072 + hh * 512 + half * 256:3072 + hh * 512 + half * 256 + 256]),
                    "grT": _c(gt[8:24]), "W2": _c(W2[:, hh * 256:(hh + 1) * 256]),
                    "b2": _c(gb[hh * 256:(hh + 1) * 256].reshape(2, 128).T)})
            rB = _launch(kB, insB)
            hT = np.empty((4096, SEQ), f32)
            for c in range(NC_):
                hh, half = c // 2, c % 2
                hT[hh * 512 + half * 256:hh * 512 + half * 256 + 256] = rB[c]["hml"]
                hT[2048 + hh * 512 + half * 256:2048 + hh * 512 + half * 256 + 256] = rB[c]["hgla"]
            kC = build_cd(T, True)
            hg = _gl(np.concatenate([np.asarray(ml_head_norm, f32)[0].ravel(), np.asarray(gla_head_norm, f32)[0].ravel()]))
            insC = [{"xT": xT[c], "hT": _c(hT[:, c * T:(c + 1) * T]), "ogT": rA[c]["og"], "hg": hg,
                     "w_out": np.asarray(ab_w_out, f32)[0], "g_post_mix": _gl(mix_norm_post[layer])} for c in range(NC_)]
        else:
            jobs = [("fm", 0, 4096, "qk", 0), ("fm", 4096, 4096, "qk", 4096), ("tm", 8192, 4096, "vt", 0)]
            outs = {"qk": ([8192, T], BF16), "vt": ([T, 4096], BF16)}
            w_in = np.asarray(sb_w_qkv, f32)[0]
            kA = build_proj(T, w_in.shape[1], jobs, outs)
            g = _gl(mix_norm_pre[layer])
            rA = _launch(kA, [{"xT": xT[c], "g": g, "w": w_in} for c in range(NC_)])
            qk = np.concatenate([r["qk"] for r in rA], axis=1)
            vt = np.concatenate([r["vt"] for r in rA], axis=0)
            kB = build_sb(SEQ, 4)
            rB = _launch(kB, [{"qT": _c(qk[c * 512:(c + 1) * 512]), "kT": _c(qk[4096 + c * 512:4096 + (c + 1) * 512]),
                               "v": _c(vt[:, c * 512:(c + 1) * 512])} for c in range(NC_)])
            hT = np.concatenate([r["oT"] for r in rB], axis=0)
            kC = build_cd(T, False)
            insC = [{"xT": xT[c], "hT": _c(hT[:, c * T:(c + 1) * T]), "w_out": np.asarray(sb_w_out, f32)[0],
                     "g_post_mix": _gl(mix_norm_post[layer])} for c in range(NC_)]
        insD = {"memT": memT, "g_pre": _gl(xa_norm_pre[layer]), "g_mem": _gl(mem_norm[layer]),
                "g_post": _gl(xa_norm_post[layer]), "wq": np.asarray(xa_wq, f32)[layer], "wk": np.asarray(xa_wk, f32)[layer],
                "wv": np.asarray(xa_wv, f32)[layer], "wo": np.asarray(xa_wo, f32)[layer]}
        rD = _launch(kC, [dict(insD, **insC[c]) for c in range(NC_)])
        xT = [r["x_out"] for r in rD]
        kE = build_ffn(T)
        cw = np.asarray(ffn_conv_w, f32)[layer]
        insE = {"g_pre": _gl(ffn_norm_pre[layer]), "g_post": _gl(ffn_norm_post[layer]),
                "w_gate": np.asarray(ffn_w_gate, f32)[layer], "w_up": np.asarray(ffn_w_up, f32)[layer],
                "conv_w": _c(cw.T.reshape(NFB, 128, 3).transpose(1, 0, 2)), "conv_b": _gl(np.asarray(ffn_conv_b, f32)[layer]),
                "w_down": np.asarray(ffn_w_down, f32)[layer]}
        mapsE = []
        for c in range(NC_):
            halo = xT[c - 1][:, -2:] if c > 0 else np.zeros((D, 2), f32)
            mapsE.append(dict(insE, xT=_c(np.concatenate([halo, xT[c]], axis=1)),
                              hmask=np.full((128, 1), 0.0 if c == 0 else 1.0, f32)))
        rE = _launch(kE, mapsE)
        xT = [r["x_out"] for r in rE]
    out = np.empty((1, SEQ, D), f32)
    for c in range(NC_):
        out[0, c * T:(c + 1) * T, :] = xT[c].T
    return out
```

```python
import contextlib
import numpy as np
import concourse.bass as bass
import concourse.mybir as mybir
from concourse.bass_utils import run_bass_kernel_spmd

F32 = mybir.dt.float32
BF16 = mybir.dt.bfloat16
AF = mybir.ActivationFunctionType
ALU = mybir.AluOpType
AX = mybir.AxisListType

D = 4096
KC = 32
EPS = 1e-6
NCORES = 8
TOK = 1024
ENGS = ("pe", "act", "dve", "pool", "sp")


class Sch:
    def __init__(self, nc, same_eng_sync=True):
        self.nc = nc
        self.ops = {e: [] for e in ENGS}
        self.res = {}
        self.waited = {e: {} for e in ENGS}
        self.same_eng_sync = same_eng_sync
        self.dma_cnt = {}
        self.dma_sems = {}
        self.slot_alias = {}

    def _deps_for(self, reads, writes):
        deps = []
        for r in reads:
            st = self.res.get(r)
            if st and st["w"] is not None:
                deps.append(st["w"])
        for w in writes:
            st = self.res.get(w)
            if st:
                if st["w"] is not None:
                    deps.append(st["w"])
                deps.extend(st["r"].values())
        return deps

    def _commit(self, tok, reads, writes):
        for r in reads:
            st = self.res.setdefault(r, {"w": None, "r": {}})
            key = tok[0] if tok[0] != "dma" else ("dma", tok[1])
            st["r"][key] = tok
        for w in writes:
            self.res[w] = {"w": tok, "r": {}}

    def _add_waits(self, eng, deps):
        waits = []
        wd = self.waited[eng]
        for d in deps:
            if d[0] == "dma":
                key = ("dma", d[1])
                if wd.get(key, -1) >= d[2]:
                    continue
                wd[key] = d[2]
                waits.append(d)
            else:
                e2, idx = d
                if e2 == eng and (eng == "pe" or not self.same_eng_sync):
                    continue
                if wd.get(e2, -1) >= idx:
                    continue
                wd[e2] = idx
                self.ops[e2][idx]["sig"] = True
                waits.append(d)
        return waits

    def op(self, eng, fn, reads=(), writes=()):
        deps = self._deps_for(reads, writes)
        waits = self._add_waits(eng, deps)
        idx = len(self.ops[eng])
        self.ops[eng].append({"fn": fn, "waits": waits, "sig": False, "dma": None})
        self._commit((eng, idx), reads, writes)
        return (eng, idx)

    def dma(self, q, fn, reads=(), writes=(), slot=None, inc=16):
        assert slot is not None
        if inc == 16:
            key = (q, slot)
            if key not in self.slot_alias:
                n = sum(1 for (qq, _) in self.slot_alias if qq == q)
                self.slot_alias[key] = f"{'w' if q == 'pool' else 'q'}{n}"
            slot = self.slot_alias[key]
        deps = self._deps_for(reads, writes)
        waits = self._add_waits(q, deps)
        cnt = self.dma_cnt.get(slot, 0) + inc
        self.dma_cnt[slot] = cnt
        self.ops[q].append({"fn": fn, "waits": waits, "sig": False, "dma": slot, "inc": inc})
        self._commit(("dma", slot, cnt), reads, writes)
        return ("dma", slot, cnt)

    def barrier(self):
        toks = [("dma", s_, c) for s_, c in self.dma_cnt.items()]
        for e in ENGS:
            for i in range(len(self.ops[e]) - 1, -1, -1):
                o = self.ops[e][i]
                if o["fn"] is not None and o["dma"] is None:
                    toks.append((e, i))
                    break
        for e in ENGS:
            mine = [t for t in toks if not (t[0] == e)]
            waits = self._add_waits(e, mine)
            if waits:
                self.ops[e].append({"fn": None, "waits": waits, "sig": False, "dma": None})
        self.slot_alias = {}

    def finish(self, eng="sp"):
        toks = [("dma", s_, c) for s_, c in self.dma_cnt.items()]
        waits = self._add_waits(eng, toks)
        if waits:
            self.ops[eng].append({"fn": None, "waits": waits, "sig": False, "dma": None})

    def emit(self):
        nc = self.nc
        with contextlib.ExitStack() as st:
            esem = {e: st.enter_context(nc.semaphore("s_" + e)) for e in ENGS}
            for slot in self.dma_cnt:
                self.dma_sems[slot] = st.enter_context(nc.semaphore("d_" + str(slot)))
            sigcnt = {}
            for e in ENGS:
                c = 0
                arr = []
                for o in self.ops[e]:
                    if o["sig"]:
                        c += 1
                    arr.append(c)
                sigcnt[e] = arr
            block = st.enter_context(nc.Block())

            def run(e, engobj):
                for o in self.ops[e]:
                    for d in o["waits"]:
                        if d[0] == "dma":
                            engobj.wait_ge(self.dma_sems[d[1]], d[2])
                        else:
                            engobj.wait_ge(esem[d[0]], sigcnt[d[0]][d[1]])
                    if o["fn"] is None:
                        continue
                    ins = o["fn"](engobj)
                    if o["dma"] is not None:
                        ins.then_inc(self.dma_sems[o["dma"]], o["inc"])
                    elif o["sig"]:
                        ins.then_inc(esem[e], 1)

            @block.tensor
            def _(eng):
                run("pe", eng)

            @block.scalar
            def _(eng):
                run("act", eng)

            @block.vector
            def _(eng):
                run("dve", eng)

            @block.gpsimd
            def _(eng):
                run("pool", eng)

            @block.sync
            def _(eng):
                run("sp", eng)


class Buf:
    def __init__(self, ap, name):
        self.ap = ap
        self.name = name

    def __getitem__(self, key):
        return self.ap[key]


def _dsize(dt):
    return 4 if dt == F32 else 2


class KB:
    ARENA_BYTES = 204 * 1024

    def __init__(self):
        self.nc = bass.Bass("TRN2", target_bir_lowering=False)
        self.s = Sch(self.nc)
        self.st = contextlib.ExitStack()
        self.pi = 0
        self.uid = 0
        self.arena = self.st.enter_context(self.nc.sbuf_tensor("arena", [128, self.ARENA_BYTES // 2], BF16))
        self.aoff = 0
        self.ps = [self.st.enter_context(self.nc.psum_tensor(f"ps{i}", [128, 512], F32)) for i in range(8)]
        self.held = set()
        s = self.s
        self.ones_bf = self.sb("ones_bf", [128], BF16)
        s.op("pool", lambda e: e.memset(self.ones_bf[:], 1.0), writes=["ones_bf"])
        self.eps_sb = self.sb("eps", [1], F32)
        s.op("pool", lambda e: e.memset(self.eps_sb[:], EPS), writes=["eps"])

    def din(self, name, shape, dt=F32):
        return self.nc.dram_tensor(name, list(shape), dt, kind="ExternalInput").ap()

    def dout(self, name, shape, dt=F32):
        return self.nc.dram_tensor(name, list(shape), dt, kind="ExternalOutput").ap()

    def dscr(self, name, shape, dt=F32):
        return self.nc.dram_tensor(name, list(shape), dt, kind="Internal").ap()

    def sb(self, name, fshape, dt=F32):
        n = 1
        for d in fshape:
            n *= d
        nb = n * _dsize(dt)
        self.aoff = (self.aoff + 31) // 32 * 32
        assert self.aoff + nb <= self.ARENA_BYTES, f"arena overflow at {name}: {self.aoff + nb}"
        ap = self.arena[:, self.aoff // 2:(self.aoff + nb) // 2]
        self.aoff += nb
        if dt == F32:
            ap = ap.bitcast(F32)
        if len(fshape) == 2:
            ap = ap.rearrange("p (a b) -> p a b", a=fshape[0])
        elif len(fshape) == 3:
            ap = ap.rearrange("p (a b c) -> p a b c", a=fshape[0], b=fshape[1])
        return Buf(ap, name)

    def mark(self):
        return self.aoff

    def release(self, mark):
        self.s.barrier()
        self.aoff = mark

    def bank(self):
        while True:
            b = self.pi % 8
            self.pi += 1
            if b not in self.held:
                return b

    def hold_bank(self):
        b = self.bank()
        self.held.add(b)
        return b

    def unhold(self, b):
        self.held.discard(b)

    def load_small(self, buf, src_ap):
        self.s.dma("sp", lambda e: e.dma_start(out=buf[:], in_=src_ap), writes=[buf.name], slot=buf.name)

    def close(self):
        self.s.finish()
        self.s.emit()
        self.st.close()


def _ap(o, e):
    return o(e) if callable(o) else o


def _sub(o, e, r0, r1, c0=None, c1=None):
    if callable(o):
        return o(e, r0, r1, c0, c1)
    if c0 is None:
        return o[r0:r1, :]
    return o[r0:r1, c0:c1]


def _io(k, io, name, shape, dt, kind):
    if io is not None:
        return io[name]
    return k.din(name, shape, dt) if kind == "in" else k.dout(name, shape, dt)


def run_kb(k, in_maps, trace=False):
    return run_bass_kernel_spmd(k.nc, in_maps, core_ids=list(range(len(in_maps))), trace=trace)


def tiles_of(n, w=512):
    return [(a, min(a + w, n)) for a in range(0, n, w)]


def rstd_from_psum(k, bk, dst, a, b, dim, dres):
    s = k.s
    s.op("act", lambda e: e.activation(out=dst[:, a:b], in_=k.ps[bk][:, :b - a], func=AF.Ln,
                                       scale=1.0 / dim, bias=k.eps_sb[:, 0:1]),
         reads=[f"ps{bk}", "eps"], writes=[dres])
    s.op("act", lambda e: e.activation(out=dst[:, a:b], in_=dst[:, a:b], func=AF.Exp, scale=-0.5),
         reads=[dres], writes=[dres])


def norm_load(k, x_ap, TT, gain, xg, rstd, tiles, xstage, sq, dim=D, parts=None):
    s = k.s
    nchunk = dim // 128
    banks = [k.hold_bank() for _ in tiles]
    for c in range(nchunk):
        stg = xstage[c % len(xstage)]
        if parts is None:
            s.dma("sp", lambda e, stg=stg, c=c: e.dma_start(out=stg[:, :TT], in_=_ap(x_ap, e)[c * 128:(c + 1) * 128, :]),
                  writes=[stg.name], slot=stg.name)
        else:
            for pi, (pa, pb, srcf) in enumerate(parts):
                s.dma("sp", lambda e, stg=stg, c=c, pa=pa, pb=pb, srcf=srcf: e.dma_start(out=stg[:, pa:pb], in_=srcf(c, e)),
                      writes=[stg.name] if pi == 0 else [(stg.name, pi)], slot=f"{stg.name}_{pi}")
        prd = [stg.name] + ([(stg.name, pi) for pi in range(1, len(parts))] if parts else [])
        s.op("act", lambda e, stg=stg, c=c: e.activation(out=xg[:, c, :TT], in_=stg[:, :TT], func=AF.Copy,
                                                          scale=gain[:, c:c + 1]),
             reads=prd + [gain.name], writes=[(xg.name, c)])
        sqb = sq[c % len(sq)]
        s.op("dve", lambda e, stg=stg, sqb=sqb: e.tensor_tensor(out=sqb[:, :TT], in0=stg[:, :TT], in1=stg[:, :TT],
                                                                 op=ALU.mult),
             reads=prd, writes=[sqb.name])
        for ti, (a, b) in enumerate(tiles):
            s.op("pe", lambda e, sqb=sqb, a=a, b=b, bk=banks[ti], c=c: e.matmul(
                k.ps[bk][:, :b - a], lhsT=k.ones_bf[:], rhs=sqb[:, a:b], start=(c == 0), stop=(c == nchunk - 1)),
                reads=[sqb.name, "ones_bf"], writes=[f"ps{banks[ti]}"])
    for ti, (a, b) in enumerate(tiles):
        rstd_from_psum(k, banks[ti], rstd, a, b, dim, (rstd.name, ti))
        k.unhold(banks[ti])


class WStream:
    def __init__(self, k, slots, q="pool"):
        self.k = k
        self.slots = slots
        self.jobs = []
        self.issued = 0
        self.q = q

    def add(self, fn):
        self.jobs.append(fn)
        return len(self.jobs) - 1

    def ensure(self, j):
        while self.issued <= j and self.issued < len(self.jobs):
            i = self.issued
            sl = self.slots[i % len(self.slots)]
            self.k.s.dma(self.q, self.jobs[i](sl), writes=[sl.name], slot=sl.name)
            self.issued += 1

    def get(self, j):
        self.ensure(j)
        sl = self.slots[j % len(self.slots)]
        return sl

    def prefetch(self, j):
        self.ensure(j)


def proj_fm(k, w_ap, col0, ncols, nkc, rhs_fn, rhs_reads, tiles, wslots, gw, evac):
    s = k.s
    wv = w_ap.rearrange("(kc p) n -> p kc n", p=128)
    ngroups = (ncols + gw - 1) // gw
    ws = WStream(k, wslots)
    geo = []
    for g in range(ngroups):
        c0 = col0 + g * gw
        cw = min(gw, col0 + ncols - c0)
        geo.append((c0, cw))
        ws.add(lambda sl, c0=c0, cw=cw: (lambda e: e.dma_start(out=sl[:, :nkc, :cw], in_=wv[:, :, c0:c0 + cw])))
    for g in range(ngroups):
        c0, cw = geo[g]
        wt = ws.get(g)
        ws.prefetch(g + 1)
        for cb in range((cw + 127) // 128):
            m = min(128, cw - cb * 128)
            for ti, (a, b) in enumerate(tiles):
                bk = k.bank()
                for kc in range(nkc):
                    s.op("pe", lambda e, wt=wt, kc=kc, cb=cb, m=m, a=a, b=b, bk=bk: e.matmul(
                        k.ps[bk][:m, :b - a], lhsT=wt[:, kc, cb * 128:cb * 128 + m], rhs=rhs_fn(kc, a, b),
                        start=(kc == 0), stop=(kc == nkc - 1)),
                        reads=[wt.name] + rhs_reads(kc), writes=[f"ps{bk}"])
                evac((c0 - col0) // 128 + cb, m, ti, a, b, bk)


class YSink:
    def __init__(self, k, T, y_scr, dim=D):
        self.k = k
        self.T = T
        self.y_scr = y_scr
        self.dim = dim
        self.tiles = tiles_of(T)
        self.ysb = [k.sb(f"ysb{i}", [T], F32) for i in range(2)]
        self.sq = [k.sb(f"ysq{i}", [T], BF16) for i in range(2)]
        self.ssb = [k.hold_bank() for _ in self.tiles]
        self.nblk = dim // 128

    def evac(self, cbg, m, ti, a, b, bk):
        k, s = self.k, self.k.s
        y_ = self.ysb[cbg % 2]
        q_ = self.sq[cbg % 2]
        s.op("act", lambda e: e.copy(out=y_[:, a:b], in_=k.ps[bk][:, :b - a]),
             reads=[f"ps{bk}"], writes=[(y_.name, ti)])
        s.op("dve", lambda e: e.tensor_tensor(out=q_[:, a:b], in0=k.ps[bk][:, :b - a], in1=y_[:, a:b], op=ALU.mult),
             reads=[f"ps{bk}", (y_.name, ti)], writes=[(q_.name, ti)])
        sb_ = self.ssb[ti]
        s.op("pe", lambda e: e.matmul(k.ps[sb_][:, :b - a], lhsT=k.ones_bf[:], rhs=q_[:, a:b],
                                      start=(cbg == 0), stop=(cbg == self.nblk - 1)),
             reads=[(q_.name, ti), "ones_bf"], writes=[f"ps{sb_}"])
        if ti == len(self.tiles) - 1:
            s.dma("sp", lambda e: e.dma_start(out=self.y_scr[cbg * 128:(cbg + 1) * 128, :], in_=y_[:]),
                  reads=[(y_.name, i) for i in range(len(self.tiles))], writes=[("y_scr", cbg)], slot=y_.name + "o")

    def finish(self, x_src, gpost, x_out):
        k, s, T = self.k, self.k.s, self.T
        rstd2 = k.sb("rstd2", [T], F32)
        for ti, (a, b) in enumerate(self.tiles):
            rstd_from_psum(k, self.ssb[ti], rstd2, a, b, self.dim, ("rstd2", ti))
            k.unhold(self.ssb[ti])
        r2 = [("rstd2", i) for i in range(len(self.tiles))]
        xin = [k.sb(f"xin{i}", [T], F32) for i in range(2)]
        yin = [k.sb(f"yin{i}", [T], F32) for i in range(2)]
        for c in range(self.nblk):
            xi = xin[c % 2]
            yi = yin[c % 2]
            s.dma("sp", lambda e, xi=xi, c=c: e.dma_start(out=xi[:], in_=x_src(c)), writes=[xi.name], slot=xi.name)
            s.dma("sp", lambda e, yi=yi, c=c: e.dma_start(out=yi[:], in_=self.y_scr[c * 128:(c + 1) * 128, :]),
                  reads=[("y_scr", c)], writes=[yi.name], slot=yi.name)
            s.op("dve", lambda e, yi=yi, c=c: e.scalar_tensor_tensor(
                out=yi[:], in0=yi[:], scalar=gpost[:, c:c + 1], in1=rstd2[:], op0=ALU.mult, op1=ALU.mult),
                reads=[yi.name, gpost.name] + r2, writes=[yi.name])
            s.op("dve", lambda e, yi=yi, xi=xi: e.tensor_tensor(out=xi[:], in0=xi[:], in1=yi[:], op=ALU.add),
                 reads=[yi.name, xi.name], writes=[xi.name])
            s.dma("sp", lambda e, xi=xi, c=c: e.dma_start(out=x_out[c * 128:(c + 1) * 128, :], in_=xi[:]),
                  reads=[xi.name], writes=[("x_out", c)], slot=xi.name + "o")


DFF = 11008
NFB = DFF // 128


def build_ffn(T=TOK, DFF=DFF, dbg=0, k=None, io=None):
    NFB = DFF // 128
    own = k is None
    if own:
        k = KB()
    s = k.s
    TH = T + 2
    if io is None:
        xT = k.din("xT", [D, TH])
        hmask = k.din("hmask", [128, 1])
        xparts = None
        x_own = lambda c: xT[c * 128:(c + 1) * 128, 2:TH]
        h_scr = k.dscr("h_scr", [DFF, T], BF16)
        y_scr = k.dscr("y_scr", [D, T])
    else:
        xT = None
        hmask = None
        xparts = io["xparts"]
        x_own = io["x_own"]
        h_scr = io["h_scr"]
        y_scr = io["y_scr"]
    g_pre = _io(k, io, "g_pre", [128, KC], F32, "in")
    g_post = _io(k, io, "g_post", [128, KC], F32, "in")
    w_gate = _io(k, io, "w_gate", [D, DFF], F32, "in")
    w_up = _io(k, io, "w_up", [D, DFF], F32, "in")
    conv_w = _io(k, io, "conv_w", [128, NFB, 3], F32, "in")
    conv_b = _io(k, io, "conv_b", [128, NFB], F32, "in")
    w_down = _io(k, io, "w_down", [DFF, D], F32, "in")
    x_out = _io(k, io, "x_out", [D, T], F32, "out")
    mk_start = k.mark()

    gpost = k.sb("gpost", [KC])
    k.load_small(gpost, g_post)
    mk0 = k.mark()
    gpre = k.sb("gpre", [KC])
    cw_sb = k.sb("cw", [NFB, 3])
    cb_sb = k.sb("cb", [NFB])
    hm_sb = k.sb("hm", [1])
    for buf, src in ((gpre, g_pre), (cw_sb, conv_w), (cb_sb, conv_b)):
        k.load_small(buf, src)
    if hmask is not None:
        k.load_small(hm_sb, hmask)
    else:
        s.op("pool", lambda e: e.memset(hm_sb[:], 1.0), writes=["hm"])
    rstd = k.sb("rstd", [TH])

    xg = k.sb("xg", [KC, TH], BF16)
    xstage = [k.sb(f"xst{i}", [TH], F32) for i in range(3)]
    sq = [k.sb(f"sq{i}", [TH], BF16) for i in range(2)]
    GW = 256
    wgu = [k.sb(f"wgu{i}", [2, KC, GW], BF16) for i in range(2)]
    gsb = [k.sb(f"gsb{i}", [TH], F32) for i in range(2)]
    cc = [k.sb(f"cc{i}", [T], F32) for i in range(2)]
    gl = [k.sb(f"gl{i}", [T], F32) for i in range(2)]
    hb = [k.sb(f"hb{i}", [T], BF16) for i in range(2)]

    tiles_h = tiles_of(TH)
    norm_load(k, xT, TH, gpre, xg, rstd, tiles_h, xstage, sq, parts=xparts)
    s.op("dve", lambda e: e.tensor_scalar(out=rstd[:, 0:2], in0=rstd[:, 0:2], scalar1=hm_sb[:, 0:1], scalar2=None,
                                          op0=ALU.mult),
         reads=[("rstd", 0), "hm"], writes=[("rstd", 0)])
    rstd_reads = [("rstd", i) for i in range(len(tiles_h))]
    tiles_u = [(2 + a, 2 + b) for (a, b) in tiles_of(T)]
    wgv = w_gate.rearrange("(kc p) n -> p kc n", p=128)
    wuv = w_up.rearrange("(kc p) n -> p kc n", p=128)
    NG = DFF // GW
    ws = WStream(k, wgu)
    for g in range(NG):
        ws.add(lambda sl, g=g: (lambda e: e.dma_start(out=sl[:, 0, :, :], in_=wgv[:, :, g * GW:(g + 1) * GW])))
    upres = lambda g: f"wup{g % 2}"

    def issue_up(g):
        sl = wgu[g % 2]
        s.dma("pool", lambda e, sl=sl, g=g: e.dma_start(out=sl[:, 1, :, :], in_=wuv[:, :, g * GW:(g + 1) * GW]),
              writes=[upres(g)], slot=upres(g))

    ws.ensure(0)
    issue_up(0)
    for g in range(NG):
        wt = ws.get(g)
        if g + 1 < NG:
            ws.prefetch(g + 1)
            issue_up(g + 1)
        for cb in range(GW // 128):
            fb = g * (GW // 128) + cb
            gs = gsb[fb % 2]
            for ti, (a, b) in enumerate(tiles_h):
                bk = k.bank()
                for kc in range(KC):
                    s.op("pe", lambda e, wt=wt, kc=kc, cb=cb, a=a, b=b, bk=bk: e.matmul(
                        k.ps[bk][:, :b - a], lhsT=wt[:, 0, kc, cb * 128:(cb + 1) * 128], rhs=xg[:, kc, a:b],
                        start=(kc == 0), stop=(kc == KC - 1)),
                        reads=[wt.name, ("xg", kc)], writes=[f"ps{bk}"])
                s.op("dve", lambda e, gs=gs, a=a, b=b, bk=bk: e.tensor_tensor(
                    out=gs[:, a:b], in0=k.ps[bk][:, :b - a], in1=rstd[:, a:b], op=ALU.mult),
                    reads=[f"ps{bk}", ("rstd", ti)], writes=[(gs.name, ti)])
            ubanks = []
            for ti, (a, b) in enumerate(tiles_u):
                bk = k.bank()
                ubanks.append(bk)
                for kc in range(KC):
                    s.op("pe", lambda e, wt=wt, kc=kc, cb=cb, a=a, b=b, bk=bk: e.matmul(
                        k.ps[bk][:, :b - a], lhsT=wt[:, 1, kc, cb * 128:(cb + 1) * 128], rhs=xg[:, kc, a:b],
                        start=(kc == 0), stop=(kc == KC - 1)),
                        reads=[upres(g), ("xg", kc)], writes=[f"ps{bk}"])
            c_ = cc[fb % 2]
            g_reads = [(gs.name, i) for i in range(len(tiles_h))]
            s.op("dve", lambda e, c_=c_, gs=gs, fb=fb: e.tensor_scalar(
                out=c_[:], in0=gs[:, 2:TH], scalar1=cw_sb[:, fb, 2:3], scalar2=cb_sb[:, fb:fb + 1],
                op0=ALU.mult, op1=ALU.add), reads=g_reads + ["cw", "cb"], writes=[c_.name])
            s.op("dve", lambda e, c_=c_, gs=gs, fb=fb: e.scalar_tensor_tensor(
                out=c_[:], in0=gs[:, 1:TH - 1], scalar=cw_sb[:, fb, 1:2], in1=c_[:], op0=ALU.mult, op1=ALU.add),
                reads=g_reads + ["cw", c_.name], writes=[c_.name])
            s.op("dve", lambda e, c_=c_, gs=gs, fb=fb: e.scalar_tensor_tensor(
                out=c_[:], in0=gs[:, 0:T], scalar=cw_sb[:, fb, 0:1], in1=c_[:], op0=ALU.mult, op1=ALU.add),
                reads=g_reads + ["cw", c_.name], writes=[c_.name])
            gl_ = gl[fb % 2]
            s.op("act", lambda e, c_=c_, gl_=gl_: e.activation(out=gl_[:], in_=c_[:], func=AF.Gelu_apprx_tanh),
                 reads=[c_.name], writes=[gl_.name])
            s.op("dve", lambda e, gl_=gl_: e.tensor_tensor(out=gl_[:], in0=gl_[:], in1=rstd[:, 2:TH], op=ALU.mult),
                 reads=[gl_.name] + rstd_reads, writes=[gl_.name])
            h_ = hb[fb % 2]
            for ti, (a, b) in enumerate(tiles_u):
                bk = ubanks[ti]
                s.op("dve", lambda e, h_=h_, gl_=gl_, a=a, b=b, bk=bk: e.tensor_tensor(
                    out=h_[:, a - 2:b - 2], in0=k.ps[bk][:, :b - a], in1=gl_[:, a - 2:b - 2], op=ALU.mult),
                    reads=[f"ps{bk}", gl_.name], writes=[(h_.name, ti)])
            s.dma("sp", lambda e, h_=h_, fb=fb: e.dma_start(out=h_scr[fb * 128:(fb + 1) * 128, :], in_=h_[:]),
                  reads=[(h_.name, i) for i in range(len(tiles_u))], writes=[("h_scr", fb)], slot=h_.name + "o")
    k.release(mk0)

    hT = k.sb("hT", [NFB, T], BF16)
    _q = [(NFB * i) // 4 for i in range(5)]
    SPL = [(_q[i], _q[i + 1]) for i in range(4) if _q[i + 1] > _q[i]]
    wd = [k.sb(f"wd{i}", [max(b - a for a, b in SPL), 128], BF16) for i in range(2)]
    hv = h_scr.rearrange("(fc p) t -> p fc t", p=128)
    NHL = 8
    per = (NFB + NHL - 1) // NHL
    for i in range(NHL):
        f0, f1 = i * per, min(NFB, (i + 1) * per)
        if f0 >= f1:
            continue
        s.dma("sp", lambda e, f0=f0, f1=f1: e.dma_start(out=hT[:, f0:f1, :], in_=hv[:, f0:f1, :]),
              reads=[("h_scr", f) for f in range(f0, f1)], writes=[("hT", i)], slot=f"hT{i}")
    wdv = w_down.rearrange("(fc p) n -> p fc n", p=128)
    tiles_t = tiles_of(T)
    ysink = YSink(k, T, y_scr)
    ws = WStream(k, wd)
    for db in range(KC):
        for (f0, f1) in SPL:
            ws.add(lambda sl, db=db, f0=f0, f1=f1: (lambda e: e.dma_start(
                out=sl[:, :f1 - f0, :], in_=wdv[:, f0:f1, db * 128:(db + 1) * 128])))
    for db in range(KC):
        ybanks = [k.bank() for _ in tiles_t]
        for qi, (f0, f1) in enumerate(SPL):
            j = db * len(SPL) + qi
            wt = ws.get(j)
            ws.prefetch(j + 1)
            for ti, (a, b) in enumerate(tiles_t):
                bk = ybanks[ti]
                for fc in range(f0, f1):
                    s.op("pe", lambda e, wt=wt, fl=fc - f0, fc=fc, a=a, b=b, bk=bk: e.matmul(
                        k.ps[bk][:, :b - a], lhsT=wt[:, fl, :], rhs=hT[:, fc, a:b],
                        start=(fc == 0), stop=(fc == NFB - 1)),
                        reads=[wt.name, ("hT", fc // per)], writes=[f"ps{bk}"])
        for ti, (a, b) in enumerate(tiles_t):
            ysink.evac(db, 128, ti, a, b, ybanks[ti])
    mk1 = k.mark()
    k.release(mk0)
    ysink.finish(x_own, gpost, x_out)
    if own:
        k.close()
    else:
        k.release(mk_start)
    return k


def make_ident(k, name="ident"):
    idt = k.sb(name, [128], F32)
    k.s.op("pool", lambda e: e.memset(idt[:], 1.0), writes=[name])
    k.s.op("pool", lambda e: e.affine_select(out=idt[:], in_=idt[:], pattern=[[-1, 128]], compare_op=ALU.is_equal,
                                             fill=0.0, base=0, channel_multiplier=1), reads=[name], writes=[name])
    return idt


def build_proj(T, N, jobs, outs, k=None, io=None):
    own = k is None
    if own:
        k = KB()
    s = k.s
    xT = _io(k, io, "xT", [D, T], F32, "in")
    g = _io(k, io, "g", [128, KC], F32, "in")
    w = _io(k, io, "w", [D, N], F32, "in")
    od = {name: _io(k, io, name, shape, dt, "out") for name, (shape, dt) in outs.items()}
    mk_start = k.mark()
    gsb = k.sb("g", [KC])
    k.load_small(gsb, g)
    rstd = k.sb("rstd", [T])
    xg = k.sb("xg", [KC, T], BF16)
    xstage = [k.sb(f"xst{i}", [T], F32) for i in range(3)]
    sq = [k.sb(f"sq{i}", [T], BF16) for i in range(2)]
    tiles = tiles_of(T)
    norm_load(k, xT, T, gsb, xg, rstd, tiles, xstage, sq)
    rres = [("rstd", i) for i in range(len(tiles))]
    NTB = T // 128
    rcol = k.sb("rcol", [NTB], F32)
    if any(j[0] == "tm" for j in jobs):
        idt = make_ident(k)
        bk = k.bank()
        for tb in range(NTB):
            s.op("pe", lambda e, tb=tb, bk=bk: e.matmul(k.ps[bk][:, tb:tb + 1], lhsT=rstd[:, tb * 128:(tb + 1) * 128],
                                                        rhs=idt[:, 0:1], start=True, stop=True),
                 reads=rres + ["ident"], writes=[f"ps{bk}"])
        s.op("dve", lambda e, bk=bk: e.tensor_copy(out=rcol[:], in_=k.ps[bk][:, 0:NTB]), reads=[f"ps{bk}"],
             writes=["rcol"])
    GW = 512
    wsl = [k.sb(f"w{i}", [KC, GW], BF16) for i in range(2)]
    st16 = [k.sb(f"st16_{i}", [T], BF16) for i in range(2)]
    st32 = [k.sb(f"st32_{i}", [T], F32) for i in range(2)]
    sttm = [k.sb(f"sttm{i}", [GW], BF16) for i in range(3)]
    cnt = {"fm": 0, "tm": 0}
    wv = w.rearrange("(kc p) n -> p kc n", p=128)
    for (mode, col0, ncols, oname, off) in jobs:
        o = od[oname]
        odt = outs[oname][1]
        if mode == "fm":
            pool_ = st16 if odt == BF16 else st32

            def evac(cbg, m, ti, a, b, bk, pool_=pool_, o=o, off=off):
                if ti == 0:
                    cnt["fm"] += 1
                stg = pool_[cnt["fm"] % 2]
                s.op("dve", lambda e: e.tensor_tensor(out=stg[:m, a:b], in0=k.ps[bk][:m, :b - a], in1=rstd[:m, a:b],
                                                      op=ALU.mult),
                     reads=[f"ps{bk}", ("rstd", ti)], writes=[(stg.name, ti)])
                if ti == len(tiles) - 1:
                    s.dma("sp", lambda e: e.dma_start(out=_sub(o, e, off + cbg * 128, off + cbg * 128 + m), in_=stg[:m, :]),
                          reads=[(stg.name, i) for i in range(len(tiles))], writes=[(oname, off, cbg)],
                          slot=stg.name + "o")

            proj_fm(k, w, col0, ncols, KC, lambda kc, a, b: xg[:, kc, a:b], lambda kc: [("xg", kc)], tiles, wsl, GW, evac)
        else:
            ngroups = (ncols + GW - 1) // GW
            ws = WStream(k, wsl)
            geo = []
            for gi in range(ngroups):
                c0 = col0 + gi * GW
                cw = min(GW, col0 + ncols - c0)
                geo.append((c0, cw))
                ws.add(lambda sl, c0=c0, cw=cw: (lambda e: e.dma_start(out=sl[:, :, :cw], in_=wv[:, :, c0:c0 + cw])))
            for gi in range(ngroups):
                c0, cw = geo[gi]
                wt = ws.get(gi)
                ws.prefetch(gi + 1)
                for tb in range(NTB):
                    bk = k.bank()
                    for kc in range(KC):
                        s.op("pe", lambda e, wt=wt, kc=kc, tb=tb, cw=cw, bk=bk: e.matmul(
                            k.ps[bk][:, :cw], lhsT=xg[:, kc, tb * 128:(tb + 1) * 128], rhs=wt[:, kc, :cw],
                            start=(kc == 0), stop=(kc == KC - 1)),
                            reads=[wt.name, ("xg", kc)], writes=[f"ps{bk}"])
                    cnt["tm"] += 1
                    stg = sttm[cnt["tm"] % 3]
                    s.op("act", lambda e, stg=stg, tb=tb, cw=cw, bk=bk: e.activation(
                        out=stg[:, :cw], in_=k.ps[bk][:, :cw], func=AF.Copy, scale=rcol[:, tb:tb + 1]),
                        reads=[f"ps{bk}", "rcol"], writes=[stg.name])
                    s.dma("sp", lambda e, stg=stg, tb=tb, cw=cw, c0=c0, o=o, off=off, col0=col0: e.dma_start(
                        out=_sub(o, e, tb * 128, (tb + 1) * 128, off + c0 - col0, off + c0 - col0 + cw), in_=stg[:, :cw]),
                        reads=[stg.name], writes=[(oname, tb, c0)], slot=stg.name + "o")
    if own:
        k.close()
    else:
        k.release(mk_start)
    return k


def build_outproj(T, layer0, k=None, io=None):
    own = k is None
    if own:
        k = KB()
    s = k.s
    xT = _io(k, io, "xT", [D, T], F32, "in")
    hT = _io(k, io, "hT", [D, T], F32 if layer0 else BF16, "in")
    w_out = _io(k, io, "w_out", [D, D], F32, "in")
    g_post = _io(k, io, "g_post", [128, KC], F32, "in")
    x_out = _io(k, io, "x_out", [D, T], F32, "out")
    y_scr = io["y_scr"] if io else k.dscr("y_scr", [D, T])
    mk_start = k.mark()
    gpost = k.sb("gpost", [KC])
    k.load_small(gpost, g_post)
    hn = k.sb("hn", [KC, T], BF16)
    tiles = tiles_of(T)
    mk0 = k.mark()
    if layer0:
        ogT = _io(k, io, "ogT", [D, T], F32, "in")
        hg = _io(k, io, "hg", [128, KC], F32, "in")
        hgs = k.sb("hg", [KC])
        k.load_small(hgs, hg)
        hst = [k.sb(f"hst{i}", [4, T], F32) for i in range(2)]
        ogs = [k.sb(f"ogs{i}", [T], F32) for i in range(2)]
        sqh = [k.sb(f"sqh{i}", [T], BF16) for i in range(2)]
        rh = [k.sb(f"rh{i}", [T], F32) for i in range(2)]
        for hd in range(8):
            hb = hst[hd % 2]
            banks = [k.hold_bank() for _ in tiles]
            for j in range(4):
                c = hd * 4 + j
                s.dma("sp", lambda e, hb=hb, j=j, c=c: e.dma_start(out=hb[:, j, :], in_=_sub(hT, e, c * 128, (c + 1) * 128)),
                      writes=[(hb.name, j)], slot=f"{hb.name}_{j}")
                q_ = sqh[c % 2]
                s.op("dve", lambda e, hb=hb, j=j, q_=q_: e.tensor_tensor(out=q_[:], in0=hb[:, j, :], in1=hb[:, j, :],
                                                                         op=ALU.mult),
                     reads=[(hb.name, j)], writes=[q_.name])
                for ti, (a, b) in enumerate(tiles):
                    s.op("pe", lambda e, q_=q_, a=a, b=b, bk=banks[ti], j=j: e.matmul(
                        k.ps[bk][:, :b - a], lhsT=k.ones_bf[:], rhs=q_[:, a:b], start=(j == 0), stop=(j == 3)),
                        reads=[q_.name, "ones_bf"], writes=[f"ps{banks[ti]}"])
            r_ = rh[hd % 2]
            for ti, (a, b) in enumerate(tiles):
                rstd_from_psum(k, banks[ti], r_, a, b, 512, (r_.name, ti))
                k.unhold(banks[ti])
            rr = [(r_.name, i) for i in range(len(tiles))]
            for j in range(4):
                c = hd * 4 + j
                og = ogs[c % 2]
                s.dma("sp", lambda e, og=og, c=c: e.dma_start(out=og[:], in_=ogT[c * 128:(c + 1) * 128, :]),
                      writes=[og.name], slot=og.name)
                s.op("act", lambda e, og=og, hd=hd: e.activation(out=og[:], in_=og[:],
                                                                 func=(AF.Sigmoid if hd < 4 else AF.Silu)),
                     reads=[og.name], writes=[og.name])
                s.op("dve", lambda e, hb=hb, j=j, c=c, r_=r_: e.scalar_tensor_tensor(
                    out=hb[:, j, :], in0=hb[:, j, :], scalar=hgs[:, c:c + 1], in1=r_[:], op0=ALU.mult, op1=ALU.mult),
                    reads=[(hb.name, j), "hg"] + rr, writes=[(hb.name, j)])
                s.op("dve", lambda e, hb=hb, j=j, c=c, og=og: e.tensor_tensor(out=hn[:, c, :], in0=hb[:, j, :], in1=og[:],
                                                                              op=ALU.mult),
                     reads=[(hb.name, j), og.name], writes=[("hn", c)])
        k.release(mk0)
    else:
        for i in range(4):
            s.dma("sp", lambda e, i=i: e.dma_start(
                out=hn[:, i * 8:(i + 1) * 8, :],
                in_=_sub(hT, e, i * 1024, (i + 1) * 1024).rearrange("(c p) t -> p c t", p=128)),
                  writes=[("hn", c) for c in range(i * 8, i * 8 + 8)], slot=f"hn{i}")
    GW = 512
    wsl = [k.sb(f"w{i}", [KC, GW], BF16) for i in range(2)]
    ysink = YSink(k, T, y_scr)
    proj_fm(k, w_out, 0, D, KC, lambda kc, a, b: hn[:, kc, a:b], lambda kc: [("hn", kc)], tiles, wsl, GW, ysink.evac)
    k.release(mk0)
    ysink.finish(lambda c: xT[c * 128:(c + 1) * 128, :], gpost, x_out)
    if own:
        k.close()
    else:
        k.release(mk_start)
    return k


XA_W = 1024
NMEM = 256


def build_xattn(T, k=None, io=None):
    own = k is None
    if own:
        k = KB()
    s = k.s
    xT = _io(k, io, "xT", [D, T], F32, "in")
    memT = _io(k, io, "memT", [D, NMEM], F32, "in")
    g_pre = _io(k, io, "g_pre", [128, KC], F32, "in")
    g_mem = _io(k, io, "g_mem", [128, KC], F32, "in")
    g_post = _io(k, io, "g_post", [128, KC], F32, "in")
    wq = _io(k, io, "wq", [D, XA_W], F32, "in")
    wk = _io(k, io, "wk", [D, XA_W], F32, "in")
    wv = _io(k, io, "wv", [D, XA_W], F32, "in")
    wo = _io(k, io, "wo", [XA_W, D], F32, "in")
    x_out = _io(k, io, "x_out", [D, T], F32, "out")
    y_scr = io["y_scr"] if io else k.dscr("y_scr", [D, T])
    mk_start = k.mark()
    gpre = k.sb("gpre", [KC]); gmem = k.sb("gmem", [KC]); gpost = k.sb("gpost", [KC])
    for b_, src in ((gpre, g_pre), (gmem, g_mem), (gpost, g_post)):
        k.load_small(b_, src)
    NQC = XA_W // 128
    KT = k.sb("KT", [NQC, NMEM], BF16)
    VT = k.sb("VT", [NQC, NMEM], BF16)
    V = k.sb("V", [2, XA_W], BF16)
    qT = k.sb("qT", [NQC, T], BF16)
    rstd = k.sb("rstd", [T])
    rstdm = k.sb("rstdm", [NMEM])
    idb = k.sb("identb", [128], BF16)
    idt = make_ident(k)
    s.op("dve", lambda e: e.tensor_copy(out=idb[:], in_=idt[:]), reads=["ident"], writes=["identb"])
    tiles = tiles_of(T)
    tm_ = tiles_of(NMEM)
    mk0 = k.mark()
    xg = k.sb("xg", [KC, T], BF16)
    mg = k.sb("mg", [KC, NMEM], BF16)
    xstage = [k.sb(f"xst{i}", [T], F32) for i in range(3)]
    sq = [k.sb(f"sq{i}", [T], BF16) for i in range(2)]
    GW = 256
    wsl = [k.sb(f"w{i}", [KC, GW], BF16) for i in range(2)]
    norm_load(k, memT, NMEM, gmem, mg, rstdm, tm_, xstage, sq)
    norm_load(k, xT, T, gpre, xg, rstd, tiles, xstage, sq)

    def evac_to(dst, rs, rname):
        def f(cbg, m, ti, a, b, bk):
            s.op("dve", lambda e: e.tensor_tensor(out=dst[:, cbg, a:b], in0=k.ps[bk][:, :b - a], in1=rs[:, a:b],
                                                  op=ALU.mult),
                 reads=[f"ps{bk}", (rname, ti)], writes=[(dst.name, cbg)])
        return f

    proj_fm(k, wk, 0, XA_W, KC, lambda kc, a, b: mg[:, kc, a:b], lambda kc: [("mg", kc)], tm_, wsl, GW,
            evac_to(KT, rstdm, "rstdm"))
    proj_fm(k, wv, 0, XA_W, KC, lambda kc, a, b: mg[:, kc, a:b], lambda kc: [("mg", kc)], tm_, wsl, GW,
            evac_to(VT, rstdm, "rstdm"))
    proj_fm(k, wq, 0, XA_W, KC, lambda kc, a, b: xg[:, kc, a:b], lambda kc: [("xg", kc)], tiles, wsl, GW,
            evac_to(qT, rstd, "rstd"))
    psb = k.sb("pstmp", [128], BF16)
    for c in range(NQC):
        for mb in range(2):
            bk = k.bank()
            pst = k.ps[bk][:].bitcast(BF16)
            s.op("pe", lambda e, c=c, mb=mb, pst=pst: e.transpose(out=pst[:, 0:128], in_=VT[:, c, mb * 128:(mb + 1) * 128],
                                                                  identity=idb[:]),
                 reads=[("VT", c), "identb"], writes=[f"ps{bk}"])
            s.op("act", lambda e, c=c, mb=mb, pst=pst: e.copy(out=V[:, mb, c * 128:(c + 1) * 128], in_=pst[:, 0:128]),
                 reads=[f"ps{bk}"], writes=[("V", mb, c)])
    k.release(mk0)
    oT = k.sb("oT", [NQC, T], BF16)
    pT = [k.sb(f"pT{i}", [2, T], BF16) for i in range(2)]
    ee = [k.sb(f"ee{i}", [NMEM], F32) for i in range(2)]
    pp = [k.sb(f"pp{i}", [NMEM], BF16) for i in range(2)]
    mx = [k.sb(f"mx{i}", [1], F32) for i in range(2)]
    sm = [k.sb(f"sm{i}", [1], F32) for i in range(2)]
    SC = 1.0 / 16.0
    it = 0
    for h in range(4):
        pT_ = pT[h % 2]
        for tb in range(T // 128):
            i2 = it % 2
            it += 1
            bk = k.bank()
            for j in range(2):
                s.op("pe", lambda e, h=h, j=j, tb=tb, bk=bk: e.matmul(
                    k.ps[bk][:, :NMEM], lhsT=qT[:, 2 * h + j, tb * 128:(tb + 1) * 128], rhs=KT[:, 2 * h + j, :],
                    start=(j == 0), stop=(j == 1)),
                    reads=[("qT", 2 * h + j), ("KT", 2 * h + j)], writes=[f"ps{bk}"])
            m_, s_, e_, p_ = mx[i2], sm[i2], ee[i2], pp[i2]
            s.op("dve", lambda e, m_=m_, bk=bk: e.reduce_max(out=m_[:], in_=k.ps[bk][:, :NMEM], axis=AX.X),
                 reads=[f"ps{bk}"], writes=[m_.name])
            s.op("dve", lambda e, m_=m_: e.tensor_scalar(out=m_[:], in0=m_[:], scalar1=-SC, scalar2=None, op0=ALU.mult),
                 reads=[m_.name], writes=[m_.name])
            s.op("act", lambda e, m_=m_, s_=s_, e_=e_, bk=bk: e.activation(
                out=e_[:], in_=k.ps[bk][:, :NMEM], func=AF.Exp, scale=SC, bias=m_[:, 0:1], accum_out=s_[:, 0:1]),
                reads=[f"ps{bk}", m_.name], writes=[e_.name, s_.name])
            s.op("dve", lambda e, s_=s_: e.reciprocal(out=s_[:], in_=s_[:]), reads=[s_.name], writes=[s_.name])
            s.op("dve", lambda e, e_=e_, p_=p_, s_=s_: e.tensor_scalar(out=p_[:], in0=e_[:], scalar1=s_[:, 0:1],
                                                                       scalar2=None, op0=ALU.mult),
                 reads=[e_.name, s_.name], writes=[p_.name])
            for mb in range(2):
                bk2 = k.bank()
                pst = k.ps[bk2][:].bitcast(BF16)
                s.op("pe", lambda e, p_=p_, mb=mb, pst=pst: e.transpose(out=pst[:, 0:128], in_=p_[:, mb * 128:(mb + 1) * 128],
                                                                        identity=idb[:]),
                     reads=[p_.name, "identb"], writes=[f"ps{bk2}"])
                s.op("act", lambda e, pT_=pT_, mb=mb, tb=tb, pst=pst: e.copy(
                    out=pT_[:, mb, tb * 128:(tb + 1) * 128], in_=pst[:, 0:128]),
                    reads=[f"ps{bk2}"], writes=[(pT_.name, mb, tb)])
        for j in range(2):
            for ti, (a, b) in enumerate(tiles):
                bk = k.bank()
                for mb in range(2):
                    s.op("pe", lambda e, h=h, j=j, a=a, b=b, mb=mb, bk=bk, pT_=pT_: e.matmul(
                        k.ps[bk][:, :b - a], lhsT=V[:, mb, (2 * h + j) * 128:(2 * h + j + 1) * 128], rhs=pT_[:, mb, a:b],
                        start=(mb == 0), stop=(mb == 1)),
                        reads=[("V", mb, 2 * h + j)] + [(pT_.name, mb, tb) for tb in range(a // 128, b // 128)],
                        writes=[f"ps{bk}"])
                s.op("act", lambda e, h=h, j=j, a=a, b=b, bk=bk: e.copy(out=oT[:, 2 * h + j, a:b], in_=k.ps[bk][:, :b - a]),
                     reads=[f"ps{bk}"], writes=[("oT", 2 * h + j)])
    mk1 = k.mark()
    wso = [k.sb(f"wo{i}", [NQC, 512], BF16) for i in range(2)]
    ysink = YSink(k, T, y_scr)
    proj_fm(k, wo, 0, D, NQC, lambda kc, a, b: oT[:, kc, a:b], lambda kc: [("oT", kc)], tiles, wso, 512, ysink.evac)
    k.release(mk1)
    ysink.finish(lambda c: xT[c * 128:(c + 1) * 128, :], gpost, x_out)
    if own:
        k.close()
    else:
        k.release(mk_start)
    return k


def build_sb(S, NH, k=None, io=None):
    own = k is None
    if own:
        k = KB()
    s = k.s
    DH = 128
    if io is None:
        qT = k.din("qT", [NH * DH, S], BF16)
        kT = k.din("kT", [NH * DH, S], BF16)
        v = k.din("v", [S, NH * DH], BF16)
        oT = k.dout("oT", [NH * DH, S], BF16)
        vv = v.rearrange("(b p) c -> p b c", p=128)
        q_src = lambda e, h, r: qT[h * DH:(h + 1) * DH, r * (S // 8):(r + 1) * (S // 8)]
        k_src = lambda e, h, r: kT[h * DH:(h + 1) * DH, r * (S // 8):(r + 1) * (S // 8)]
        v_src = lambda e, h: vv[:, :, h * DH:(h + 1) * DH]
        o_dst = lambda e, h, QT: oT[h * DH:(h + 1) * DH, QT * 512:(QT + 1) * 512]
    else:
        q_src, k_src, v_src, o_dst = io["q_src"], io["k_src"], io["v_src"], io["o_dst"]
    mk_start = k.mark()
    NB = S // 128
    NQT = S // 512
    SCALE = DH ** -0.5
    U = k.sb("U", [128], BF16)
    negones = k.sb("negones", [128], BF16)
    utmp = k.sb("utmp", [128], F32)
    s.op("pool", lambda e: e.memset(utmp[:], -1.0), writes=["utmp"])
    s.op("pool", lambda e: e.affine_select(out=utmp[:], in_=utmp[:], pattern=[[-1, 128]], compare_op=ALU.is_ge,
                                           fill=0.0, base=0, channel_multiplier=1), reads=["utmp"], writes=["utmp"])
    s.op("dve", lambda e: e.tensor_copy(out=U[:], in_=utmp[:]), reads=["utmp"], writes=["U"])
    s.op("pool", lambda e: e.memset(negones[:], -1.0), writes=["negones"])
    qs = [k.sb(f"qs{i}", [S], BF16) for i in range(2)]
    ks = [k.sb(f"ks{i}", [S], BF16) for i in range(2)]
    vs = [k.sb(f"vs{i}", [NB, DH], BF16) for i in range(2)]
    e1 = [k.sb(f"e1_{i}", [512], F32) for i in range(2)]
    sp = [k.sb(f"sp{i}", [512], BF16) for i in range(2)]
    arg = [k.sb(f"arg{i}", [512], F32) for i in range(2)]
    att = [k.sb(f"att{i}", [512], BF16) for i in range(2)]
    carry = [k.sb(f"carry{i}", [512], F32) for i in range(2)]
    ost = [k.sb(f"ost{i}", [512], BF16) for i in range(2)]
    step = 0
    qti = 0
    for h in range(NH):
        q_, k_, v_ = qs[h % 2], ks[h % 2], vs[h % 2]
        S8 = S // 8
        for r in range(8):
            s.dma("sp", lambda e, q_=q_, h=h, r=r: e.dma_start(out=q_[:, r * S8:(r + 1) * S8], in_=q_src(e, h, r)),
                  writes=[q_.name] if r == 0 else [(q_.name, r)], slot=f"{q_.name}_{r}")
        s.op("act", lambda e, q_=q_: e.activation(out=q_[:], in_=q_[:], func=AF.Copy, scale=SCALE),
             reads=[q_.name] + [(q_.name, r) for r in range(1, 8)], writes=[q_.name])
        for r in range(8):
            s.dma("sp", lambda e, k_=k_, h=h, r=r: e.dma_start(out=k_[:, r * S8:(r + 1) * S8], in_=k_src(e, h, r)),
                  writes=[k_.name] if r == 0 else [(k_.name, r)], slot=f"{k_.name}_{r}")
        s.dma("sp", lambda e, v_=v_, h=h: e.dma_start(out=v_[:], in_=v_src(e, h)), writes=[v_.name], slot=v_.name)
        kres = [k_.name] + [(k_.name, r) for r in range(1, 8)]
        plan = []
        for QT in range(NQT):
            nkb = 4 * QT + 4
            for idx, kb in enumerate(range(nkb - 1, -1, -1)):
                plan.append({"QT": QT, "idx": idx, "kb": kb, "nkb": nkb})

        def stage0(st, q_=q_, k_=k_, kres=kres):
            nonlocal step, qti
            QT, idx, kb = st["QT"], st["idx"], st["kb"]
            if idx == 0:
                st["cr"] = carry[qti % 2]
                st["o_"] = ost[qti % 2]
                qti += 1
                cr = st["cr"]
                s.op("pool", lambda e: e.memset(cr[:], 0.0), writes=[cr.name])
                st["bO"] = k.hold_bank()
                qt_state[QT] = st
            st["cr"], st["o_"], st["bO"] = qt_state[QT]["cr"], qt_state[QT]["o_"], qt_state[QT]["bO"]
            i = kb - 4 * QT
            qa = 128 * i if i >= 0 else 0
            N = 512 - qa
            q0 = QT * 512 + qa
            st_ = step % 2
            step += 1
            e_, sp_ = e1[st_], sp[st_]
            st["ar_"], st["at_"] = arg[st_], att[st_]
            bA = k.bank()
            st.update(i=i, qa=qa, N=N, q0=q0, bA=bA, e_=e_, sp_=sp_)
            s.op("pe", lambda e: e.matmul(k.ps[bA][:, :N], lhsT=k_[:, kb * 128:(kb + 1) * 128], rhs=q_[:, q0:q0 + N],
                                          start=True, stop=True), reads=kres + [q_.name], writes=[f"ps{bA}"])

        def stage1(st, q_=q_, k_=k_, kres=kres):
            kb = st["kb"]
            i, qa, N, q0, bA, e_, sp_ = st["i"], st["qa"], st["N"], st["q0"], st["bA"], st["e_"], st["sp_"]
            bB, bT = k.bank(), k.bank()
            st.update(bB=bB, bT=bT)
            s.op("act", lambda e: e.activation(out=e_[:, :N], in_=k.ps[bA][:, :N], func=AF.Exp),
                 reads=[f"ps{bA}"], writes=[e_.name])
            s.op("act", lambda e: e.activation(out=sp_[:, :N], in_=e_[:, :N], func=AF.Ln, bias=1.0),
                 reads=[e_.name], writes=[sp_.name])
            if i >= 0:
                s.op("pool", lambda e: e.affine_select(out=sp_[:, 0:128], in_=sp_[:, 0:128], pattern=[[1, 128]],
                                                       compare_op=ALU.is_gt, fill=0.0, base=0, channel_multiplier=-1),
                     reads=[sp_.name], writes=[sp_.name])
            s.op("pe", lambda e: e.matmul(k.ps[bB][:, :N], lhsT=k_[:, kb * 128:(kb + 1) * 128], rhs=q_[:, q0:q0 + N],
                                          start=True, stop=False), reads=kres + [q_.name], writes=[f"ps{bB}"])
            s.op("pe", lambda e: e.matmul(k.ps[bB][:, :N], lhsT=U[:], rhs=sp_[:, :N], start=False, stop=True),
                 reads=[sp_.name, "U"], writes=[f"ps{bB}"])
            s.op("pe", lambda e: e.matmul(k.ps[bT][:, :N], lhsT=negones[:], rhs=sp_[:, :N], start=True, stop=True),
                 reads=[sp_.name, "negones"], writes=[f"ps{bT}"])

        def stage2(st, v_=v_, h=h):
            QT, idx, kb, nkb = st["QT"], st["idx"], st["kb"], st["nkb"]
            cr, o_, bO, ar_, at_ = st["cr"], st["o_"], st["bO"], st["ar_"], st["at_"]
            i, qa, N, bB, bT = st["i"], st["qa"], st["N"], st["bB"], st["bT"]
            s.op("dve", lambda e: e.tensor_tensor(out=ar_[:, :N], in0=k.ps[bB][:, :N], in1=cr[:, qa:512], op=ALU.add),
                 reads=[f"ps{bB}", cr.name], writes=[ar_.name])
            s.op("act", lambda e: e.activation(out=at_[:, :N], in_=ar_[:, :N], func=AF.Exp),
                 reads=[ar_.name], writes=[at_.name])
            if i >= 0:
                s.op("pool", lambda e: e.affine_select(out=at_[:, 0:128], in_=at_[:, 0:128], pattern=[[1, 128]],
                                                       compare_op=ALU.is_gt, fill=0.0, base=0, channel_multiplier=-1),
                     reads=[at_.name], writes=[at_.name])
            s.op("pe", lambda e: e.matmul(k.ps[bO][:, qa:512], lhsT=v_[:, kb, :], rhs=at_[:, :N], start=(idx == 0),
                                          stop=(idx == nkb - 1), skip_group_check=True),
                 reads=[v_.name, at_.name], writes=[f"ps{bO}"])
            s.op("dve", lambda e: e.tensor_tensor(out=cr[:, qa:512], in0=k.ps[bT][:, :N], in1=cr[:, qa:512], op=ALU.add),
                 reads=[f"ps{bT}", cr.name], writes=[cr.name])
            if idx == nkb - 1:
                s.op("act", lambda e: e.copy(out=o_[:], in_=k.ps[bO][:, :]), reads=[f"ps{bO}"], writes=[o_.name])
                k.unhold(bO)
                s.dma("sp", lambda e: e.dma_start(out=o_dst(e, h, QT), in_=o_[:]),
                      reads=[o_.name], writes=[("oT", h, QT)], slot=o_.name + "o")

        qt_state = {}
        NP = len(plan)
        stage0(plan[0])
        if NP > 1:
            stage0(plan[1])
        stage1(plan[0])
        for pi_ in range(NP):
            if pi_ + 2 < NP:
                stage0(plan[pi_ + 2])
            if pi_ + 1 < NP:
                stage1(plan[pi_ + 1])
            stage2(plan[pi_])
    if own:
        k.close()
    else:
        k.release(mk_start)
    return k


LN16 = -2.772588722239781
PAD = 64


def build_ab(S, k=None, io=None):
    own = k is None
    if own:
        k = KB()
    s = k.s
    NSEG = S // 512
    if io is None:
        mqT = k.din("mqT", [256, S], BF16)
        mkT = k.din("mkT", [256, S], BF16)
        mkt = k.din("mkt", [S, 256], BF16)
        mv = k.din("mv", [S, 256], BF16)
        mgate = k.din("mgate", [2, S])
        mbias = k.din("mbias", [1, 2])
        gqT = k.din("gqT", [256, S], BF16)
        gkT = k.din("gkT", [256, S], BF16)
        gv = k.din("gv", [S, 256], BF16)
        grT = k.din("grT", [16, S])
        W2 = k.din("W2", [16, 256])
        b2 = k.din("b2", [128, 2])
        hml = k.dout("hml", [256, S])
        hgla = k.dout("hgla", [256, S])
        scr = k.dscr("gscr", [3, S + 2 * PAD])
        fm3 = lambda t: t.rearrange("(j p) t -> p j t", p=128)
        tk3 = lambda t, t0: t[t0:t0 + 512, :].rearrange("(c s) d -> s c d", s=64)
        acc = {
            "mq": lambda e, t0: fm3(mqT)[:, :, t0:t0 + 512], "mk": lambda e, t0: fm3(mkT)[:, :, t0:t0 + 512],
            "gq": lambda e, t0: fm3(gqT)[:, :, t0:t0 + 512], "gk": lambda e, t0: fm3(gkT)[:, :, t0:t0 + 512],
            "mkt": lambda e, t0: tk3(mkt, t0), "mv": lambda e, t0: tk3(mv, t0), "gv": lambda e, t0: tk3(gv, t0),
            "gr": lambda e, t0: grT[:, t0:t0 + 512],
            "gate": lambda e, w, r: mgate[w:w + 1, r * (S // 8):(r + 1) * (S // 8)],
            "bias": lambda e, w: mbias[0:1, w:w + 1],
            "W2": lambda e: W2, "b2": lambda e: b2,
            "hml": lambda e, t0: fm3(hml)[:, :, t0:t0 + 512], "hgla": lambda e, t0: fm3(hgla)[:, :, t0:t0 + 512],
        }
    else:
        acc = io["acc"]
        scr = io["scr"]
    mk_start = k.mark()

    idt = make_ident(k)
    idb = k.sb("identb", [128], BF16)
    s.op("dve", lambda e: e.tensor_copy(out=idb[:], in_=idt[:]), reads=["ident"], writes=["identb"])
    tri = k.sb("tri", [64], F32)
    nm = k.sb("nm", [64], F32)
    s.op("pool", lambda e: e.memset(tri[:], 1.0), writes=["tri"])
    s.op("pool", lambda e: e.affine_select(out=tri[:64, :], in_=tri[:64, :], pattern=[[1, 64]], compare_op=ALU.is_ge,
                                           fill=0.0, base=0, channel_multiplier=-1), reads=["tri"], writes=["tri"])
    s.op("pool", lambda e: e.memset(nm[:], LN16), writes=["nm"])
    s.op("pool", lambda e: e.affine_select(out=nm[:64, :], in_=nm[:64, :], pattern=[[1, 64]], compare_op=ALU.is_ge,
                                           fill=-1e30, base=0, channel_multiplier=-1), reads=["nm"], writes=["nm"])

    mk0 = k.mark()
    gi = k.sb("gi", [S], F32)
    gf = k.sb("gf", [S], F32)
    ones_r = k.sb("ones_r", [S], F32)
    bn = k.sb("bn", [S], F32)
    arow = Buf(gi.ap, "gi")
    mrow = Buf(gf.ap, "gf")
    mtrow = Buf(bn.ap, "bn")
    bsb = k.sb("bsb", [2], F32)
    zpad = k.sb("zpad", [PAD], F32)
    S8 = S // 8
    for r in range(8):
        s.dma("sp", lambda e, r=r: e.dma_start(out=gi[0:1, r * S8:(r + 1) * S8], in_=acc["gate"](e, 0, r)),
              writes=[("gi", r)], slot=f"gi{r}")
        s.dma("sp", lambda e, r=r: e.dma_start(out=gf[0:1, r * S8:(r + 1) * S8], in_=acc["gate"](e, 1, r)),
              writes=[("gf", r)], slot=f"gf{r}")
    s.dma("sp", lambda e: e.dma_start(out=bsb[0:1, 0:1], in_=acc["bias"](e, 0)), writes=["bsb"], slot="bsb")
    s.dma("sp", lambda e: e.dma_start(out=bsb[0:1, 1:2], in_=acc["bias"](e, 1)), writes=[("bsb", 1)], slot="bsb1")
    s.op("dve", lambda e: e.tensor_scalar(out=bsb[0:1, :], in0=bsb[0:1, :], scalar1=1.0 / 15.0, scalar2=None, op0=ALU.mult),
         reads=["bsb", ("bsb", 1)], writes=["bsb"])
    s.op("pool", lambda e: e.memset(ones_r[0:1, :], 1.0), writes=["ones_r"])
    s.op("pool", lambda e: e.memset(zpad[0:1, :], 0.0), writes=["zpad"])
    s.op("act", lambda e: e.activation(out=gi[0:1, :], in_=gi[0:1, :], func=AF.Tanh, scale=1.0 / 15.0, bias=bsb[0:1, 0:1]),
         reads=[("gi", r) for r in range(8)] + ["bsb"], writes=["gi"])
    s.op("act", lambda e: e.activation(out=gf[0:1, :], in_=gf[0:1, :], func=AF.Tanh, scale=1.0 / 15.0, bias=bsb[0:1, 1:2]),
         reads=[("gf", r) for r in range(8)] + ["bsb"], writes=["gf"])
    s.op("act", lambda e: e.activation(out=gf[0:1, :], in_=gf[0:1, :], func=AF.Exp, scale=-15.0), reads=["gf"], writes=["gf"])
    s.op("act", lambda e: e.activation(out=gf[0:1, :], in_=gf[0:1, :], func=AF.Ln, bias=1.0), reads=["gf"], writes=["gf"])
    s.op("dve", lambda e: e.tensor_tensor_scan(out=bn[0:1, :], data0=ones_r[0:1, :], data1=gf[0:1, :], initial=0.0,
                                               op0=ALU.mult, op1=ALU.add), reads=["ones_r", "gf"], writes=["bn"])
    s.op("dve", lambda e: e.scalar_tensor_tensor(out=arow[0:1, :], in0=gi[0:1, :], scalar=15.0, in1=bn[0:1, :],
                                                 op0=ALU.mult, op1=ALU.add), reads=["gi", "bn"], writes=["gi"])
    s.op("dve", lambda e: e.tensor_tensor_scan(out=mrow[0:1, :], data0=ones_r[0:1, :], data1=arow[0:1, :], initial=0.0,
                                               op0=ALU.mult, op1=ALU.max), reads=["ones_r", "gi", "gf"], writes=["gf"])
    s.op("dve", lambda e: e.tensor_tensor(out=mtrow[0:1, :], in0=mrow[0:1, :], in1=bn[0:1, :], op=ALU.subtract),
         reads=["gf", "bn"], writes=["bn"])
    for r, (row, nm_) in enumerate(((arow, "gi"), (mrow, "gf"), (mtrow, "bn"))):
        s.dma("sp", lambda e, r=r: e.dma_start(out=scr[r:r + 1, 0:PAD], in_=zpad[0:1, :]), reads=["zpad"],
              writes=[("scrpad", r)], slot=f"scrp{r}")
        s.dma("sp", lambda e, r=r, row=row: e.dma_start(out=scr[r:r + 1, PAD:PAD + S], in_=row[0:1, :]), reads=[nm_],
              writes=[("scr", r)], slot=f"scr{r}")
    k.release(mk0)
    scr_reads = [("scr", r) for r in range(3)] + [("scrpad", r) for r in range(3)]

    CN = k.sb("CN", [2, 384], F32)
    CNb2 = [k.sb(f"CNb_{p}", [2, 384], BF16) for p in range(2)]
    SS = k.sb("SS", [2, 256], F32)
    SSb2 = [k.sb(f"SSb_{p}", [2, 256], BF16) for p in range(2)]
    s.op("pool", lambda e: e.memset(CN[:], 0.0), writes=["CN0", "CN1"])
    for p_ in range(2):
        s.op("pool", lambda e, p_=p_: e.memset(CNb2[p_][:], 0.0), writes=[f"CNb{p_}0", f"CNb{p_}1"])
        s.op("pool", lambda e, p_=p_: e.memset(SSb2[p_][:], 0.0), writes=[f"SSb{p_}0", f"SSb{p_}1"])
    s.op("pool", lambda e: e.memset(SS[:], 0.0), writes=["SS0", "SS1"])
    W2s = k.sb("W2s", [256], F32)
    W2b = k.sb("W2b", [256], BF16)
    nb2 = k.sb("nb2", [2], F32)
    s.dma("sp", lambda e: e.dma_start(out=W2s[0:16, :], in_=acc["W2"](e)), writes=["W2s"], slot="W2s")
    s.op("act", lambda e: e.copy(out=W2b[0:16, :], in_=W2s[0:16, :]), reads=["W2s"], writes=["W2b"])
    s.dma("sp", lambda e: e.dma_start(out=nb2[:], in_=acc["b2"](e)), writes=["nb2"], slot="nb2")
    s.op("dve", lambda e: e.tensor_scalar(out=nb2[:], in0=nb2[:], scalar1=-1.0, scalar2=None, op0=ALU.mult),
         reads=["nb2"], writes=["nb2"])

    def dbl(name, fshape, dt):
        return [k.sb(f"{name}{i}", fshape, dt) for i in range(2)]

    Mrep, mtrep = dbl("Mrep", [8, 64], F32), dbl("mtrep", [512], F32)
    acol, Mend = dbl("acol", [8], F32), dbl("Mend", [9], F32)
    rr, Dm = dbl("rr", [8, 64], F32), dbl("Dm", [8, 64], F32)
    wcol, dec = dbl("wcol", [8], F32), dbl("dec", [8], F32)
    qsg, ksg = dbl("mq", [2, 512], BF16), dbl("mk", [2, 512], BF16)
    qsc = dbl("qsc", [2, 512], BF16)
    ktk, vtk = dbl("mkt", [8, 256], BF16), dbl("mv", [8, 256], BF16)
    hst = dbl("hst", [2, 512], F32)
    gq, gk = dbl("gq", [2, 512], BF16), dbl("gk", [2, 512], BF16)
    gqt, gkt = dbl("gqt", [2, 512], BF16), dbl("gkt", [2, 512], BF16)
    gvt = dbl("gvt", [8, 256], BF16)
    grs, grb = dbl("grs", [512], F32), dbl("grb", [512], BF16)
    ez, la = dbl("ez", [2, 512], F32), dbl("la", [2, 512], F32)
    GG = dbl("GG", [2, 8, 64], F32)
    egq, engk = dbl("egq", [2, 512], F32), dbl("engk", [2, 512], F32)
    egl = dbl("egl", [2, 8], F32)
    ost = dbl("ost", [2, 512], F32)
    PT, kw, dd = dbl("PT", [64], BF16), dbl("kw", [256], BF16), dbl("dd", [64], F32)
    PG, ktil = dbl("PG", [64], BF16), dbl("ktil", [256], BF16)
    onesg = k.sb("ones_g", [S if S < 512 else 512], F32)
    s.op("pool", lambda e: e.memset(onesg[:], 1.0), writes=["ones_g"])


    def ld(buf, src, res=None, extra_reads=()):
        s.dma("sp", lambda e: e.dma_start(out=buf, in_=src), reads=list(extra_reads), writes=[res], slot=res)

    def seg_ml(g):
        b = g % 2
        t0 = g * 512
        Mr, mt_, ac, Me = Mrep[b], mtrep[b], acol[b], Mend[b]
        s.dma("sp", lambda e, Mr=Mr, t0=t0: e.dma_start(
            out=Mr[:].rearrange("p c t -> p (c t)"), in_=scr[1:2, PAD + t0:PAD + t0 + 512].broadcast_to([128, 512])),
            reads=scr_reads, writes=[Mr.name], slot=Mr.name)
        s.dma("sp", lambda e, mt_=mt_, t0=t0: e.dma_start(
            out=mt_[:], in_=scr[2:3, PAD + t0:PAD + t0 + 512].broadcast_to([128, 512])),
            reads=scr_reads, writes=[mt_.name], slot=mt_.name)
        s.dma("sp", lambda e, ac=ac, t0=t0: e.dma_start(
            out=ac[0:64, :], in_=scr[0:1, PAD + t0:PAD + t0 + 512].rearrange("o (c s) -> (o s) c", s=64),
            allow_slow_non_contiguous=True), reads=scr_reads, writes=[ac.name], slot=ac.name)
        s.dma("sp", lambda e, Me=Me, t0=t0: e.dma_start(
            out=Me[:], in_=scr[1:2, PAD + t0 - 1:PAD + t0 - 1 + 9 * 64].rearrange("o (i s) -> o i s", s=64)[:, :, 0]
            .broadcast_to([128, 9]), allow_slow_non_contiguous=True), reads=scr_reads, writes=[Me.name], slot=Me.name)
        q_, k_, kt_, v_ = qsg[b], ksg[b], ktk[b], vtk[b]
        s.dma("sp", lambda e, q_=q_, t0=t0: e.dma_start(out=q_[:], in_=acc["mq"](e, t0)), writes=[q_.name], slot=q_.name)
        s.dma("sp", lambda e, k_=k_, t0=t0: e.dma_start(out=k_[:], in_=acc["mk"](e, t0)), writes=[k_.name], slot=k_.name)
        s.dma("sp", lambda e, kt_=kt_, t0=t0: e.dma_start(
            out=kt_[0:64], in_=acc["mkt"](e, t0)), writes=[kt_.name], slot=kt_.name)
        s.dma("sp", lambda e, v_=v_, t0=t0: e.dma_start(
            out=v_[0:64], in_=acc["mv"](e, t0)), writes=[v_.name], slot=v_.name)
        r_, D_, w_, d_, qs_, h_ = rr[b], Dm[b], wcol[b], dec[b], qsc[b], hst[b]
        s.op("dve", lambda e, r_=r_, Me=Me, Mr=Mr: e.tensor_tensor(
            out=r_[:], in0=Me[:, 0:8].unsqueeze(2).to_broadcast([128, 8, 64]), in1=Mr[:], op=ALU.subtract),
            reads=[Me.name, Mr.name], writes=[r_.name])
        s.op("act", lambda e, r_=r_: e.activation(out=r_[:], in_=r_[:], func=AF.Exp), reads=[r_.name], writes=[r_.name])
        for j in range(2):
            s.op("dve", lambda e, qs_=qs_, q_=q_, r_=r_, j=j: e.tensor_tensor(
                out=qs_[:, j, :], in0=q_[:, j, :], in1=r_[:].rearrange("p c t -> p (c t)"), op=ALU.mult),
                reads=[q_.name, r_.name], writes=[(qs_.name, j)])
        s.op("dve", lambda e, D_=D_, ac=ac, Mr=Mr: e.tensor_tensor(
            out=D_[0:64], in0=ac[0:64, :].unsqueeze(2).to_broadcast([64, 8, 64]), in1=Mr[0:64], op=ALU.subtract),
            reads=[ac.name, Mr.name], writes=[D_.name])
        s.op("dve", lambda e, D_=D_: e.tensor_tensor(
            out=D_[0:64], in0=D_[0:64], in1=nm[0:64, :].unsqueeze(1).to_broadcast([64, 8, 64]), op=ALU.add),
            reads=[D_.name, "nm"], writes=[D_.name])
        s.op("act", lambda e, D_=D_: e.activation(out=D_[0:64], in_=D_[0:64], func=AF.Exp), reads=[D_.name], writes=[D_.name])
        s.op("dve", lambda e, w_=w_, ac=ac, Me=Me: e.tensor_tensor(out=w_[0:64, :], in0=ac[0:64, :], in1=Me[0:64, 1:9],
                                                                   op=ALU.subtract), reads=[ac.name, Me.name], writes=[w_.name])
        s.op("act", lambda e, w_=w_: e.activation(out=w_[0:64, :], in_=w_[0:64, :], func=AF.Exp, bias=nm[0:64, 63:64]),
             reads=[w_.name, "nm"], writes=[w_.name])
        s.op("dve", lambda e, d_=d_, Me=Me: e.tensor_tensor(out=d_[:], in0=Me[:, 0:8], in1=Me[:, 1:9], op=ALU.subtract),
             reads=[Me.name], writes=[d_.name])
        s.op("act", lambda e, d_=d_: e.activation(out=d_[:], in_=d_[:], func=AF.Exp), reads=[d_.name], writes=[d_.name])
        s.op("act", lambda e, mt_=mt_: e.activation(out=mt_[:], in_=mt_[:], func=AF.Exp, scale=-1.0), reads=[mt_.name],
             writes=[mt_.name])
        def chunk(c):
            cs = slice(c * 64, (c + 1) * 64)
            gc = g * 8 + c
            ci = gc % 2
            CNr, CNw = CNb2[(gc + 1) % 2], CNb2[gc % 2]
            pr, pw = (gc + 1) % 2, gc % 2
            P_, kw_, dd_ = PT[ci], kw[ci], dd[ci]
            bS = k.bank()
            for j in range(2):
                s.op("pe", lambda e, j=j: e.matmul(
                    k.ps[bS][:64, :64], lhsT=k_[:, j, cs], rhs=q_[:, j, cs], start=(j == 0), stop=(j == 1)),
                    reads=[k_.name, q_.name], writes=[f"ps{bS}"])
            s.op("dve", lambda e: e.tensor_tensor(
                out=P_[0:64, :], in0=k.ps[bS][:64, :64], in1=D_[0:64, c, :], op=ALU.mult),
                reads=[f"ps{bS}", D_.name], writes=[P_.name])
            s.op("act", lambda e: e.activation(
                out=kw_[0:64, :], in_=kt_[0:64, c, :], func=AF.Copy, scale=w_[0:64, c:c + 1]),
                reads=[kt_.name, w_.name], writes=[kw_.name])
            bCs = []
            for j in range(2):
                bC = k.bank()
                bCs.append(bC)
                s.op("pe", lambda e, j=j, bC=bC: e.matmul(
                    k.ps[bC][:, 0:256], lhsT=kw_[0:64, j * 128:(j + 1) * 128], rhs=v_[0:64, c, :], start=True, stop=True),
                    reads=[kw_.name, v_.name], writes=[f"ps{bC}"])
                s.op("pe", lambda e, j=j, bC=bC: e.matmul(
                    k.ps[bC][:, 256:384], lhsT=kw_[0:64, j * 128:(j + 1) * 128], rhs=k.ones_bf[0:64, :], start=True,
                    stop=True), reads=[kw_.name, "ones_bf"], writes=[f"ps{bC}"])
            for j in range(2):
                bC = bCs[j]
                s.op("dve", lambda e, j=j, bC=bC: e.scalar_tensor_tensor(
                    out=CN[:, j, :], in0=CN[:, j, :], scalar=d_[:, c:c + 1], in1=k.ps[bC][:, 0:384], op0=ALU.mult,
                    op1=ALU.add), reads=[f"CN{j}", d_.name, f"ps{bC}"], writes=[f"CN{j}"])
                s.op("act", lambda e, j=j: e.copy(out=CNw[:, j, :], in_=CN[:, j, :]), reads=[f"CN{j}"],
                     writes=[f"CNb{pw}{j}"])
            bN = k.bank()
            grp = [(0, lambda: v_[0:64, c, 0:128], lambda j: CNr[:, j, 0:128]),
                   (1, lambda: v_[0:64, c, 128:256], lambda j: CNr[:, j, 128:256]),
                   (2, lambda: k.ones_bf[0:64, :], lambda j: CNr[:, j, 256:384])]
            for (gi_, lh0, lhj) in grp:
                oc = slice(gi_ * 64, gi_ * 64 + 64)
                s.op("pe", lambda e, lh0=lh0, oc=oc: e.matmul(
                    k.ps[bN][:, oc], lhsT=lh0(), rhs=P_[0:64, :], start=True, stop=False),
                    reads=[v_.name, P_.name, "ones_bf"], writes=[f"ps{bN}"])
                for j in range(2):
                    s.op("pe", lambda e, lhj=lhj, j=j, oc=oc: e.matmul(
                        k.ps[bN][:, oc], lhsT=lhj(j), rhs=qs_[:, j, cs], start=False, stop=(j == 1)),
                        reads=[f"CNb{pr}{j}", (qs_.name, j)], writes=[f"ps{bN}"])
            s.op("act", lambda e: e.activation(out=dd_[:], in_=k.ps[bN][:, 128:192], func=AF.Abs),
                 reads=[f"ps{bN}"], writes=[dd_.name])
            s.op("dve", lambda e: e.tensor_tensor(out=dd_[:], in0=dd_[:], in1=mt_[:, cs], op=ALU.max),
                 reads=[dd_.name, mt_.name], writes=[dd_.name])
            s.op("dve", lambda e: e.reciprocal(out=dd_[:], in_=dd_[:]), reads=[dd_.name], writes=[dd_.name])
            for i in range(2):
                s.op("dve", lambda e, i=i: e.tensor_tensor(
                    out=h_[:, i, cs], in0=k.ps[bN][:, i * 64:(i + 1) * 64], in1=dd_[:], op=ALU.mult),
                    reads=[f"ps{bN}", dd_.name], writes=[(h_.name, i, c)])
        for c in range(8):
            chunk(c)
        s.dma("sp", lambda e, h_=h_, t0=t0: e.dma_start(out=acc["hml"](e, t0), in_=h_[:]),
              reads=[(h_.name, i, c) for i in range(2) for c in range(8)], writes=[("hml", g)], slot=h_.name + "o")

    def seg_gla(g):
        b = g % 2
        t0 = g * 512
        q_, k_, v_, gs_, gb_ = gq[b], gk[b], gvt[b], grs[b], grb[b]
        s.dma("sp", lambda e, q_=q_, t0=t0: e.dma_start(out=q_[:], in_=acc["gq"](e, t0)), writes=[q_.name], slot=q_.name)
        s.dma("sp", lambda e, k_=k_, t0=t0: e.dma_start(out=k_[:], in_=acc["gk"](e, t0)), writes=[k_.name], slot=k_.name)
        s.dma("sp", lambda e, v_=v_, t0=t0: e.dma_start(
            out=v_[0:64], in_=acc["gv"](e, t0)), writes=[v_.name], slot=v_.name)
        s.dma("sp", lambda e, gs_=gs_, t0=t0: e.dma_start(out=gs_[0:16, :], in_=acc["gr"](e, t0)), writes=[gs_.name],
              slot=gs_.name)
        s.op("act", lambda e, gs_=gs_, gb_=gb_: e.copy(out=gb_[0:16, :], in_=gs_[0:16, :]), reads=[gs_.name], writes=[gb_.name])
        ez_, la_, G_, eq_, ek_, el_, qt_, kt2, o_ = ez[b], la[b], GG[b], egq[b], engk[b], egl[b], gqt[b], gkt[b], ost[b]
        for j in range(2):
            bZ = k.bank()
            s.op("pe", lambda e, j=j, gb_=gb_, bZ=bZ: e.matmul(
                k.ps[bZ][:, :512], lhsT=W2b[0:16, j * 128:(j + 1) * 128], rhs=gb_[0:16, :], start=True, stop=True),
                reads=["W2b", gb_.name], writes=[f"ps{bZ}"])
            s.op("act", lambda e, j=j, ez_=ez_, bZ=bZ: e.activation(
                out=ez_[:, j, :], in_=k.ps[bZ][:, :512], func=AF.Exp, scale=-1.0, bias=nb2[:, j:j + 1]),
                reads=[f"ps{bZ}", "nb2"], writes=[(ez_.name, j)])
            s.op("act", lambda e, j=j, ez_=ez_, la_=la_: e.activation(out=la_[:, j, :], in_=ez_[:, j, :], func=AF.Ln, bias=1.0),
                 reads=[(ez_.name, j)], writes=[(la_.name, j)])
            for c in range(8):
                s.op("dve", lambda e, j=j, c=c, G_=G_, la_=la_: e.tensor_tensor_scan(
                    out=G_[:, j, c, :], data0=onesg[:, 0:64], data1=la_[:, j, c * 64:(c + 1) * 64], initial=0.0,
                    op0=ALU.mult, op1=ALU.add), reads=[(la_.name, j), "ones_g"], writes=[(G_.name, j)])
            s.op("act", lambda e, j=j, G_=G_, eq_=eq_: e.activation(
                out=eq_[:, j, :], in_=G_[:, j].rearrange("p c t -> p (c t)"), func=AF.Exp, scale=-1.0 / 16.0,
                bias=nm[:, 63:64]), reads=[(G_.name, j), "nm"], writes=[(eq_.name, j)])
            s.op("act", lambda e, j=j, G_=G_, ek_=ek_: e.activation(
                out=ek_[:, j, :], in_=G_[:, j].rearrange("p c t -> p (c t)"), func=AF.Exp, scale=1.0 / 16.0),
                reads=[(G_.name, j)], writes=[(ek_.name, j)])
            s.op("act", lambda e, j=j, G_=G_, el_=el_: e.activation(
                out=el_[:, j, :], in_=G_[:, j, :, 63], func=AF.Exp, scale=-1.0 / 16.0),
                reads=[(G_.name, j)], writes=[(el_.name, j)])
            s.op("dve", lambda e, j=j, qt_=qt_, q_=q_, eq_=eq_: e.tensor_tensor(
                out=qt_[:, j, :], in0=q_[:, j, :], in1=eq_[:, j, :], op=ALU.mult),
                reads=[q_.name, (eq_.name, j)], writes=[(qt_.name, j)])
            s.op("dve", lambda e, j=j, kt2=kt2, k_=k_, ek_=ek_: e.tensor_tensor(
                out=kt2[:, j, :], in0=k_[:, j, :], in1=ek_[:, j, :], op=ALU.mult),
                reads=[k_.name, (ek_.name, j)], writes=[(kt2.name, j)])
        def chunk(c):
            cs = slice(c * 64, (c + 1) * 64)
            gc = g * 8 + c
            ci = gc % 2
            SSr, SSw = SSb2[(gc + 1) % 2], SSb2[gc % 2]
            pr, pw = (gc + 1) % 2, gc % 2
            P_, kl_ = PG[ci], ktil[ci]
            bA = k.bank()
            for j in range(2):
                s.op("pe", lambda e, j=j: e.matmul(
                    k.ps[bA][:64, :64], lhsT=kt2[:, j, cs], rhs=qt_[:, j, cs], start=(j == 0), stop=(j == 1)),
                    reads=[(kt2.name, j), (qt_.name, j)], writes=[f"ps{bA}"])
            s.op("dve", lambda e: e.tensor_tensor(out=P_[0:64, :], in0=k.ps[bA][:64, :64], in1=tri[0:64, :], op=ALU.mult),
                 reads=[f"ps{bA}", "tri"], writes=[P_.name])
            bT = k.bank()
            pst = k.ps[bT][:].bitcast(BF16)
            for j in range(2):
                s.op("pe", lambda e, j=j: e.transpose(
                    out=pst[0:64, j * 128:(j + 1) * 128], in_=kt2[:, j, cs], identity=idb[:]),
                    reads=[(kt2.name, j), "identb"], writes=[f"ps{bT}"])
            s.op("act", lambda e: e.copy(out=kl_[0:64, :], in_=pst[0:64, 0:256]), reads=[f"ps{bT}"], writes=[kl_.name])
            bCs = []
            for j in range(2):
                bC = k.bank()
                bCs.append(bC)
                s.op("pe", lambda e, j=j, bC=bC: e.matmul(
                    k.ps[bC][:, 0:256], lhsT=kl_[0:64, j * 128:(j + 1) * 128], rhs=v_[0:64, c, :], start=True, stop=True),
                    reads=[kl_.name, v_.name], writes=[f"ps{bC}"])
            for j in range(2):
                bC = bCs[j]
                s.op("dve", lambda e, j=j, bC=bC: e.tensor_tensor(out=SS[:, j, :], in0=SS[:, j, :], in1=k.ps[bC][:, 0:256],
                                                                  op=ALU.add), reads=[f"SS{j}", f"ps{bC}"], writes=[f"SS{j}"])
                s.op("dve", lambda e, j=j: e.tensor_scalar(
                    out=SS[:, j, :], in0=SS[:, j, :], scalar1=el_[:, j, c:c + 1], scalar2=None, op0=ALU.mult),
                    reads=[f"SS{j}", (el_.name, j)], writes=[f"SS{j}"])
                s.op("act", lambda e, j=j: e.copy(out=SSw[:, j, :], in_=SS[:, j, :]), reads=[f"SS{j}"],
                     writes=[f"SSb{pw}{j}"])
            bO = k.bank()
            for i in range(2):
                oc = slice(i * 64, i * 64 + 64)
                s.op("pe", lambda e, i=i, oc=oc: e.matmul(
                    k.ps[bO][:, oc], lhsT=v_[0:64, c, i * 128:(i + 1) * 128], rhs=P_[0:64, :], start=True, stop=False),
                    reads=[v_.name, P_.name], writes=[f"ps{bO}"])
                for j in range(2):
                    s.op("pe", lambda e, i=i, j=j, oc=oc: e.matmul(
                        k.ps[bO][:, oc], lhsT=SSr[:, j, i * 128:(i + 1) * 128], rhs=qt_[:, j, cs], start=False,
                        stop=(j == 1)), reads=[f"SSb{pr}{j}", (qt_.name, j)], writes=[f"ps{bO}"])
            for i in range(2):
                s.op("act", lambda e, i=i: e.copy(out=o_[:, i, cs], in_=k.ps[bO][:, i * 64:(i + 1) * 64]),
                     reads=[f"ps{bO}"], writes=[(o_.name, i, c)])
        for c in range(8):
            chunk(c)
        s.dma("sp", lambda e, o_=o_, t0=t0: e.dma_start(out=acc["hgla"](e, t0), in_=o_[:]),
              reads=[(o_.name, i, c) for i in range(2) for c in range(8)], writes=[("hgla", g)], slot=o_.name + "o")

    for g in range(NSEG):
        seg_ml(g)
        seg_gla(g)
    if own:
        k.close()
    else:
        k.release(mk_start)
    return k


def build_cd(T, layer0):
    k = KB()
    io = {"xT": k.din("xT", [D, T]), "hT": k.din("hT", [D, T], F32 if layer0 else BF16), "w_out": k.din("w_out", [D, D]),
          "g_post": k.din("g_post_mix", [128, KC]), "x_out": k.dscr("x_mid", [D, T]), "y_scr": k.dscr("y_scr", [D, T])}
    if layer0:
        io["ogT"] = k.din("ogT", [D, T])
        io["hg"] = k.din("hg", [128, KC])
    build_outproj(T, layer0, k=k, io=io)
    io2 = {"xT": io["x_out"], "memT": k.din("memT", [D, NMEM]), "g_pre": k.din("g_pre", [128, KC]),
           "g_mem": k.din("g_mem", [128, KC]), "g_post": k.din("g_post", [128, KC]), "wq": k.din("wq", [D, XA_W]),
           "wk": k.din("wk", [D, XA_W]), "wv": k.din("wv", [D, XA_W]), "wo": k.din("wo", [XA_W, D]),
           "x_out": k.dout("x_out", [D, T]), "y_scr": io["y_scr"]}
    build_xattn(T, k=k, io=io2)
    k.close()
    return k


def build_ea(T, N, jobs, outs):
    k = KB()
    TH = T + 2
    xT = k.din("xT", [D, TH])
    xo = k.dout("x_out", [D, T])
    io = {"g_pre": k.din("g_pre", [128, KC]), "g_post": k.din("g_post", [128, KC]), "w_gate": k.din("w_gate", [D, DFF]),
          "w_up": k.din("w_up", [D, DFF]), "conv_w": k.din("conv_w", [128, NFB, 3]), "conv_b": k.din("conv_b", [128, NFB]),
          "w_down": k.din("w_down", [DFF, D]), "x_out": xo, "h_scr": k.dscr("h_scr", [DFF, T], BF16),
          "y_scr": k.dscr("y_scr", [D, T]),
          "xparts": [(0, TH, lambda c, e: xT[c * 128:(c + 1) * 128, :])],
          "x_own": lambda c: xT[c * 128:(c + 1) * 128, 2:TH]}
    build_ffn(T, DFF, k=k, io=io)
    io2 = {"xT": xo, "g": k.din("g", [128, KC]), "w": k.din("w", [D, N])}
    for name, (shape, dt) in outs.items():
        io2[name] = k.dout(name, shape, dt)
    build_proj(T, N, jobs, outs, k=k, io=io2)
    k.close()
    return k


SEQ = 8192


def _gl(v):
    v = np.ascontiguousarray(np.asarray(v, np.float32).reshape(-1, 128).T)
    return v


def _launch(k, in_maps):
    res = run_bass_kernel_spmd(k.nc, in_maps, core_ids=list(range(len(in_maps))))
    return res.results


def _c(a):
    return np.ascontiguousarray(a)


def kernel(x, mem, mix_norm_pre, mix_norm_post, ab_w_in, ml_i_bias, ml_f_bias, ml_head_norm,
           gla_w_gate, gla_gate_bias, gla_head_norm, ab_w_out, sb_w_qkv, sb_w_out,
           xa_norm_pre, xa_norm_post, mem_norm, xa_wq, xa_wk, xa_wv, xa_wo,
           ffn_norm_pre, ffn_norm_post, ffn_w_gate, ffn_w_up, ffn_conv_w, ffn_conv_b, ffn_w_down):
    f32 = np.float32
    NC_ = NCORES
    T = TOK
    x = np.asarray(x, f32)
    xT = [_c(x[0, c * T:(c + 1) * T, :].T) for c in range(NC_)]
    memT = _c(np.asarray(mem, f32)[0].T)
    for layer in range(2):
        if layer == 0:
            jobs = [("fm", 0, 1024, "qk", 0), ("fm", 1024, 1024, "qk", 1024), ("fm", 6152, 1024, "qk", 2048),
                    ("fm", 7176, 1024, "qk", 3072), ("fm", 4096, 2048, "og", 0), ("fm", 10248, 2048, "og", 2048),
                    ("fm", 6144, 8, "gt", 0), ("fm", 12296, 16, "gt", 8),
                    ("tm", 1024, 1024, "vt", 0), ("tm", 2048, 2048, "vt", 1024), ("tm", 8200, 2048, "vt", 3072)]
            outs = {"qk": ([4096, T], BF16), "og": ([4096, T], F32), "gt": ([24, T], F32), "vt": ([T, 5120], BF16)}
            w_in = np.asarray(ab_w_in, f32)[0]
            kA = build_proj(T, w_in.shape[1], jobs, outs)
            g = _gl(mix_norm_pre[layer])
            rA = _launch(kA, [{"xT": xT[c], "g": g, "w": w_in} for c in range(NC_)])
            qk = np.concatenate([r["qk"] for r in rA], axis=1)
            vt = np.concatenate([r["vt"] for r in rA], axis=0)
            gt = np.concatenate([r["gt"] for r in rA], axis=1)
            kB = build_ab(SEQ)
            W2 = np.asarray(gla_w_gate, f32)[0]
            gb = np.asarray(gla_gate_bias, f32)[0]
            insB = []
            for c in range(NC_):
                hh, half = c // 2, c % 2
                insB.append({
                    "mqT": _c(qk[hh * 256:(hh + 1) * 256]), "mkT": _c(qk[1024 + hh * 256:1024 + (hh + 1) * 256]),
                    "mkt": _c(vt[:, hh * 256:(hh + 1) * 256]),
                    "mv": _c(vt[:, 1024 + hh * 512 + half * 256:1024 + hh * 512 + half * 256 + 256]),
                    "mgate": _c(gt[[hh, 4 + hh]]),
                    "mbias": np.array([[np.asarray(ml_i_bias, f32)[0, hh], np.asarray(ml_f_bias, f32)[0, hh]]], f32),
                    "gqT": _c(qk[2048 + hh * 256:2048 + (hh + 1) * 256]), "gkT": _c(qk[3072 + hh * 256:3072 + (hh + 1) * 256]),
                    "gv": _c(vt[:, 3072 + hh * 512 + half * 256:3072 + hh * 512 + half * 256 + 256]),
                    "grT": _c(gt[8:24]), "W2": _c(W2[:, hh * 256:(hh + 1) * 256]),
                    "b2": _c(gb[hh * 256:(hh + 1) * 256].reshape(2, 128).T)})
            rB = _launch(kB, insB)
            hT = np.empty((4096, SEQ), f32)
            for c in range(NC_):
                hh, half = c // 2, c % 2
                hT[hh * 512 + half * 256:hh * 512 + half * 256 + 256] = rB[c]["hml"]
                hT[2048 + hh * 512 + half * 256:2048 + hh * 512 + half * 256 + 256] = rB[c]["hgla"]
            kC = build_cd(T, True)
            hg = _gl(np.concatenate([np.asarray(ml_head_norm, f32)[0].ravel(), np.asarray(gla_head_norm, f32)[0].ravel()]))
            insC = [{"xT": xT[c], "hT": _c(hT[:, c * T:(c + 1) * T]), "ogT": rA[c]["og"], "hg": hg,
                     "w_out": np.asarray(ab_w_out, f32)[0], "g_post_mix": _gl(mix_norm_post[layer])} for c in range(NC_)]
        else:
            rA = pending
            qk = np.concatenate([r["qk"] for r in rA], axis=1)
            vt = np.concatenate([r["vt"] for r in rA], axis=0)
            kB = build_sb(SEQ, 4)
            rB = _launch(kB, [{"qT": _c(qk[c * 512:(c + 1) * 512]), "kT": _c(qk[4096 + c * 512:4096 + (c + 1) * 512]),
                               "v": _c(vt[:, c * 512:(c + 1) * 512])} for c in range(NC_)])
            hT = np.concatenate([r["oT"] for r in rB], axis=0)
            kC = build_cd(T, False)
            insC = [{"xT": xT[c], "hT": _c(hT[:, c * T:(c + 1) * T]), "w_out": np.asarray(sb_w_out, f32)[0],
                     "g_post_mix": _gl(mix_norm_post[layer])} for c in range(NC_)]
        insD = {"memT": memT, "g_pre": _gl(xa_norm_pre[layer]), "g_mem": _gl(mem_norm[layer]),
                "g_post": _gl(xa_norm_post[layer]), "wq": np.asarray(xa_wq, f32)[layer], "wk": np.asarray(xa_wk, f32)[layer],
                "wv": np.asarray(xa_wv, f32)[layer], "wo": np.asarray(xa_wo, f32)[layer]}
        rD = _launch(kC, [dict(insD, **insC[c]) for c in range(NC_)])
        xT = [r["x_out"] for r in rD]
        jobs1 = [("fm", 0, 4096, "qk", 0), ("fm", 4096, 4096, "qk", 4096), ("tm", 8192, 4096, "vt", 0)]
        outs1 = {"qk": ([8192, T], BF16), "vt": ([T, 4096], BF16)}
        kE = build_ea(T, 3 * D, jobs1, outs1) if layer == 0 else build_ffn(T)
        cw = np.asarray(ffn_conv_w, f32)[layer]
        insE = {"g_pre": _gl(ffn_norm_pre[layer]), "g_post": _gl(ffn_norm_post[layer]),
                "w_gate": np.asarray(ffn_w_gate, f32)[layer], "w_up": np.asarray(ffn_w_up, f32)[layer],
                "conv_w": _c(cw.T.reshape(NFB, 128, 3).transpose(1, 0, 2)), "conv_b": _gl(np.asarray(ffn_conv_b, f32)[layer]),
                "w_down": np.asarray(ffn_w_down, f32)[layer]}
        if layer == 0:
            insE["g"] = _gl(mix_norm_pre[1])
            insE["w"] = np.asarray(sb_w_qkv, f32)[0]
        mapsE = []
        for c in range(NC_):
            halo = xT[c - 1][:, -2:] if c > 0 else np.zeros((D, 2), f32)
            m_ = dict(insE, xT=_c(np.concatenate([halo, xT[c]], axis=1)))
            if layer == 1:
                m_["hmask"] = np.full((128, 1), 0.0 if c == 0 else 1.0, f32)
            mapsE.append(m_)
        rE = _launch(kE, mapsE)
        pending = rE
        xT = [r["x_out"] for r in rE]
    out = np.empty((1, SEQ, D), f32)
    for c in range(NC_):
        out[0, c * T:(c + 1) * T, :] = xT[c].T
    return out
```
